# Optimizing a Trainium2 kernel written in Bass

```python
import jax, jax.numpy as jnp
from jax import lax
import numpy as np

D_MODEL = 1024
BATCH = 8
SEQ = 2048
DEPTH = 1

D_CONV = D_MODEL // 2
CONV_WIDTH = 3
N_HEADS = 8
N_KV_HEADS = 2
HEAD_DIM = 64
D_ATTN = N_HEADS * HEAD_DIM
IDX_HEADS = 8
IDX_DIM = 64
TOPK_MAX = 256
Q_BLOCK = 128
N_BRANCH = 2
D_FF = 4 * D_MODEL
LN_EPS = 1e-5
ALPHA = (2 * DEPTH) ** 0.25
BETA = (8 * DEPTH) ** -0.25

IN_SIZES = (D_CONV, D_CONV, D_CONV,
            N_HEADS * HEAD_DIM, N_KV_HEADS * HEAD_DIM, N_KV_HEADS * HEAD_DIM,
            IDX_HEADS * IDX_DIM, IDX_DIM, IDX_HEADS,
            N_BRANCH * D_MODEL)
D_IN = sum(IN_SIZES)

kernel_name = "hybrid_gatedconv_dsa_sqrelu_deepnorm"


def layer_norm(x, g, b):
    xf = x.astype(jnp.float32)
    mu = jnp.mean(xf, axis=-1, keepdims=True)
    xc = xf - mu
    var = jnp.mean(xc * xc, axis=-1, keepdims=True)
    y = xc * lax.rsqrt(var + LN_EPS) * g.astype(jnp.float32) + b.astype(jnp.float32)
    return y.astype(x.dtype)


def causal_depthwise_conv(v, w):
    c = v.shape[-1]
    rhs = w[:, None, :]
    return lax.conv_general_dilated(
        v, rhs, window_strides=(1,), padding=((CONV_WIDTH - 1, 0),),
        dimension_numbers=("NWC", "WIO", "NWC"), feature_group_count=c)


def dsa_sparse_attention(q, k, v, qi, ki, wi):
    b, s = q.shape[0], q.shape[1]
    k_sel = min(TOPK_MAX, s // 4)
    nb = s // Q_BLOCK
    idx_scale = (IDX_DIM ** -0.5) * (IDX_HEADS ** -0.5)
    attn_scale = HEAD_DIM ** -0.5
    rep = N_HEADS // N_KV_HEADS
    key_pos = jnp.arange(s)

    def to_blocks(a):
        a = a.reshape((b, nb, Q_BLOCK) + a.shape[2:])
        return jnp.moveaxis(a, 1, 0)

    def one_block(args):
        qb, qib, wib, blk = args
        t = blk * Q_BLOCK + jnp.arange(Q_BLOCK)
        logits = jnp.einsum("bqhd,bsd->bqhs", qib, ki).astype(jnp.float32)
        score = jnp.einsum("bqh,bqhs->bqs", wib.astype(jnp.float32),
                           jax.nn.relu(logits)) * idx_scale
        causal = key_pos[None, :] <= t[:, None]
        score = jnp.where(causal[None], score, -jnp.inf)
        _, sel = lax.top_k(score, k_sel)
        valid = sel <= t[None, :, None]
        kg = jax.vmap(lambda kk, ii: kk[ii])(k, sel)
        vg = jax.vmap(lambda vv, ii: vv[ii])(v, sel)
        qg = qb.reshape(b, Q_BLOCK, N_KV_HEADS, rep, HEAD_DIM)
        att = jnp.einsum("bqgrd,bqkgd->bqgrk", qg, kg).astype(jnp.float32) * attn_scale
        att = jnp.where(valid[:, :, None, None, :], att, -jnp.inf)
        p = jax.nn.softmax(att, axis=-1).astype(vg.dtype)
        o = jnp.einsum("bqgrk,bqkgd->bqgrd", p, vg)
        return o.reshape(b, Q_BLOCK, D_ATTN)

    out = lax.map(one_block, (to_blocks(q), to_blocks(qi), to_blocks(wi), jnp.arange(nb)))
    return jnp.moveaxis(out, 0, 1).reshape(b, s, D_ATTN)


def setup_inputs(seed: int = 0) -> dict:
    key = jax.random.key(seed)
    ks = jax.random.split(key, 16)
    f32 = jnp.float32
    nrm = lambda k, shape, scale: jax.random.normal(k, shape, f32) * scale
    x = jax.random.normal(ks[0], (BATCH, SEQ, D_MODEL), f32)
    w_in = nrm(ks[1], (DEPTH, D_MODEL, D_IN), D_MODEL ** -0.5)
    conv_w = nrm(ks[2], (DEPTH, CONV_WIDTH, D_CONV), CONV_WIDTH ** -0.5)
    idx_k_norm_g = 1.0 + nrm(ks[3], (DEPTH, IDX_DIM), 0.02)
    idx_k_norm_b = nrm(ks[4], (DEPTH, IDX_DIM), 0.02)
    w_branch = nrm(ks[5], (DEPTH, N_BRANCH, D_CONV, D_MODEL), BETA * D_CONV ** -0.5)
    w_o = nrm(ks[6], (DEPTH, D_MODEL, D_MODEL), BETA * D_MODEL ** -0.5)
    ln1_g = 1.0 + nrm(ks[7], (DEPTH, D_MODEL), 0.02)
    ln1_b = nrm(ks[8], (DEPTH, D_MODEL), 0.02)
    w_up = nrm(ks[9], (DEPTH, D_MODEL, D_FF), BETA * D_MODEL ** -0.5)
    w_down = nrm(ks[10], (DEPTH, D_FF, D_MODEL), BETA * D_FF ** -0.5)
    ln2_g = 1.0 + nrm(ks[11], (DEPTH, D_MODEL), 0.02)
    ln2_b = nrm(ks[12], (DEPTH, D_MODEL), 0.02)
    return {"x": x, "w_in": w_in, "conv_w": conv_w,
            "idx_k_norm_g": idx_k_norm_g, "idx_k_norm_b": idx_k_norm_b,
            "w_branch": w_branch, "w_o": w_o, "ln1_g": ln1_g, "ln1_b": ln1_b,
            "w_up": w_up, "w_down": w_down, "ln2_g": ln2_g, "ln2_b": ln2_b}


def reference(x, w_in, conv_w, idx_k_norm_g, idx_k_norm_b, w_branch, w_o,
              ln1_g, ln1_b, w_up, w_down, ln2_g, ln2_b):
    b, s, _ = x.shape
    offs = np.cumsum(IN_SIZES)[:-1].tolist()
    for l in range(DEPTH):
        z = x @ w_in[l]
        (b_gate, c_gate, u, q, k, v, qi, ki, wi, gates) = jnp.split(z, offs, axis=-1)
        y_a = b_gate * causal_depthwise_conv(c_gate * u, conv_w[l])
        ki = layer_norm(ki, idx_k_norm_g[l], idx_k_norm_b[l])
        y_b = dsa_sparse_attention(
            q.reshape(b, s, N_HEADS, HEAD_DIM),
            k.reshape(b, s, N_KV_HEADS, HEAD_DIM),
            v.reshape(b, s, N_KV_HEADS, HEAD_DIM),
            qi.reshape(b, s, IDX_HEADS, IDX_DIM), ki, wi)
        y_br = jnp.stack([y_a, y_b], axis=2)
        proj = jnp.einsum("bsnc,ncd->bsnd", y_br, w_branch[l])
        g = jax.nn.sigmoid(gates.reshape(b, s, N_BRANCH, D_MODEL))
        mix = jnp.sum(g * proj, axis=2) @ w_o[l]
        h = layer_norm(ALPHA * x + mix, ln1_g[l], ln1_b[l])
        ff = jnp.square(jax.nn.relu(h @ w_up[l])) @ w_down[l]
        x = layer_norm(ALPHA * h + ff, ln2_g[l], ln2_b[l])
    return x
```

```python
import os
import numpy as np
import ml_dtypes
import concourse.bass as bass
import concourse.mybir as mybir
from concourse.bass_utils import run_bass_kernel_spmd

F32 = mybir.dt.float32
BF16 = mybir.dt.bfloat16
ALU = mybir.AluOpType
AF = mybir.ActivationFunctionType
AX = mybir.AxisListType

S = 2048
D = 1024
NT = S // 128
NCH = D // 128
D_IN = 4936
DFF = 4096
NF = DFF // 128
ALPHA = 2.0 ** 0.25
LN_EPS = 1e-5
TOPK = 256
NIT = 10
NEG = -1.0e30
MASKNEG = -30000.0
DVE_RELU_HEADS = (1, 3, 5, 7)

C_B, C_C, C_U, C_Q, C_K, C_V, C_QI, C_KI, C_WI, C_GA, C_GB = 0, 512, 1024, 1536, 2048, 2176, 2304, 2816, 2880, 2888, 3912

SEM_LIMIT = 30000


def _esize(dt):
    return {F32: 4, BF16: 2}.get(dt, 4)


class _Eng:
    def __init__(self, name, eng, sem):
        self.name = name
        self.eng = eng
        self.sem = sem
        self.count = 0
        self.pending = False
        self.seen = {}


class Tracker:
    BLK = 256

    def __init__(self, nc):
        self.nc = nc
        self.nsem = 0
        self.engs = {}
        for name, eng in (("pe", nc.tensor), ("act", nc.scalar), ("dve", nc.vector),
                          ("pool", nc.gpsimd), ("sp", nc.sync)):
            self.engs[name] = _Eng(name, eng, self._newsem(name))
        self.blocks = {}
        self.dma_sems = {}
        self.dma_sems_by_id = {}
        self.nwaits = 0
        self.ninst = 0

    def _newsem(self, name):
        self.nsem += 1
        return self.nc.alloc_semaphore("s_%s_%d" % (name, self.nsem))

    def _keys(self, ap):
        t = ap.tensor
        space = "P" if "PSum" in type(t).__name__ else "S"
        pairs = ap.ap
        es = _esize(ap.dtype)
        pstride = pairs[0][0]
        npart = pairs[0][1]
        p0 = ap.offset // pstride if pstride else 0
        base = (ap.offset % pstride) * es if pstride else ap.offset * es
        halves = set()
        if p0 < 64:
            halves.add(0)
        if p0 + npart > 64:
            halves.add(1)
        free = pairs[1:]
        ranges = []
        if not free:
            ranges.append((base, base + es))
        else:
            outer = free[:-1]
            lstep, lcnt = free[-1]
            span = ((lcnt - 1) * abs(lstep) + 1) * es

            def rec(i, off):
                if i == len(outer):
                    ranges.append((off, off + span))
                    return
                st, cn = outer[i]
                for k in range(cn):
                    rec(i + 1, off + k * st * es)
            rec(0, base)
        keys = set()
        B = 2048 if space == "P" else self.BLK
        for lo, hi in ranges:
            for b in range(lo // B, (hi - 1) // B + 1):
                for h in halves:
                    keys.add((space, h, b))
        return keys

    def _allkeys(self, items):
        keys = set()
        for it in items:
            if it is None:
                continue
            if isinstance(it, (str, tuple)):
                keys.add(("D", it))
            else:
                keys |= self._keys(it)
        return keys

    def _deps(self, ename, rkeys, wkeys, is_dma):
        deps = {}

        def add(tk):
            sem, val, own = tk
            if (not is_dma) and own == ename and ename == "pe":
                return
            k = id(sem)
            if k not in deps or deps[k][1] < val:
                deps[k] = (sem, val, own)

        def add_waw(tk):
            sem, val, own = tk
            if (not is_dma) and own == ename and ename == "pe":
                return
            k = id(sem)
            if k not in deps or deps[k][1] < val:
                deps[k] = (sem, val, own)

        def add_raw(tk):
            sem, val, own = tk
            if (not is_dma) and own == ename and ename == "pe":
                return
            k = id(sem)
            if k not in deps or deps[k][1] < val:
                deps[k] = (sem, val, own)

        for key in rkeys:
            st = self.blocks.get(key)
            if st and st["w"] is not None:
                add_raw(st["w"])
            if st and key[0] == "P":
                for tk in st["r"].values():
                    add(tk)
        for key in wkeys:
            st = self.blocks.get(key)
            if st:
                if st["w"] is not None:
                    add_waw(st["w"])
                for tk in st["r"].values():
                    add(tk)
        return deps

    def _wait(self, E, deps):
        for k, (sem, val, own) in deps.items():
            if E.seen.get(k, 0) >= val:
                continue
            if own in self.engs:
                P = self.engs[own]
                if P.sem is sem:
                    assert val <= P.count, "wait on future inc (%s waits %s)" % (E.name, own)
            elif own.startswith("dma:"):
                val = self.dma_sems_by_id[k][1]
            E.eng.wait_ge(sem, val)
            E.seen[k] = val
            self.nwaits += 1

    def _update(self, rkeys, wkeys, tk):
        for key in rkeys:
            st = self.blocks.setdefault(key, {"w": None, "r": {}})
            k = id(tk[0])
            old = st["r"].get(k)
            if old is None or old[1] < tk[1]:
                st["r"][k] = tk
        for key in wkeys:
            self.blocks[key] = {"w": tk, "r": {}}

    def op(self, ename, fn, reads=(), writes=(), inc=True):
        E = self.engs[ename]
        rkeys = self._allkeys(reads)
        wkeys = self._allkeys(writes)
        deps = self._deps(ename, rkeys, wkeys, False)
        self._wait(E, deps)
        inst = fn(E.eng)
        self.ninst += 1
        if inc:
            E.count += 1
            inst.then_inc(E.sem, 1)
            E.pending = False
            tk = (E.sem, E.count, ename)
        else:
            E.pending = True
            tk = (E.sem, E.count + 1, ename)
        self._update(rkeys, wkeys, tk)
        if inc and E.count >= SEM_LIMIT:
            E.sem = self._newsem(ename)
            E.count = 0
        return inst

    def dma(self, qname, out, in_, skey, reads=(), writes=(), chain=False):
        E = self.engs[qname]
        rkeys = self._allkeys(list(reads))
        wkeys = self._allkeys(list(writes))
        deps = self._deps(qname, rkeys, wkeys, True)
        self._wait(E, deps)
        if skey not in self.dma_sems:
            self.dma_sems[skey] = [self._newsem("dma"), 0]
            self.dma_sems_by_id[id(self.dma_sems[skey][0])] = self.dma_sems[skey]
        rec = self.dma_sems[skey]
        if (not chain) and rec[1] > 0 and E.seen.get(id(rec[0]), 0) < rec[1]:
            E.eng.wait_ge(rec[0], rec[1])
            E.seen[id(rec[0])] = rec[1]
            self.nwaits += 1
        rec[1] += 16
        E.eng.dma_start(out=out, in_=in_).then_inc(rec[0], 16)
        self.ninst += 1
        tk = (rec[0], rec[1], "dma:" + str(skey))
        self._update(rkeys, wkeys, tk)

    def final_wait(self, qname, skey):
        rec = self.dma_sems[skey]
        self.engs[qname].eng.wait_ge(rec[0], rec[1])


class Arena:
    def __init__(self, nc, nbytes):
        self.nbytes = nbytes // 256 * 256
        self.t = nc.alloc_sbuf_tensor("arena", [128, self.nbytes // 2], BF16)
        self.free = [(0, self.nbytes)]
        self.live = {}
        self.rings = {}
        self.peak = 0

    def alloc(self, name, shape, dt):
        n = 1
        for s_ in shape:
            n *= s_
        nb = (n * _esize(dt) + 255) // 256 * 256
        for i, (lo, hi) in enumerate(self.free):
            if hi - lo >= nb:
                self.free[i] = (lo + nb, hi)
                if self.free[i][0] == self.free[i][1]:
                    self.free.pop(i)
                self.live[name] = (lo, nb)
                used = self.nbytes - sum(h - l for l, h in self.free)
                self.peak = max(self.peak, used)
                v = self.t[:, lo // 2:(lo + nb) // 2]
                if dt == F32:
                    v = v.bitcast(F32)
                v = v[:, 0:n]
                if len(shape) == 2:
                    v = v.rearrange("p (a b) -> p a b", a=shape[0])
                elif len(shape) == 3:
                    v = v.rearrange("p (a b c) -> p a b c", a=shape[0], b=shape[1])
                return v
        raise RuntimeError("arena OOM for %s (%d B); live=%s" % (name, nb, {k: v[1] for k, v in self.live.items()}))

    def ralloc(self, name, shape, dt, n=2):
        if name not in self.rings:
            self.rings[name] = [[self.alloc("%s#%d" % (name, i), shape, dt) for i in range(n)], 0]
        r = self.rings[name]
        r[1] += 1
        return r[0][r[1] % len(r[0])]

    def rfree(self, name):
        for i in range(len(self.rings[name][0])):
            self.release("%s#%d" % (name, i))
        del self.rings[name]

    def release(self, name):
        if name in self.rings:
            return
        lo, nb = self.live.pop(name)
        self.free.append((lo, lo + nb))
        self.free.sort()
        merged = []
        for l, h in self.free:
            if merged and merged[-1][1] == l:
                merged[-1] = (merged[-1][0], h)
            else:
                merged.append((l, h))
        self.free = merged


class PsumPool:
    def __init__(self, nc):
        self.t = nc.alloc_psum_tensor("psum", [128, 4096], F32)
        self.order = list(range(8))
        self.held = set()

    def get(self, hold=False):
        for b in self.order:
            if b not in self.held:
                self.order.remove(b)
                self.order.append(b)
                if hold:
                    self.held.add(b)
                return b
        raise RuntimeError("no free PSUM bank")

    def release(self, b):
        self.held.discard(b)

    def f32(self, b):
        return self.t[:, b * 512:(b + 1) * 512]

    def bf16(self, b):
        return self.t[:, b * 512:(b + 1) * 512].bitcast(BF16)


class _Stop(Exception):
    pass


def build_program(debug=False, stop_after=None):
    nc = bass.Bass("TRN2", target_bir_lowering=False)
    T = Tracker(nc)
    dbg = {}
    try:
        _build_body(nc, T, dbg, debug, stop_after)
    except _Stop:
        pass
    for skey in list(T.dma_sems.keys()):
        if skey.startswith("ost") or skey == "dbg":
            T.final_wait("sp", skey)
    info = {"ninst": T.ninst, "nwaits": T.nwaits, "nsem": T.nsem,
            "counts": {k: v.count for k, v in T.engs.items()}}
    return nc, info


def _build_body(nc, T, dbg, debug, stop_after):
    def stop(tag):
        if stop_after == tag:
            raise _Stop()

    def din(name, shape, dt=F32):
        return nc.dram_tensor(name, list(shape), dt, kind="ExternalInput").ap()

    x_d = din("x", [S, D])
    w_in_d = din("w_in", [D, D_IN])
    convw_d = din("conv_w_t", [128, 12])
    ikg_d = din("ikg", [128, 64])
    ikb_d = din("ikb", [128, 64])
    w_br_d = din("w_branch", [2, 512, D])
    w_o_d = din("w_o", [D, D])
    ln1g_d = din("ln1g", [128, D])
    ln1b_d = din("ln1b", [128, D])
    ln1gc_d = din("ln1gc", [128, 8])
    ln1bc_d = din("ln1bc", [128, 8])
    w_up_d = din("w_up", [D, DFF])
    w_down_d = din("w_down", [DFF, D])
    ln2g_d = din("ln2g", [128, D])
    ln2b_d = din("ln2b", [128, D])
    ident_d = din("ident", [128, 128], BF16)
    tri_d = din("tri", [128, 128], BF16)
    negtri_d = din("negtri", [128, 128])
    pow2_d = din("pow2", [128, 2 * NIT])
    out_d = nc.dram_tensor("out", [S, D], F32, kind="ExternalOutput").ap()
    h_scr = nc.dram_tensor("h_scr", [S, D], F32, kind="ExternalOutput" if debug else "Internal").ap()

    def dbg_out(name, shape, dt=F32):
        if not debug:
            return None
        dbg[name] = nc.dram_tensor("dbg_" + name, list(shape), dt, kind="ExternalOutput").ap()
        return dbg[name]

    AR = Arena(nc, int(nc.sbuf_bytes_remaining) - 1024)
    PS = PsumPool(nc)

    w_in_v = w_in_d.rearrange("(c p) n -> p c n", p=128)

    def dump(name, ap_sb, shape, dt=F32):
        d = dbg_out(name, shape, dt)
        if d is None:
            return
        T.dma("sp", d, ap_sb, "dbg", reads=[ap_sb], writes=["dbg_" + name])

    def rstd_from_var(mv):
        T.op("dve", lambda e: e.tensor_scalar(out=mv[:, 3:4], in0=mv[:, 1:2], scalar1=LN_EPS, scalar2=None,
                                              op0=ALU.add), reads=[mv[:, 1:2]], writes=[mv[:, 3:4]])
        T.op("act", lambda e: e.activation(out=mv[:, 3:4], in_=mv[:, 3:4], func=AF.Sqrt), reads=[mv[:, 3:4]],
             writes=[mv[:, 3:4]])
        T.op("dve", lambda e: e.reciprocal(out=mv[:, 2:3], in_=mv[:, 3:4]), reads=[mv[:, 3:4]], writes=[mv[:, 2:3]])

    ident = AR.alloc("ident", [128], BF16)
    tri = AR.alloc("tri", [128], BF16)
    negtri = AR.alloc("negtri", [128], F32)
    pow2 = AR.alloc("pow2", [2 * NIT], F32)
    convw = AR.alloc("convw", [12], F32)
    ikg = AR.alloc("ikg", [64], F32)
    ikb = AR.alloc("ikb", [64], F32)
    for dst, src in ((ident, ident_d), (tri, tri_d), (negtri, negtri_d), (pow2, pow2_d),
                     (convw, convw_d), (ikg, ikg_d), (ikb, ikb_d)):
        T.dma("sp", dst, src, "const", writes=[dst], chain=True)

    def load_w_cols(name, col0, ncols, dup64=False):
        w = AR.alloc(name, [NCH, ncols], BF16)
        T.dma("pool", w, w_in_v[:, :, col0:col0 + ncols], "w_" + name, writes=[w])
        return w

    xT = AR.alloc("xT", [NCH, S], BF16)
    xts = [AR.alloc("xt%d" % i, [D], F32) for i in range(2)]
    xbs = [AR.alloc("xb%d" % i, [D], BF16) for i in range(2)]
    w_kw = load_w_cols("w_kw", C_KI, 72)
    w_qi = load_w_cols("w_qi", C_QI, 512)
    for tt in range(NT):
        xt = xts[tt % 2]
        xb = xbs[tt % 2]
        T.dma("sp", xt, x_d[tt * 128:(tt + 1) * 128, :], "xld%d" % (tt % 2), writes=[xt])
        T.op("act", lambda e: e.copy(out=xb, in_=xt), reads=[xt], writes=[xb])
        b = PS.get()
        pb = PS.bf16(b)
        for c in range(NCH):
            o = pb[:, c * 128:(c + 1) * 128]
            i_ = xb[:, c * 128:(c + 1) * 128]
            T.op("pe", lambda e: e.transpose(out=o, in_=i_, identity=ident), reads=[i_, ident], writes=[o],
                 inc=(c == NCH - 1))
        dst = xT[:, :, tt * 128:(tt + 1) * 128]
        src = pb.rearrange("p (c t) -> p c t", c=NCH)
        T.op("dve", lambda e: e.tensor_copy(out=dst, in_=src), reads=[pb], writes=[dst])
    AR.release("xt0"); AR.release("xt1"); AR.release("xb0"); AR.release("xb1")
    if debug:
        dump("xT", xT, [128, NCH, S], BF16)
    stop("A")

    def proj_feat(w, wc0, dst_fn, evac):
        for m in range(4):
            b = PS.get()
            p = PS.f32(b)
            for c in range(NCH):
                l = w[:, c, wc0:wc0 + 128]
                r = xT[:, c, m * 512:(m + 1) * 512]
                T.op("pe", lambda e: e.matmul(out=p, lhsT=l, rhs=r, start=(c == 0), stop=(c == NCH - 1)),
                     reads=[l, r], writes=[p], inc=(c == NCH - 1))
            evac(m, p)

    cp_toggle = [0]

    def evac_copy(dst, src):
        cp_toggle[0] ^= 1
        if cp_toggle[0]:
            T.op("act", lambda e: e.copy(out=dst, in_=src), reads=[src], writes=[dst])
        else:
            T.op("dve", lambda e: e.tensor_copy(out=dst, in_=src), reads=[src], writes=[dst])

    w_q = load_w_cols("w_q", C_Q, 512)
    w_k = [AR.alloc("w_k%d" % g, [NCH, 128], BF16) for g in range(2)]
    for g in range(2):
        for half in range(2):
            dstw = w_k[g][:, :, half * 64:(half + 1) * 64]
            T.dma("pool", dstw, w_in_v[:, :, C_K + g * 64:C_K + (g + 1) * 64], "w_k%d%d" % (g, half), writes=[dstw])
    w_v = load_w_cols("w_v", C_V, 128)
    kiT2 = AR.alloc("kiT2", [S], BF16)
    wi = AR.alloc("wi", [NT, 8], F32)
    qiT = AR.alloc("qiT", [4, S], BF16)
    qT = AR.alloc("qT", [4, S], BF16)
    kT2 = AR.alloc("kT2", [2, S], BF16)

    def ki_gen():
        for tt in range(NT):
            b = PS.get()
            p = PS.f32(b)[:, 0:72]
            for c in range(NCH):
                l = xT[:, c, tt * 128:(tt + 1) * 128]
                r = w_kw[:, c, :]
                T.op("pe", lambda e: e.matmul(out=p, lhsT=l, rhs=r, start=(c == 0), stop=(c == NCH - 1)),
                     reads=[l, r], writes=[p], inc=(c == NCH - 1))
            st = AR.ralloc("kst", [8], F32, 3)
            mv = AR.ralloc("kmv", [4], F32, 3)
            kn = AR.ralloc("kn", [64], F32, 3)
            kn2 = AR.ralloc("kn2", [128], BF16, 3)
            pk = p[:, 0:64]
            T.op("dve", lambda e: e.bn_stats(out=st[:, 0:6], in_=pk), reads=[pk], writes=[st])
            T.op("dve", lambda e: e.bn_aggr(out=mv[:, 0:2], in_=st[:, 0:6]), reads=[st], writes=[mv[:, 0:2]])
            rstd_from_var(mv)
            T.op("dve", lambda e: e.tensor_scalar(out=kn, in0=pk, scalar1=mv[:, 0:1], scalar2=mv[:, 2:3],
                                                  op0=ALU.subtract, op1=ALU.mult), reads=[pk, mv], writes=[kn])
            wsrc = p[:, 64:72]
            wdst = wi[:, tt, :]
            T.op("dve", lambda e: e.tensor_copy(out=wdst, in_=wsrc), reads=[wsrc], writes=[wdst])
            T.op("dve", lambda e: e.tensor_tensor(out=kn, in0=kn, in1=ikg, op=ALU.mult), reads=[kn, ikg], writes=[kn])
            T.op("dve", lambda e: e.tensor_tensor(out=kn2[:, 0:64], in0=kn, in1=ikb, op=ALU.add),
                 reads=[kn, ikb], writes=[kn2[:, 0:64]])
            T.op("dve", lambda e: e.tensor_copy(out=kn2[:, 64:128], in_=kn2[:, 0:64]), reads=[kn2[:, 0:64]],
                 writes=[kn2[:, 64:128]])
            yield
            b2 = PS.get()
            pb = PS.bf16(b2)[:, 0:128]
            T.op("pe", lambda e: e.transpose(out=pb, in_=kn2, identity=ident), reads=[kn2, ident], writes=[pb])
            kd = kiT2[:, tt * 128:(tt + 1) * 128]
            T.op("act", lambda e: e.copy(out=kd, in_=pb), reads=[pb], writes=[kd])
            yield

    def projB_gen():
        for (w, wc0, dst) in ([(w_qi, j * 128, qiT[:, j, :]) for j in range(4)]
                              + [(w_q, j * 128, qT[:, j, :]) for j in range(4)]
                              + [(w_k[g], 0, kT2[:, g, :]) for g in range(2)]):
            for m in range(4):
                b = PS.get()
                p = PS.f32(b)
                for c in range(NCH):
                    l = w[:, c, wc0:wc0 + 128]
                    r = xT[:, c, m * 512:(m + 1) * 512]
                    T.op("pe", lambda e: e.matmul(out=p, lhsT=l, rhs=r, start=(c == 0), stop=(c == NCH - 1)),
                         reads=[l, r], writes=[p], inc=(c == NCH - 1))
                evac_copy(dst[:, m * 512:(m + 1) * 512], p)
                yield

    def run2(ga, gb, ra, rb):
        alive = [True, True]
        gens = [ga, gb]
        while any(alive):
            for q, reps in ((0, ra), (1, rb)):
                if not alive[q]:
                    continue
                for _ in range(reps):
                    try:
                        next(gens[q])
                    except StopIteration:
                        alive[q] = False
                        break
    run2(ki_gen(), projB_gen(), 1, 1)
    AR.release("w_kw"); AR.release("w_qi"); AR.release("w_q"); AR.release("w_k0"); AR.release("w_k1")
    for n_ in ("kst", "kmv", "kn", "kn2"):
        AR.rfree(n_)
    stop("B3")
    Vaug = [[AR.alloc("Vaug%d%d" % (g, e), [NT, 128], BF16) for e in range(2)] for g in range(2)]
    for g in range(2):
        for e_ in range(2):
            va = Vaug[g][e_]
            T.op("dve", lambda e: e.memset(va, 1.0), writes=[va])
    for tt in range(NT):
        b = PS.get()
        p = PS.f32(b)[:, 0:128]
        for c in range(NCH):
            l = xT[:, c, tt * 128:(tt + 1) * 128]
            r = w_v[:, c, :]
            T.op("pe", lambda e: e.matmul(out=p, lhsT=l, rhs=r, start=(c == 0), stop=(c == NCH - 1)),
                 reads=[l, r], writes=[p], inc=(c == NCH - 1))
        for g in range(2):
            src = p[:, g * 64:(g + 1) * 64]
            d0 = Vaug[g][0][:, tt, 0:64]
            d1 = Vaug[g][1][:, tt, 64:128]
            T.op("act", lambda e: e.copy(out=d0, in_=src), reads=[src], writes=[d0])
            T.op("dve", lambda e: e.tensor_copy(out=d1, in_=src), reads=[src], writes=[d1])
    AR.release("w_v")
    stop("B4")
    if debug:
        dump("qiT", qiT, [128, 4, S], BF16)
        dump("kiT2", kiT2, [128, S], BF16)
        dump("wi", wi, [128, NT, 8])
        dump("qT", qT, [128, 4, S], BF16)
        dump("kT2", kT2, [128, 2, S], BF16)
        dump("Vaug00", Vaug[0][0], [128, NT, 128], BF16)

    stop("B")
    xT_scr = nc.dram_tensor("xT_scr", [128, NCH, S], BF16, kind="Internal").ap()
    T.dma("sp", xT_scr, xT, "xTsp", reads=[xT], writes=["xT_scr"])
    AR.release("xT")

    y_bT = AR.alloc("y_bT", [4, S], BF16)
    maskTs = [AR.alloc("maskT%d" % i, [NT, 512], BF16) for i in range(2)]
    junks = [AR.alloc("junk%d" % i, [S], BF16) for i in range(2)]
    junkd = [AR.alloc("junkd%d" % i, [S], BF16) for i in range(1)]
    negtri_b = AR.alloc("negtri_b", [128], BF16)
    T.op("dve", lambda e: e.tensor_scalar(out=negtri_b, in0=negtri, scalar1=MASKNEG / NEG, scalar2=None, op0=ALU.mult),
         reads=[negtri], writes=[negtri_b])
    scores = [AR.alloc("score%d" % i, [S], F32) for i in range(4)]
    maskbs = [AR.alloc("maskb%d" % i, [S], BF16) for i in range(2)]
    rtmps = [AR.alloc("rtmp%d" % i, [512], BF16) for i in range(10)]
    dws = [AR.alloc("dw%d" % i, [8, 128], BF16) for i in range(4)]
    ptiles = [AR.alloc("ptile%d" % i, [512], BF16) for i in range(8)]
    rcs = [AR.alloc("rc%d" % i, [512], F32) for i in range(2)]
    ctr = {"rtmp": 0, "ptile": 0, "rc": 0, "maskb": 0, "junk": 0, "junkd": 0}

    def ring(lst, key):
        ctr[key] += 1
        return lst[ctr[key] % len(lst)]

    if debug:
        dbg_mask = dbg_out("mask", [NT, 128, S], BF16)
        dbg_score = dbg_out("score", [NT, 128, S], F32)

    def topk_chunk(m):
        maskT = maskTs[m % 2]
        tiles = list(range(4 * m, 4 * m + 4))
        sel = [i for i in tiles if i >= 2]
        nb = len(sel)
        col = {i: c for c, i in enumerate(sel)}
        for i in sel:
            N = 128 * (i + 1)
            c_ = col[i]
            score = scores[c_]
            for h in range(8):
                dwt = dws[c_][:, h, :]
                wsc = wi[:, i, h:h + 1]
                T.op("dve", lambda e: e.tensor_scalar(out=dwt, in0=ident, scalar1=wsc, scalar2=None, op0=ALU.mult),
                     reads=[ident, wsc], writes=[dwt])
            for kc in range((N + 511) // 512):
                k0 = kc * 512
                n = min(512, N - k0)
                sc = score[:, k0:k0 + n]
                sb_ = PS.get(hold=True)
                sacc = PS.f32(sb_)[:, 0:n]
                pendq = []

                def emit_acc(it):
                    tv_, h_ = it
                    dw_ = dws[c_][:, h_, :]
                    T.op("pe", lambda e: e.matmul(out=sacc, lhsT=dw_, rhs=tv_, start=(h_ == 0), stop=(h_ == 7)),
                         reads=[dw_, tv_], writes=[sacc], inc=(h_ == 7))
                for hp in range(4):
                    pair = []
                    for h in (2 * hp, 2 * hp + 1):
                        e2 = h % 2
                        b = PS.get()
                        p = PS.f32(b)[:, 0:n]
                        l = qiT[64 * e2:64 * e2 + 64, h // 2, i * 128:(i + 1) * 128]
                        r = kiT2[64 * e2:64 * e2 + 64, k0:k0 + n]
                        T.op("pe", lambda e: e.matmul(out=p, lhsT=l, rhs=r, start=True, stop=True),
                             reads=[l, r], writes=[p])
                        pair.append((h, p))
                    for h, p in pair:
                        tv = ring(rtmps, "rtmp")[:, 0:n]
                        if h in DVE_RELU_HEADS:
                            T.op("dve", lambda e: e.tensor_scalar(out=tv, in0=p, scalar1=0.0, scalar2=None, op0=ALU.max),
                                 reads=[p], writes=[tv])
                        else:
                            T.op("act", lambda e: e.activation(out=tv, in_=p, func=AF.Relu), reads=[p], writes=[tv])
                        pendq.append((tv, h))
                    while len(pendq) > 8:
                        emit_acc(pendq.pop(0))
                while pendq:
                    emit_acc(pendq.pop(0))
                if k0 + n == N:
                    nd = n - 128
                    if nd > 0:
                        T.op("dve", lambda e: e.tensor_copy(out=sc[:, 0:nd], in_=sacc[:, 0:nd]), reads=[sacc],
                             writes=[sc[:, 0:nd]])
                    T.op("dve", lambda e: e.tensor_tensor(out=sc[:, nd:n], in0=sacc[:, nd:n], in1=negtri, op=ALU.add),
                         reads=[sacc, negtri], writes=[sc[:, nd:n]])
                else:
                    T.op("dve", lambda e: e.tensor_copy(out=sc, in_=sacc), reads=[sacc], writes=[sc])
                PS.release(sb_)
                yield 'S'
        taus = {}
        if nb:
            half = (nb + 1) // 2
            groups = [g_ for g_ in (sel[:half], sel[half:]) if g_]
            sts = []
            for gi, grp in enumerate(groups):
                ng = len(grp)
                sm = AR.alloc("tk_small%d" % gi, [64 + 3 * NIT * 2], F32)
                st_ = {"hi": sm[:, 0:ng], "lo": sm[:, 2:2 + ng], "R": sm[:, 4:4 + ng], "nmid": sm[:, 6:6 + ng],
                       "tq": sm[:, 8:8 + ng], "tau": sm[:, 10:10 + ng], "npl": sm[:, 12:12 + ng],
                       "thr": sm[:, 14:14 + ng],
                       "Rk": sm[:, 64:64 + 2 * NIT].rearrange("p (k c) -> p k c", k=NIT),
                       "Rk2": sm[:, 64 + 2 * NIT:64 + 4 * NIT].rearrange("p (k c) -> p k c", k=NIT),
                       "cnt": sm[:, 64 + 4 * NIT:64 + 6 * NIT].rearrange("p (k c) -> p k c", k=NIT),
                       "mid": sm[:, 16:16 + ng], "grp": grp, "ng": ng, "gi": gi}
                sts.append(st_)
                for lc_, i in enumerate(grp):
                    c = col[i]
                    N = 128 * (i + 1)
                    sN = scores[c][:, 0:N]
                    sL = scores[c][:, 0:128 * i]
                    hc = st_["hi"][:, lc_:lc_ + 1]; lc = st_["lo"][:, lc_:lc_ + 1]; tc0 = st_["thr"][:, lc_:lc_ + 1]
                    T.op("dve", lambda e: e.memset(tc0, float(2 * TOPK - 2 - N) if gi == 0 else float(TOPK - 1)),
                         writes=[tc0])
                    T.op("dve", lambda e: e.tensor_reduce(out=hc, in_=sN, axis=AX.X, op=ALU.max), reads=[sN], writes=[hc])
                    T.op("dve", lambda e: e.tensor_reduce(out=lc, in_=sL, axis=AX.X, op=ALU.min), reads=[sL], writes=[lc])
                    yield 'S'
                hi = st_["hi"]; lo = st_["lo"]; R = st_["R"]; nmid = st_["nmid"]
                T.op("dve", lambda e: e.tensor_tensor(out=R, in0=hi, in1=lo, op=ALU.subtract), reads=[hi, lo], writes=[R])
                T.op("dve", lambda e: e.scalar_tensor_tensor(out=nmid, in0=R, scalar=-0.5, in1=lo, op0=ALU.mult,
                                                             op1=ALU.subtract), reads=[R, lo], writes=[nmid])
                mid_ = st_["mid"]
                T.op("dve", lambda e: e.tensor_scalar(out=mid_, in0=nmid, scalar1=-1.0, scalar2=None, op0=ALU.mult),
                     reads=[nmid], writes=[mid_])
                for lc_ in range(ng):
                    rc_ = R[:, lc_:lc_ + 1]
                    o1 = st_["Rk"][:, :, lc_]
                    o2 = st_["Rk2"][:, :, lc_]
                    T.op("dve", lambda e: e.tensor_scalar(out=o1, in0=pow2[:, 0:NIT], scalar1=rc_, scalar2=None,
                                                          op0=ALU.mult), reads=[pow2, rc_], writes=[o1])
                    T.op("dve", lambda e: e.tensor_scalar(out=o2, in0=pow2[:, NIT:2 * NIT], scalar1=rc_, scalar2=None,
                                                          op0=ALU.mult), reads=[pow2, rc_], writes=[o2])
            for k in range(NIT):
                for st_ in sts:
                    for lc_, i in enumerate(st_["grp"]):
                        c = col[i]
                        N = 128 * (i + 1)
                        sN = scores[c][:, 0:N]
                        jn = ring(junks, "junk")[:, 0:N]
                        ck = st_["cnt"][:, k, lc_:lc_ + 1]
                        if st_["gi"] == 0:
                            mc = st_["nmid"][:, lc_:lc_ + 1]
                            T.op("act", lambda e: e.activation(out=jn, in_=sN, func=AF.Sign, bias=mc, scale=1.0,
                                                               accum_out=ck), reads=[sN, mc], writes=[jn, ck])
                        else:
                            mc = st_["mid"][:, lc_:lc_ + 1]
                            jn = ring(junkd, "junkd")[:, 0:N]
                            T.op("dve", lambda e: e.tensor_scalar(out=jn, in0=sN, scalar1=mc, scalar2=0.0,
                                                                  op0=ALU.is_ge, op1=ALU.add, accum_out=ck),
                                 reads=[sN, mc], writes=[jn, ck])
                    yield 'B'
                for st_ in sts:
                    ng = st_["ng"]
                    r1 = st_["Rk"][:, k, 0:ng]
                    r2 = st_["Rk2"][:, k, 0:ng]
                    ckk = st_["cnt"][:, k, 0:ng]
                    nmid = st_["nmid"]; npl = st_["npl"]; tq = st_["tq"]; thr = st_["thr"]
                    if st_["gi"] == 1:
                        mid_ = st_["mid"]
                        T.op("dve", lambda e: e.tensor_tensor(out=npl, in0=mid_, in1=r1, op=ALU.subtract),
                             reads=[mid_, r1], writes=[npl])
                        T.op("dve", lambda e: e.scalar_tensor_tensor(out=tq, in0=ckk, scalar=TOPK - 0.5, in1=r2,
                                                                     op0=ALU.is_ge, op1=ALU.mult),
                             reads=[ckk, r2], writes=[tq])
                        T.op("dve", lambda e: e.tensor_tensor(out=mid_, in0=npl, in1=tq, op=ALU.add),
                             reads=[npl, tq], writes=[mid_])
                        continue
                    T.op("pool", lambda e: e.tensor_tensor(out=npl, in0=nmid, in1=r1, op=ALU.add), reads=[nmid, r1], writes=[npl])
                    T.op("pool", lambda e: e.tensor_tensor(out=tq, in0=ckk, in1=thr, op=ALU.subtract), reads=[ckk, thr], writes=[tq])
                    T.op("pool", lambda e: e.tensor_scalar(out=tq, in0=tq, scalar1=1.0, scalar2=0.0, op0=ALU.min,
                                                           op1=ALU.max), reads=[tq], writes=[tq])
                    T.op("pool", lambda e: e.tensor_tensor(out=tq, in0=tq, in1=r2, op=ALU.mult), reads=[tq, r2], writes=[tq])
                    T.op("pool", lambda e: e.tensor_tensor(out=nmid, in0=npl, in1=tq, op=ALU.subtract),
                         reads=[npl, tq], writes=[nmid])
            for st_ in sts:
                ng = st_["ng"]
                rl = st_["Rk"][:, NIT - 1, 0:ng]
                tau = st_["tau"]; nmid = st_["nmid"]
                if st_["gi"] == 1:
                    mid_ = st_["mid"]
                    T.op("dve", lambda e: e.tensor_tensor(out=tau, in0=mid_, in1=rl, op=ALU.subtract), reads=[mid_, rl],
                         writes=[tau])
                else:
                    T.op("dve", lambda e: e.tensor_tensor(out=tau, in0=nmid, in1=rl, op=ALU.add), reads=[nmid, rl],
                         writes=[tau])
                    T.op("dve", lambda e: e.tensor_scalar(out=tau, in0=tau, scalar1=-1.0, scalar2=None, op0=ALU.mult),
                         reads=[tau], writes=[tau])
                for lc_, i in enumerate(st_["grp"]):
                    taus[i] = tau[:, lc_:lc_ + 1]
        for i in tiles:
            N = 128 * (i + 1)
            maskb = ring(maskbs, "maskb")
            if i < 2:
                if i == 1:
                    T.op("dve", lambda e: e.memset(maskb[:, 0:128], 1.0), writes=[maskb[:, 0:128]])
                dd = maskb[:, 128 * i:128 * (i + 1)]
                T.op("dve", lambda e: e.tensor_copy(out=dd, in_=tri), reads=[tri], writes=[dd])
            else:
                c = col[i]
                sN = scores[c][:, 0:N]
                mN = maskb[:, 0:N]
                tc_ = taus[i]
                T.op("dve", lambda e: e.tensor_scalar(out=mN, in0=sN, scalar1=tc_, scalar2=None, op0=ALU.is_ge),
                     reads=[sN, tc_], writes=[mN])
                if debug:
                    T.dma("sp", dbg_score[i, :, 0:N], sN, "dbg", reads=[sN], writes=["dbg_score"])
            if debug:
                T.dma("sp", dbg_mask[i, :, 0:N], maskb[:, 0:N], "dbg", reads=[maskb[:, 0:N]], writes=["dbg_mask"])
            off = (i - 4 * m) * 128
            for j0 in range(0, i + 1, 8):
                nbk = min(8, i + 1 - j0)
                b = PS.get()
                pb = PS.bf16(b)
                for jj in range(nbk):
                    j = j0 + jj
                    o = pb[:, jj * 128:(jj + 1) * 128]
                    i_ = maskb[:, j * 128:(j + 1) * 128]
                    T.op("pe", lambda e: e.transpose(out=o, in_=i_, identity=ident), reads=[i_, ident], writes=[o],
                         inc=(jj == nbk - 1))
                dst = maskT[:, j0:j0 + nbk, off:off + 128]
                src = pb[:, 0:nbk * 128].rearrange("p (j t) -> p j t", j=nbk)
                T.op("act", lambda e: e.copy(out=dst, in_=src), reads=[pb[:, 0:nbk * 128]], writes=[dst])
            yield 'B'
        if nb:
            for gi in range(len(groups)):
                AR.release("tk_small%d" % gi)

    def attn_chunk(m):
        maskT = maskTs[m % 2]
        jmax = 4 * m + 3
        for cq in range(4):
            g = cq // 2
            abs_ = [PS.get(hold=True) for _ in range(2)]
            accs = [PS.f32(ab) for ab in abs_]
            pend = []

            def emit_pv(it):
                va_, pv_, ao_, j_ = it
                T.op("pe", lambda e: e.matmul(out=ao_, lhsT=va_, rhs=pv_, start=(j_ == 0), stop=(j_ == jmax)),
                     reads=[va_, pv_], writes=[ao_], inc=(j_ == jmax))
            for j in range(jmax + 1):
                t0 = max(512 * m, 128 * j)
                n = 512 * (m + 1) - t0
                off = t0 - 512 * m
                ps_ = []
                for e2 in range(2):
                    b = PS.get()
                    p = PS.f32(b)[:, 0:n]
                    l = kT2[64 * e2:64 * e2 + 64, g, j * 128:(j + 1) * 128]
                    r = qT[64 * e2:64 * e2 + 64, cq, t0:t0 + n]
                    T.op("pe", lambda e: e.matmul(out=p, lhsT=l, rhs=r, start=True, stop=True), reads=[l, r], writes=[p])
                    ps_.append(p)
                mt = maskT[:, j, off:off + n]
                for e2 in range(2):
                    p = ps_[e2]
                    pv = ring(ptiles, "ptile")[:, 0:n]
                    T.op("act", lambda e: e.activation(out=pv, in_=p, func=AF.Exp, scale=0.125), reads=[p], writes=[pv])
                    T.op("dve", lambda e: e.tensor_tensor(out=pv, in0=pv, in1=mt, op=ALU.mult), reads=[pv, mt],
                         writes=[pv])
                    va = Vaug[g][e2][:, j, :]
                    ao = accs[e2][:, off:off + n]
                    pend.append((va, pv, ao, j))
                while len(pend) > 2:
                    emit_pv(pend.pop(0))
                if j % 2 == 1:
                    yield
            while pend:
                emit_pv(pend.pop(0))
            for e2 in range(2):
                acc = accs[e2]
                rc = ring(rcs, "rc")
                po = 64 * e2
                pd = 64 * (1 - e2)
                rcv = rc[po:po + 64, :]
                den = acc[pd:pd + 64, :]
                T.op("dve", lambda e: e.reciprocal(out=rcv, in_=den), reads=[den], writes=[rcv])
                yo = y_bT[po:po + 64, cq, 512 * m:512 * (m + 1)]
                num = acc[po:po + 64, :]
                T.op("dve", lambda e: e.tensor_tensor(out=yo, in0=num, in1=rcv, op=ALU.mult), reads=[num, rcv],
                     writes=[yo])
                PS.release(abs_[e2])
            yield

    def run_interleaved(ga, gb, ra=1, rb=1, warm=0):
        alive = [ga is not None, gb is not None]
        gens = [ga, gb]
        reps = [ra, rb]
        for _ in range(warm):
            try:
                next(ga)
            except StopIteration:
                alive[0] = False
                break
        while any(alive):
            for q in range(2):
                if not alive[q]:
                    continue
                for _ in range(reps[q]):
                    try:
                        next(gens[q])
                    except StopIteration:
                        alive[q] = False
                        break

    def run_weighted(ga, gb, wS, wB):
        credit = 0.0
        alive = True
        for tag in gb:
            credit += wS if tag == 'S' else wB
            while alive and credit >= 1.0:
                try:
                    next(ga)
                except StopIteration:
                    alive = False
                credit -= 1.0
        if alive:
            for _ in ga:
                pass

    run_interleaved(topk_chunk(0), None)
    for m in range(3):
        nA = 4 * (2 * m + 3)
        nS = 4 * (m + 2) + 4
        nB = 2 * NIT + 4
        wS = 0.05
        run_weighted(attn_chunk(m), topk_chunk(m + 1), wS, max(0.3, (nA - wS * nS) / nB))
    for n_ in (["junk0", "junk1", "junkd0", "negtri_b", "qiT", "kiT2", "wi"] + ["score%d" % i for i in range(4)]
               + ["maskb%d" % i for i in range(2)] + ["rtmp%d" % i for i in range(10)] + ["dw%d" % i for i in range(4)]):
        AR.release(n_)
    xT = AR.alloc("xT", [NCH, S], BF16)
    T.dma("sp", xT, xT_scr, "xTsp", reads=["xT_scr"], writes=[xT])
    w_c = load_w_cols("w_c", C_C, 512)
    w_u = load_w_cols("w_u", C_U, 512)
    w_b = load_w_cols("w_b", C_B, 512)
    def proj_chunk(w, wc0, m):
        b = PS.get()
        p = PS.f32(b)
        for c in range(NCH):
            l = w[:, c, wc0:wc0 + 128]
            r = xT[:, c, m * 512:(m + 1) * 512]
            T.op("pe", lambda e: e.matmul(out=p, lhsT=l, rhs=r, start=(c == 0), stop=(c == NCH - 1)),
                 reads=[l, r], writes=[p], inc=(c == NCH - 1))
        return p

    def conv_phase():
        for j in range(4):
            c_sb = AR.ralloc("c_sb", [S], F32, 1)
            vpad = AR.ralloc("vpad", [S + 64], F32, 2)
            cacc = AR.ralloc("cacc", [S], F32, 1)
            T.op("pool", lambda e: e.memset(vpad[:, 0:2], 0.0), writes=[vpad[:, 0:2]])
            for m in range(4):
                p = proj_chunk(w_c, j * 128, m)
                evac_copy(c_sb[:, m * 512:(m + 1) * 512], p)
                yield
            for m in range(4):
                p = proj_chunk(w_u, j * 128, m)
                d_ = vpad[:, 2 + m * 512:2 + (m + 1) * 512]
                s_ = c_sb[:, m * 512:(m + 1) * 512]
                T.op("dve", lambda e: e.tensor_tensor(out=d_, in0=p, in1=s_, op=ALU.mult), reads=[p, s_], writes=[d_])
                yield
            w0 = convw[:, j * 3 + 0:j * 3 + 1]
            w1 = convw[:, j * 3 + 1:j * 3 + 2]
            w2 = convw[:, j * 3 + 2:j * 3 + 3]
            v2 = vpad[:, 2:S + 2]; v1 = vpad[:, 1:S + 1]; v0 = vpad[:, 0:S]
            T.op("dve", lambda e: e.tensor_scalar(out=cacc, in0=v2, scalar1=w2, scalar2=None, op0=ALU.mult),
                 reads=[v2, w2], writes=[cacc])
            yield
            T.op("dve", lambda e: e.scalar_tensor_tensor(out=cacc, in0=v1, scalar=w1, in1=cacc, op0=ALU.mult,
                                                         op1=ALU.add), reads=[v1, w1, cacc], writes=[cacc])
            yield
            T.op("dve", lambda e: e.scalar_tensor_tensor(out=cacc, in0=v0, scalar=w0, in1=cacc, op0=ALU.mult,
                                                         op1=ALU.add), reads=[v0, w0, cacc], writes=[cacc])
            yield
            for m in range(4):
                p = proj_chunk(w_b, j * 128, m)
                d_ = y_aT[:, j, m * 512:(m + 1) * 512]
                s_ = cacc[:, m * 512:(m + 1) * 512]
                T.op("dve", lambda e: e.tensor_tensor(out=d_, in0=p, in1=s_, op=ALU.mult), reads=[p, s_], writes=[d_])
                yield

    AR.release("maskT0")
    y_aT = AR.alloc("y_aT", [4, S], BF16)
    run_interleaved(attn_chunk(3), conv_phase(), 1, 1, warm=10)
    for n_ in (["maskT1", "qT", "kT2", "Vaug00", "Vaug01", "Vaug10", "Vaug11"]
               + ["ptile%d" % i for i in range(8)] + ["rc%d" % i for i in range(2)]):
        AR.release(n_)
    AR.release("w_c"); AR.release("w_u"); AR.release("w_b")
    for n_ in ("c_sb", "vpad", "cacc"):
        AR.rfree(n_)
    if debug:
        dump("y_bT", y_bT, [128, 4, S], BF16)

    stop("CD")
    w_gas = [None, None]
    w_gbs = [None, None]
    w_gas[0] = load_w_cols("w_ga0", C_GA, 512)
    w_gbs[0] = load_w_cols("w_gb0", C_GB, 512)
    w_pa = AR.alloc("w_pa", [4, D], BF16)
    w_pb = AR.alloc("w_pb", [4, D], BF16)
    T.dma("pool", w_pa, w_br_d[0].rearrange("(k p) n -> p k n", p=128), "w_pa", writes=[w_pa])
    T.dma("pool", w_pb, w_br_d[1].rearrange("(k p) n -> p k n", p=128), "w_pb", writes=[w_pb])
    if debug:
        dump("y_aT", y_aT, [128, 4, S], BF16)

    stop("E")
    mT = AR.alloc("mT", [NCH, S], BF16)
    w_gas[1] = load_w_cols("w_ga1", C_GA + 512, 512)
    w_gbs[1] = load_w_cols("w_gb1", C_GB + 512, 512)
    w_o = AR.alloc("w_o", [NCH, D], BF16)
    w_o_v = w_o_d.rearrange("(c p) n -> p c n", p=128)
    for q in range(2):
        T.dma("pool", w_o[:, 4 * q:4 * q + 4, :], w_o_v[:, 4 * q:4 * q + 4, :], "w_o%d" % q, writes=[w_o[:, 4 * q:4 * q + 4, :]])
    for half in range(2):
        w_ga = w_gas[half]
        w_gb = w_gbs[half]
        for dq in range(4):
            dc = half * 4 + dq
            for m in range(4):
                tsl = slice(m * 512, (m + 1) * 512)
                sa = AR.ralloc("sa", [512], F32, 3)
                sb = AR.ralloc("sb", [512], F32, 3)
                for (wg, sg) in ((w_ga, sa), (w_gb, sb)):
                    b = PS.get()
                    p = PS.f32(b)
                    for c in range(NCH):
                        l = wg[:, c, dq * 128:(dq + 1) * 128]
                        r = xT[:, c, tsl]
                        T.op("pe", lambda e: e.matmul(out=p, lhsT=l, rhs=r, start=(c == 0), stop=(c == NCH - 1)),
                             reads=[l, r], writes=[p], inc=(c == NCH - 1))
                    T.op("act", lambda e: e.activation(out=sg, in_=p, func=AF.Sigmoid), reads=[p], writes=[sg])
                t1 = AR.ralloc("t1", [512], F32, 3)
                t2 = AR.ralloc("t2", [512], F32, 3)
                for (wp, yT, sg, tt_) in ((w_pa, y_aT, sa, t1), (w_pb, y_bT, sb, t2)):
                    b = PS.get()
                    p = PS.f32(b)
                    for k in range(4):
                        l = wp[:, k, dc * 128:(dc + 1) * 128]
                        r = yT[:, k, tsl]
                        T.op("pe", lambda e: e.matmul(out=p, lhsT=l, rhs=r, start=(k == 0), stop=(k == 3)),
                             reads=[l, r], writes=[p], inc=(k == 3))
                    T.op("dve", lambda e: e.tensor_tensor(out=tt_, in0=p, in1=sg, op=ALU.mult),
                         reads=[p, sg], writes=[tt_])
                md = mT[:, dc, tsl]
                T.op("pool", lambda e: e.tensor_tensor(out=md, in0=t1, in1=t2, op=ALU.add), reads=[t1, t2], writes=[md])
                for n_ in ("sa", "sb", "t1", "t2"):
                    AR.release(n_)
        AR.release("w_ga%d" % half); AR.release("w_gb%d" % half)
    for n_ in ("xT", "y_aT", "y_bT", "w_pa", "w_pb"):
        AR.release(n_)
    for n_ in ("sa", "sb", "t1", "t2"):
        AR.rfree(n_)
    if debug:
        dump("mT", mT, [128, NCH, S], BF16)

    stop("F1")
    hT = AR.alloc("hT", [NCH, S], BF16)
    lng = AR.alloc("lng", [D], F32)
    lnb = AR.alloc("lnb", [D], F32)
    T.dma("sp", lng, ln1g_d, "lnp", writes=[lng])
    T.dma("sp", lnb, ln1b_d, "lnp", writes=[lnb], chain=True)
    w_dn_q = [AR.alloc("w_dn%d" % qq, [8, D], BF16) for qq in range(4)]
    w_dn_v = w_down_d.rearrange("(f p) n -> p f n", p=128)
    for q in range(8):
        dq_ = w_dn_q[q // 2][:, 4 * (q % 2):4 * (q % 2) + 4, :]
        T.dma("pool", dq_, w_dn_v[:, 4 * q:4 * q + 4, :], "w_dn%d" % q, writes=[dq_])

    def layernorm_tile(r, g_t, b_t, out_t, tag):
        st = AR.ralloc("lst" + tag, [16], F32, 2)
        mv = AR.ralloc("lmv" + tag, [8], F32, 2)
        for q in range(2):
            rq = r[:, q * 512:(q + 1) * 512]
            sq = st[:, q * 6:(q + 1) * 6]
            T.op("dve", lambda e: e.bn_stats(out=sq, in_=rq), reads=[rq], writes=[sq])
        s12 = st[:, 0:12]
        T.op("dve", lambda e: e.bn_aggr(out=mv[:, 0:2], in_=s12), reads=[s12], writes=[mv[:, 0:2]])
        rstd_from_var(mv)
        nmr = mv[:, 4:5]
        T.op("dve", lambda e: e.scalar_tensor_tensor(out=nmr, in0=mv[:, 0:1], scalar=-1.0, in1=mv[:, 2:3],
                                                     op0=ALU.mult, op1=ALU.mult), reads=[mv[:, 0:3]], writes=[nmr])
        T.op("act", lambda e: e.activation(out=out_t, in_=r, func=AF.Identity, bias=nmr, scale=mv[:, 2:3]),
             reads=[r, mv[:, 2:5]], writes=[out_t])
        T.op("dve", lambda e: e.tensor_tensor(out=out_t, in0=out_t, in1=g_t, op=ALU.mult), reads=[out_t, g_t],
             writes=[out_t])
        T.op("dve", lambda e: e.tensor_tensor(out=out_t, in0=out_t, in1=b_t, op=ALU.add), reads=[out_t, b_t],
             writes=[out_t])
        AR.release("lst" + tag); AR.release("lmv" + tag)

    g1c = AR.alloc("g1c", [8], F32)
    b1c = AR.alloc("b1c", [8], F32)
    T.dma("sp", g1c, ln1gc_d, "lnc", writes=[g1c])
    T.dma("sp", b1c, ln1bc_d, "lnc", writes=[b1c], chain=True)

    def emit_hT(nb_, tt_):
        b_ = PS.get()
        pb_ = PS.bf16(b_)
        for c in range(NCH):
            o = pb_[:, c * 128:(c + 1) * 128]
            i_ = nb_[:, c * 128:(c + 1) * 128]
            T.op("pe", lambda e: e.transpose(out=o, in_=i_, identity=ident), reads=[i_, ident], writes=[o],
                 inc=(c == NCH - 1))
        for c in range(NCH):
            src = pb_[:, c * 128:(c + 1) * 128]
            dst = hT[:, c, tt_ * 128:(tt_ + 1) * 128]
            gc = g1c[:, c:c + 1]
            bc = b1c[:, c:c + 1]
            T.op("act", lambda e: e.activation(out=dst, in_=src, func=AF.Identity, bias=bc, scale=gc),
                 reads=[src, gc, bc], writes=[dst])

    pend_nb = []
    xq = []

    def issue_xload(t_):
        xt_ = AR.ralloc("xr", [D], F32, 4)
        T.dma("sp", xt_, x_d[t_ * 128:(t_ + 1) * 128, :], "xr%d" % (t_ % 4), writes=[xt_])
        xq.append(xt_)
    issue_xload(0)
    issue_xload(1)
    for tt in range(NT):
        if tt + 2 < NT:
            issue_xload(tt + 2)
        xt = xq.pop(0)
        r = AR.ralloc("r1", [D], F32, 2)
        for half in range(2):
            b = PS.get()
            p = PS.f32(b)
            for dc in range(NCH):
                l = mT[:, dc, tt * 128:(tt + 1) * 128]
                rr = w_o[:, dc, half * 512:(half + 1) * 512]
                T.op("pe", lambda e: e.matmul(out=p, lhsT=l, rhs=rr, start=(dc == 0), stop=(dc == NCH - 1)),
                     reads=[l, rr], writes=[p], inc=(dc == NCH - 1))
            xh = xt[:, half * 512:(half + 1) * 512]
            rh = r[:, half * 512:(half + 1) * 512]
            T.op("dve", lambda e: e.scalar_tensor_tensor(out=rh, in0=xh, scalar=ALPHA, in1=p, op0=ALU.mult, op1=ALU.add),
                 reads=[xh, p], writes=[rh])
        st = AR.ralloc("lst1", [16], F32, 2)
        mv = AR.ralloc("lmv1", [8], F32, 2)
        for q in range(2):
            rq = r[:, q * 512:(q + 1) * 512]
            sq = st[:, q * 6:(q + 1) * 6]
            T.op("dve", lambda e: e.bn_stats(out=sq, in_=rq), reads=[rq], writes=[sq])
        s12 = st[:, 0:12]
        T.op("dve", lambda e: e.bn_aggr(out=mv[:, 0:2], in_=s12), reads=[s12], writes=[mv[:, 0:2]])
        rstd_from_var(mv)
        nmr = mv[:, 4:5]
        T.op("dve", lambda e: e.scalar_tensor_tensor(out=nmr, in0=mv[:, 0:1], scalar=-1.0, in1=mv[:, 2:3],
                                                     op0=ALU.mult, op1=ALU.mult), reads=[mv[:, 0:3]], writes=[nmr])
        nn = AR.ralloc("nn", [D], F32, 2)
        T.op("act", lambda e: e.activation(out=nn, in_=r, func=AF.Identity, bias=nmr, scale=mv[:, 2:3]),
             reads=[r, mv[:, 2:5]], writes=[nn])
        nb_t = AR.ralloc("nbt", [D], BF16, 4)
        T.op("dve", lambda e: e.tensor_copy(out=nb_t, in_=nn), reads=[nn], writes=[nb_t])
        hh = AR.ralloc("hh", [D], F32, 2)
        T.op("dve", lambda e: e.tensor_tensor(out=hh, in0=nn, in1=lng, op=ALU.mult), reads=[nn, lng], writes=[hh])
        T.op("dve", lambda e: e.tensor_tensor(out=hh, in0=hh, in1=lnb, op=ALU.add), reads=[hh, lnb], writes=[hh])
        T.dma("sp", h_scr[tt * 128:(tt + 1) * 128, :], hh, "hst%d" % (tt % 2), reads=[hh], writes=[("h", tt)])
        pend_nb.append((nb_t, tt))
        if len(pend_nb) > 2:
            emit_hT(*pend_nb.pop(0))
    AR.release("mT"); AR.release("w_o")
    AR.rfree("xr"); AR.rfree("r1")
    upT = AR.alloc("upT", [NF, 512], BF16)
    w_up_v = w_up_d.rearrange("(c p) n -> p c n", p=128)
    NSLOT = 3
    slots = [AR.alloc("wup%d" % i, [NCH, 512], BF16) for i in range(NSLOT)]

    def issue_up_load(idx):
        fq = idx % 8
        sl = slots[idx % NSLOT]
        T.dma("pool", sl, w_up_v[:, :, fq * 512:(fq + 1) * 512], "wup%d" % (idx % NSLOT), writes=[sl])

    total_loads = 4 * 8
    for idx in range(min(NSLOT, total_loads)):
        issue_up_load(idx)
    nxt_box = [NSLOT]

    def up_gen(G):
        tsl = slice(G * 512, (G + 1) * 512)
        for fq in range(8):
            idx = G * 8 + fq
            sl = slots[idx % NSLOT]
            for f4 in range(4):
                f = fq * 4 + f4
                b = PS.get()
                p = PS.f32(b)
                for c in range(NCH):
                    l = sl[:, c, f4 * 128:(f4 + 1) * 128]
                    r = hT[:, c, tsl]
                    T.op("pe", lambda e: e.matmul(out=p, lhsT=l, rhs=r, start=(c == 0), stop=(c == NCH - 1)),
                         reads=[l, r], writes=[p], inc=(c == NCH - 1))
                rt = AR.ralloc("relu_t", [512], BF16, 3)
                T.op("act", lambda e: e.activation(out=rt, in_=p, func=AF.Relu), reads=[p], writes=[rt])
                ud = upT[:, f, :]
                T.op("dve", lambda e: e.tensor_tensor(out=ud, in0=rt, in1=rt, op=ALU.mult), reads=[rt], writes=[ud])
                yield
            if nxt_box[0] < total_loads:
                issue_up_load(nxt_box[0])
                nxt_box[0] += 1

    up0 = up_gen(0)
    while pend_nb:
        for _ in range(6):
            next(up0)
        emit_hT(*pend_nb.pop(0))
    AR.release("g1c"); AR.release("b1c")
    for n_ in ("hh", "nn", "nbt", "lst1", "lmv1"):
        AR.rfree(n_)

    stop("F2")
    T.dma("sp", lng, ln2g_d, "lnp", writes=[lng])
    T.dma("sp", lnb, ln2b_d, "lnp", writes=[lnb], chain=True)
    for G in range(4):
        for _ in (up0 if G == 0 else up_gen(G)):
            pass
        hts = []
        for tq in range(4):
            tt = G * 4 + tq
            ht_ = AR.ralloc("hr", [D], F32, 4)
            T.dma("sp", ht_, h_scr[tt * 128:(tt + 1) * 128, :], "hr%d" % tq, reads=[("h", tt)], writes=[ht_])
            hts.append(ht_)
        for tq in range(4):
            tt = G * 4 + tq
            ht = hts[tq]
            r = AR.ralloc("r2", [D], F32, 2)
            for half in range(2):
                b = PS.get()
                p = PS.f32(b)
                for f in range(NF):
                    l = upT[:, f, tq * 128:(tq + 1) * 128]
                    rr = w_dn_q[f // 8][:, f % 8, half * 512:(half + 1) * 512]
                    T.op("pe", lambda e: e.matmul(out=p, lhsT=l, rhs=rr, start=(f == 0), stop=(f == NF - 1)),
                         reads=[l, rr], writes=[p], inc=(f == NF - 1))
                hq = ht[:, half * 512:(half + 1) * 512]
                rh = r[:, half * 512:(half + 1) * 512]
                T.op("dve", lambda e: e.scalar_tensor_tensor(out=rh, in0=hq, scalar=ALPHA, in1=p, op0=ALU.mult,
                                                             op1=ALU.add), reads=[hq, p], writes=[rh])
            AR.release("hr")
            ot = AR.ralloc("ot", [D], F32, 2)
            layernorm_tile(r, lng, lnb, ot, "2")
            AR.release("r2")
            T.dma("sp", out_d[tt * 128:(tt + 1) * 128, :], ot, "ost%d" % (tt % 2), reads=[ot], writes=[("o", tt)])
            AR.release("ot")
    return


def _host_consts():
    ident = np.eye(128, dtype=np.float32).astype(ml_dtypes.bfloat16)
    tri = np.triu(np.ones((128, 128), dtype=np.float32)).T
    negtri = np.where(tri > 0, 0.0, NEG).astype(np.float32)
    k = np.arange(NIT)
    p2 = np.concatenate([2.0 ** -(k + 2.0), 2.0 * 2.0 ** -(k + 2.0)]).astype(np.float32)
    pow2 = np.ascontiguousarray(np.broadcast_to(p2[None, :], (128, 2 * NIT))).astype(np.float32)
    return ident, tri.astype(ml_dtypes.bfloat16), negtri, pow2


def _bcast(v, n=128):
    v = np.asarray(v, dtype=np.float32).reshape(1, -1)
    return np.ascontiguousarray(np.broadcast_to(v, (n, v.shape[1])))


def make_in_maps(inputs, cores):
    ident, tri, negtri, pow2 = _host_consts()
    cw = np.asarray(inputs["conv_w"], dtype=np.float32)[0]
    conv_w_t = np.ascontiguousarray(cw.reshape(3, 4, 128).transpose(2, 1, 0).reshape(128, 12))
    shared = {
        "w_in": np.ascontiguousarray(inputs["w_in"][0], dtype=np.float32),
        "conv_w_t": conv_w_t,
        "ikg": _bcast(inputs["idx_k_norm_g"][0]),
        "ikb": _bcast(inputs["idx_k_norm_b"][0]),
        "w_branch": np.ascontiguousarray(inputs["w_branch"][0], dtype=np.float32),
        "w_o": np.ascontiguousarray(inputs["w_o"][0], dtype=np.float32),
        "ln1g": _bcast(inputs["ln1_g"][0]),
        "ln1b": _bcast(inputs["ln1_b"][0]),
        "ln1gc": np.ascontiguousarray(np.asarray(inputs["ln1_g"][0], dtype=np.float32).reshape(8, 128).T),
        "ln1bc": np.ascontiguousarray(np.asarray(inputs["ln1_b"][0], dtype=np.float32).reshape(8, 128).T),
        "w_up": np.ascontiguousarray(inputs["w_up"][0], dtype=np.float32),
        "w_down": np.ascontiguousarray(inputs["w_down"][0], dtype=np.float32),
        "ln2g": _bcast(inputs["ln2_g"][0]),
        "ln2b": _bcast(inputs["ln2_b"][0]),
        "ident": ident, "tri": tri, "negtri": negtri, "pow2": pow2,
    }
    x = np.asarray(inputs["x"], dtype=np.float32)
    maps = []
    for b in cores:
        m = dict(shared)
        m["x"] = np.ascontiguousarray(x[b])
        maps.append(m)
    return maps


def kernel(**inputs):
    nc, info = build_program(debug=False)
    in_maps = make_in_maps(inputs, list(range(8)))
    res = run_bass_kernel_spmd(nc, in_maps, core_ids=list(range(8)))
    out = np.stack([np.asarray(r["out"], dtype=np.float32) for r in res.results], axis=0)
    return out
```

```python
import os
import numpy as np
import ml_dtypes
import concourse.bass as bass
import concourse.mybir as mybir
from concourse.bass_utils import run_bass_kernel_spmd

F32 = mybir.dt.float32
BF16 = mybir.dt.bfloat16
ALU = mybir.AluOpType
AF = mybir.ActivationFunctionType
AX = mybir.AxisListType

S = 2048
D = 1024
NT = S // 128
NCH = D // 128
D_IN = 4936
DFF = 4096
NF = DFF // 128
ALPHA = 2.0 ** 0.25
LN_EPS = 1e-5
TOPK = 256
NIT = 10
NEG = -1.0e30
MASKNEG = -30000.0
DVE_RELU_HEADS = (1, 3, 5, 7)

C_B, C_C, C_U, C_Q, C_K, C_V, C_QI, C_KI, C_WI, C_GA, C_GB = 0, 512, 1024, 1536, 2048, 2176, 2304, 2816, 2880, 2888, 3912

SEM_LIMIT = 30000


def _esize(dt):
    return {F32: 4, BF16: 2}.get(dt, 4)


class _Eng:
    def __init__(self, name, eng, sem):
        self.name = name
        self.eng = eng
        self.sem = sem
        self.count = 0
        self.pending = False
        self.seen = {}


class Tracker:
    BLK = 256

    def __init__(self, nc):
        self.nc = nc
        self.nsem = 0
        self.engs = {}
        for name, eng in (("pe", nc.tensor), ("act", nc.scalar), ("dve", nc.vector),
                          ("pool", nc.gpsimd), ("sp", nc.sync)):
            self.engs[name] = _Eng(name, eng, self._newsem(name))
        self.blocks = {}
        self.dma_sems = {}
        self.dma_sems_by_id = {}
        self.nwaits = 0
        self.ninst = 0

    def _newsem(self, name):
        self.nsem += 1
        return self.nc.alloc_semaphore("s_%s_%d" % (name, self.nsem))

    def _keys(self, ap):
        t = ap.tensor
        space = "P" if "PSum" in type(t).__name__ else "S"
        pairs = ap.ap
        es = _esize(ap.dtype)
        pstride = pairs[0][0]
        npart = pairs[0][1]
        p0 = ap.offset // pstride if pstride else 0
        base = (ap.offset % pstride) * es if pstride else ap.offset * es
        halves = set()
        if p0 < 64:
            halves.add(0)
        if p0 + npart > 64:
            halves.add(1)
        free = pairs[1:]
        ranges = []
        if not free:
            ranges.append((base, base + es))
        else:
            outer = free[:-1]
            lstep, lcnt = free[-1]
            span = ((lcnt - 1) * abs(lstep) + 1) * es

            def rec(i, off):
                if i == len(outer):
                    ranges.append((off, off + span))
                    return
                st, cn = outer[i]
                for k in range(cn):
                    rec(i + 1, off + k * st * es)
            rec(0, base)
        keys = set()
        B = 2048 if space == "P" else self.BLK
        for lo, hi in ranges:
            for b in range(lo // B, (hi - 1) // B + 1):
                for h in halves:
                    keys.add((space, h, b))
        return keys

    def _allkeys(self, items):
        keys = set()
        for it in items:
            if it is None:
                continue
            if isinstance(it, (str, tuple)):
                keys.add(("D", it))
            else:
                keys |= self._keys(it)
        return keys

    def _deps(self, ename, rkeys, wkeys, is_dma):
        deps = {}

        def add(tk):
            sem, val, own = tk
            if (not is_dma) and own == ename and ename == "pe":
                return
            k = id(sem)
            if k not in deps or deps[k][1] < val:
                deps[k] = (sem, val, own)

        def add_waw(tk):
            sem, val, own = tk
            if (not is_dma) and own == ename and ename == "pe":
                return
            k = id(sem)
            if k not in deps or deps[k][1] < val:
                deps[k] = (sem, val, own)

        def add_raw(tk):
            sem, val, own = tk
            if (not is_dma) and own == ename and ename == "pe":
                return
            k = id(sem)
            if k not in deps or deps[k][1] < val:
                deps[k] = (sem, val, own)

        for key in rkeys:
            st = self.blocks.get(key)
            if st and st["w"] is not None:
                add_raw(st["w"])
            if st and key[0] == "P":
                for tk in st["r"].values():
                    add(tk)
        for key in wkeys:
            st = self.blocks.get(key)
            if st:
                if st["w"] is not None:
                    add_waw(st["w"])
                for tk in st["r"].values():
                    add(tk)
        return deps

    def _wait(self, E, deps):
        for k, (sem, val, own) in deps.items():
            if E.seen.get(k, 0) >= val:
                continue
            if own in self.engs:
                P = self.engs[own]
                if P.sem is sem:
                    assert val <= P.count, "wait on future inc (%s waits %s)" % (E.name, own)
            elif own.startswith("dma:"):
                val = self.dma_sems_by_id[k][1]
            E.eng.wait_ge(sem, val)
            E.seen[k] = val
            self.nwaits += 1

    def _update(self, rkeys, wkeys, tk):
        for key in rkeys:
            st = self.blocks.setdefault(key, {"w": None, "r": {}})
            k = id(tk[0])
            old = st["r"].get(k)
            if old is None or old[1] < tk[1]:
                st["r"][k] = tk
        for key in wkeys:
            self.blocks[key] = {"w": tk, "r": {}}

    def op(self, ename, fn, reads=(), writes=(), inc=True):
        E = self.engs[ename]
        rkeys = self._allkeys(reads)
        wkeys = self._allkeys(writes)
        deps = self._deps(ename, rkeys, wkeys, False)
        self._wait(E, deps)
        inst = fn(E.eng)
        self.ninst += 1
        if inc:
            E.count += 1
            inst.then_inc(E.sem, 1)
            E.pending = False
            tk = (E.sem, E.count, ename)
        else:
            E.pending = True
            tk = (E.sem, E.count + 1, ename)
        self._update(rkeys, wkeys, tk)
        if inc and E.count >= SEM_LIMIT:
            E.sem = self._newsem(ename)
            E.count = 0
        return inst

    def dma(self, qname, out, in_, skey, reads=(), writes=(), chain=False):
        E = self.engs[qname]
        rkeys = self._allkeys(list(reads))
        wkeys = self._allkeys(list(writes))
        deps = self._deps(qname, rkeys, wkeys, True)
        self._wait(E, deps)
        if skey not in self.dma_sems:
            self.dma_sems[skey] = [self._newsem("dma"), 0]
            self.dma_sems_by_id[id(self.dma_sems[skey][0])] = self.dma_sems[skey]
        rec = self.dma_sems[skey]
        if (not chain) and rec[1] > 0 and E.seen.get(id(rec[0]), 0) < rec[1]:
            E.eng.wait_ge(rec[0], rec[1])
            E.seen[id(rec[0])] = rec[1]
            self.nwaits += 1
        rec[1] += 16
        E.eng.dma_start(out=out, in_=in_).then_inc(rec[0], 16)
        self.ninst += 1
        tk = (rec[0], rec[1], "dma:" + str(skey))
        self._update(rkeys, wkeys, tk)

    def final_wait(self, qname, skey):
        rec = self.dma_sems[skey]
        self.engs[qname].eng.wait_ge(rec[0], rec[1])


class Arena:
    def __init__(self, nc, nbytes):
        self.nbytes = nbytes // 256 * 256
        self.t = nc.alloc_sbuf_tensor("arena", [128, self.nbytes // 2], BF16)
        self.free = [(0, self.nbytes)]
        self.live = {}
        self.rings = {}
        self.peak = 0

    def alloc(self, name, shape, dt):
        n = 1
        for s_ in shape:
            n *= s_
        nb = (n * _esize(dt) + 255) // 256 * 256
        for i, (lo, hi) in enumerate(self.free):
            if hi - lo >= nb:
                self.free[i] = (lo + nb, hi)
                if self.free[i][0] == self.free[i][1]:
                    self.free.pop(i)
                self.live[name] = (lo, nb)
                used = self.nbytes - sum(h - l for l, h in self.free)
                self.peak = max(self.peak, used)
                v = self.t[:, lo // 2:(lo + nb) // 2]
                if dt == F32:
                    v = v.bitcast(F32)
                v = v[:, 0:n]
                if len(shape) == 2:
                    v = v.rearrange("p (a b) -> p a b", a=shape[0])
                elif len(shape) == 3:
                    v = v.rearrange("p (a b c) -> p a b c", a=shape[0], b=shape[1])
                return v
        raise RuntimeError("arena OOM for %s (%d B); live=%s" % (name, nb, {k: v[1] for k, v in self.live.items()}))

    def ralloc(self, name, shape, dt, n=2):
        if name not in self.rings:
            self.rings[name] = [[self.alloc("%s#%d" % (name, i), shape, dt) for i in range(n)], 0]
        r = self.rings[name]
        r[1] += 1
        return r[0][r[1] % len(r[0])]

    def rfree(self, name):
        for i in range(len(self.rings[name][0])):
            self.release("%s#%d" % (name, i))
        del self.rings[name]

    def release(self, name):
        if name in self.rings:
            return
        lo, nb = self.live.pop(name)
        self.free.append((lo, lo + nb))
        self.free.sort()
        merged = []
        for l, h in self.free:
            if merged and merged[-1][1] == l:
                merged[-1] = (merged[-1][0], h)
            else:
                merged.append((l, h))
        self.free = merged


class PsumPool:
    def __init__(self, nc):
        self.t = nc.alloc_psum_tensor("psum", [128, 4096], F32)
        self.order = list(range(8))
        self.held = set()

    def get(self, hold=False):
        for b in self.order:
            if b not in self.held:
                self.order.remove(b)
                self.order.append(b)
                if hold:
                    self.held.add(b)
                return b
        raise RuntimeError("no free PSUM bank")

    def release(self, b):
        self.held.discard(b)

    def f32(self, b):
        return self.t[:, b * 512:(b + 1) * 512]

    def bf16(self, b):
        return self.t[:, b * 512:(b + 1) * 512].bitcast(BF16)


class _Stop(Exception):
    pass


def build_program(debug=False, stop_after=None):
    nc = bass.Bass("TRN2", target_bir_lowering=False)
    T = Tracker(nc)
    dbg = {}
    try:
        _build_body(nc, T, dbg, debug, stop_after)
    except _Stop:
        pass
    for skey in list(T.dma_sems.keys()):
        if skey.startswith("ost") or skey == "dbg":
            T.final_wait("sp", skey)
    info = {"ninst": T.ninst, "nwaits": T.nwaits, "nsem": T.nsem,
            "counts": {k: v.count for k, v in T.engs.items()}}
    return nc, info


def _build_body(nc, T, dbg, debug, stop_after):
    def stop(tag):
        if stop_after == tag:
            raise _Stop()

    def din(name, shape, dt=F32):
        return nc.dram_tensor(name, list(shape), dt, kind="ExternalInput").ap()

    x_d = din("x", [S, D])
    w_in_d = din("w_in", [D, D_IN])
    convw_d = din("conv_w_t", [128, 12])
    ikg_d = din("ikg", [128, 64])
    ikb_d = din("ikb", [128, 64])
    w_br_d = din("w_branch", [2, 512, D])
    w_o_d = din("w_o", [D, D])
    ln1g_d = din("ln1g", [128, D])
    ln1b_d = din("ln1b", [128, D])
    ln1gc_d = din("ln1gc", [128, 8])
    ln1bc_d = din("ln1bc", [128, 8])
    w_up_d = din("w_up", [D, DFF])
    w_down_d = din("w_down", [DFF, D])
    ln2g_d = din("ln2g", [128, D])
    ln2b_d = din("ln2b", [128, D])
    ident_d = din("ident", [128, 128], BF16)
    tri_d = din("tri", [128, 128], BF16)
    negtri_d = din("negtri", [128, 128])
    pow2_d = din("pow2", [128, 2 * NIT])
    out_d = nc.dram_tensor("out", [S, D], F32, kind="ExternalOutput").ap()
    h_scr = nc.dram_tensor("h_scr", [S, D], F32, kind="ExternalOutput" if debug else "Internal").ap()

    def dbg_out(name, shape, dt=F32):
        if not debug:
            return None
        dbg[name] = nc.dram_tensor("dbg_" + name, list(shape), dt, kind="ExternalOutput").ap()
        return dbg[name]

    AR = Arena(nc, int(nc.sbuf_bytes_remaining) - 1024)
    PS = PsumPool(nc)

    w_in_v = w_in_d.rearrange("(c p) n -> p c n", p=128)

    def dump(name, ap_sb, shape, dt=F32):
        d = dbg_out(name, shape, dt)
        if d is None:
            return
        T.dma("sp", d, ap_sb, "dbg", reads=[ap_sb], writes=["dbg_" + name])

    def rstd_from_var(mv):
        T.op("dve", lambda e: e.tensor_scalar(out=mv[:, 3:4], in0=mv[:, 1:2], scalar1=LN_EPS, scalar2=None,
                                              op0=ALU.add), reads=[mv[:, 1:2]], writes=[mv[:, 3:4]])
        T.op("act", lambda e: e.activation(out=mv[:, 3:4], in_=mv[:, 3:4], func=AF.Sqrt), reads=[mv[:, 3:4]],
             writes=[mv[:, 3:4]])
        T.op("dve", lambda e: e.reciprocal(out=mv[:, 2:3], in_=mv[:, 3:4]), reads=[mv[:, 3:4]], writes=[mv[:, 2:3]])

    ident = AR.alloc("ident", [128], BF16)
    tri = AR.alloc("tri", [128], BF16)
    negtri = AR.alloc("negtri", [128], F32)
    pow2 = AR.alloc("pow2", [2 * NIT], F32)
    convw = AR.alloc("convw", [12], F32)
    ikg = AR.alloc("ikg", [64], F32)
    ikb = AR.alloc("ikb", [64], F32)
    for dst, src in ((ident, ident_d), (tri, tri_d), (negtri, negtri_d), (pow2, pow2_d),
                     (convw, convw_d), (ikg, ikg_d), (ikb, ikb_d)):
        T.dma("sp", dst, src, "const", writes=[dst], chain=True)

    def load_w_cols(name, col0, ncols, dup64=False):
        w = AR.alloc(name, [NCH, ncols], BF16)
        T.dma("pool", w, w_in_v[:, :, col0:col0 + ncols], "w_" + name, writes=[w])
        return w

    xT = AR.alloc("xT", [NCH, S], BF16)
    xts = [AR.alloc("xt%d" % i, [D], F32) for i in range(2)]
    xbs = [AR.alloc("xb%d" % i, [D], BF16) for i in range(2)]
    w_kw = load_w_cols("w_kw", C_KI, 72)
    w_qi = load_w_cols("w_qi", C_QI, 512)
    for tt in range(NT):
        xt = xts[tt % 2]
        xb = xbs[tt % 2]
        T.dma("sp", xt, x_d[tt * 128:(tt + 1) * 128, :], "xld%d" % (tt % 2), writes=[xt])
        T.op("act", lambda e: e.copy(out=xb, in_=xt), reads=[xt], writes=[xb])
        b = PS.get()
        pb = PS.bf16(b)
        for c in range(NCH):
            o = pb[:, c * 128:(c + 1) * 128]
            i_ = xb[:, c * 128:(c + 1) * 128]
            T.op("pe", lambda e: e.transpose(out=o, in_=i_, identity=ident), reads=[i_, ident], writes=[o],
                 inc=(c == NCH - 1))
        dst = xT[:, :, tt * 128:(tt + 1) * 128]
        src = pb.rearrange("p (c t) -> p c t", c=NCH)
        T.op("dve", lambda e: e.tensor_copy(out=dst, in_=src), reads=[pb], writes=[dst])
    AR.release("xt0"); AR.release("xt1"); AR.release("xb0"); AR.release("xb1")
    if debug:
        dump("xT", xT, [128, NCH, S], BF16)
    stop("A")

    def proj_feat(w, wc0, dst_fn, evac):
        for m in range(4):
            b = PS.get()
            p = PS.f32(b)
            for c in range(NCH):
                l = w[:, c, wc0:wc0 + 128]
                r = xT[:, c, m * 512:(m + 1) * 512]
                T.op("pe", lambda e: e.matmul(out=p, lhsT=l, rhs=r, start=(c == 0), stop=(c == NCH - 1)),
                     reads=[l, r], writes=[p], inc=(c == NCH - 1))
            evac(m, p)

    cp_toggle = [0]

    def evac_copy(dst, src):
        cp_toggle[0] ^= 1
        if cp_toggle[0]:
            T.op("act", lambda e: e.copy(out=dst, in_=src), reads=[src], writes=[dst])
        else:
            T.op("dve", lambda e: e.tensor_copy(out=dst, in_=src), reads=[src], writes=[dst])

    w_q = load_w_cols("w_q", C_Q, 512)
    w_k = [AR.alloc("w_k%d" % g, [NCH, 128], BF16) for g in range(2)]
    for g in range(2):
        for half in range(2):
            dstw = w_k[g][:, :, half * 64:(half + 1) * 64]
            T.dma("pool", dstw, w_in_v[:, :, C_K + g * 64:C_K + (g + 1) * 64], "w_k%d%d" % (g, half), writes=[dstw])
    w_v = load_w_cols("w_v", C_V, 128)
    kiT2 = AR.alloc("kiT2", [S], BF16)
    wi = AR.alloc("wi", [NT, 8], F32)
    qiT = AR.alloc("qiT", [4, S], BF16)
    qT = AR.alloc("qT", [4, S], BF16)
    kT2 = AR.alloc("kT2", [2, S], BF16)

    def ki_gen():
        for tt in range(NT):
            b = PS.get()
            p = PS.f32(b)[:, 0:72]
            for c in range(NCH):
                l = xT[:, c, tt * 128:(tt + 1) * 128]
                r = w_kw[:, c, :]
                T.op("pe", lambda e: e.matmul(out=p, lhsT=l, rhs=r, start=(c == 0), stop=(c == NCH - 1)),
                     reads=[l, r], writes=[p], inc=(c == NCH - 1))
            st = AR.ralloc("kst", [8], F32, 3)
            mv = AR.ralloc("kmv", [4], F32, 3)
            kn = AR.ralloc("kn", [64], F32, 3)
            kn2 = AR.ralloc("kn2", [128], BF16, 3)
            pk = p[:, 0:64]
            T.op("dve", lambda e: e.bn_stats(out=st[:, 0:6], in_=pk), reads=[pk], writes=[st])
            T.op("dve", lambda e: e.bn_aggr(out=mv[:, 0:2], in_=st[:, 0:6]), reads=[st], writes=[mv[:, 0:2]])
            rstd_from_var(mv)
            T.op("dve", lambda e: e.tensor_scalar(out=kn, in0=pk, scalar1=mv[:, 0:1], scalar2=mv[:, 2:3],
                                                  op0=ALU.subtract, op1=ALU.mult), reads=[pk, mv], writes=[kn])
            wsrc = p[:, 64:72]
            wdst = wi[:, tt, :]
            T.op("dve", lambda e: e.tensor_copy(out=wdst, in_=wsrc), reads=[wsrc], writes=[wdst])
            T.op("dve", lambda e: e.tensor_tensor(out=kn, in0=kn, in1=ikg, op=ALU.mult), reads=[kn, ikg], writes=[kn])
            T.op("dve", lambda e: e.tensor_tensor(out=kn2[:, 0:64], in0=kn, in1=ikb, op=ALU.add),
                 reads=[kn, ikb], writes=[kn2[:, 0:64]])
            T.op("dve", lambda e: e.tensor_copy(out=kn2[:, 64:128], in_=kn2[:, 0:64]), reads=[kn2[:, 0:64]],
                 writes=[kn2[:, 64:128]])
            yield
            b2 = PS.get()
            pb = PS.bf16(b2)[:, 0:128]
            T.op("pe", lambda e: e.transpose(out=pb, in_=kn2, identity=ident), reads=[kn2, ident], writes=[pb])
            kd = kiT2[:, tt * 128:(tt + 1) * 128]
            T.op("act", lambda e: e.copy(out=kd, in_=pb), reads=[pb], writes=[kd])
            yield

    def projB_gen():
        for (w, wc0, dst) in ([(w_qi, j * 128, qiT[:, j, :]) for j in range(4)]
                              + [(w_q, j * 128, qT[:, j, :]) for j in range(4)]
                              + [(w_k[g], 0, kT2[:, g, :]) for g in range(2)]):
            for m in range(4):
                b = PS.get()
                p = PS.f32(b)
                for c in range(NCH):
                    l = w[:, c, wc0:wc0 + 128]
                    r = xT[:, c, m * 512:(m + 1) * 512]
                    T.op("pe", lambda e: e.matmul(out=p, lhsT=l, rhs=r, start=(c == 0), stop=(c == NCH - 1)),
                         reads=[l, r], writes=[p], inc=(c == NCH - 1))
                evac_copy(dst[:, m * 512:(m + 1) * 512], p)
                yield

    def run2(ga, gb, ra, rb):
        alive = [True, True]
        gens = [ga, gb]
        while any(alive):
            for q, reps in ((0, ra), (1, rb)):
                if not alive[q]:
                    continue
                for _ in range(reps):
                    try:
                        next(gens[q])
                    except StopIteration:
                        alive[q] = False
                        break
    run2(ki_gen(), projB_gen(), 1, 1)
    AR.release("w_kw"); AR.release("w_qi"); AR.release("w_q"); AR.release("w_k0"); AR.release("w_k1")
    for n_ in ("kst", "kmv", "kn", "kn2"):
        AR.rfree(n_)
    stop("B3")
    Vaug = [[AR.alloc("Vaug%d%d" % (g, e), [NT, 128], BF16) for e in range(2)] for g in range(2)]
    for g in range(2):
        for e_ in range(2):
            va = Vaug[g][e_]
            T.op("dve", lambda e: e.memset(va, 1.0), writes=[va])
    for tt in range(NT):
        b = PS.get()
        p = PS.f32(b)[:, 0:128]
        for c in range(NCH):
            l = xT[:, c, tt * 128:(tt + 1) * 128]
            r = w_v[:, c, :]
            T.op("pe", lambda e: e.matmul(out=p, lhsT=l, rhs=r, start=(c == 0), stop=(c == NCH - 1)),
                 reads=[l, r], writes=[p], inc=(c == NCH - 1))
        for g in range(2):
            src = p[:, g * 64:(g + 1) * 64]
            d0 = Vaug[g][0][:, tt, 0:64]
            d1 = Vaug[g][1][:, tt, 64:128]
            T.op("act", lambda e: e.copy(out=d0, in_=src), reads=[src], writes=[d0])
            T.op("dve", lambda e: e.tensor_copy(out=d1, in_=src), reads=[src], writes=[d1])
    AR.release("w_v")
    stop("B4")
    if debug:
        dump("qiT", qiT, [128, 4, S], BF16)
        dump("kiT2", kiT2, [128, S], BF16)
        dump("wi", wi, [128, NT, 8])
        dump("qT", qT, [128, 4, S], BF16)
        dump("kT2", kT2, [128, 2, S], BF16)
        dump("Vaug00", Vaug[0][0], [128, NT, 128], BF16)

    stop("B")
    xT_scr = nc.dram_tensor("xT_scr", [128, NCH, S], BF16, kind="Internal").ap()
    T.dma("sp", xT_scr, xT, "xTsp", reads=[xT], writes=["xT_scr"])
    AR.release("xT")

    y_bT = AR.alloc("y_bT", [4, S], BF16)
    maskTs = [AR.alloc("maskT%d" % i, [NT, 512], BF16) for i in range(2)]
    junks = [AR.alloc("junk%d" % i, [S], BF16) for i in range(2)]
    junkd = [AR.alloc("junkd%d" % i, [S], BF16) for i in range(1)]
    negtri_b = AR.alloc("negtri_b", [128], BF16)
    T.op("dve", lambda e: e.tensor_scalar(out=negtri_b, in0=negtri, scalar1=MASKNEG / NEG, scalar2=None, op0=ALU.mult),
         reads=[negtri], writes=[negtri_b])
    scores = [AR.alloc("score%d" % i, [S], F32) for i in range(4)]
    maskbs = [AR.alloc("maskb%d" % i, [S], BF16) for i in range(2)]
    rtmps = [AR.alloc("rtmp%d" % i, [512], BF16) for i in range(10)]
    dws = [AR.alloc("dw%d" % i, [8, 128], BF16) for i in range(4)]
    ptiles = [AR.alloc("ptile%d" % i, [512], BF16) for i in range(10)]
    rcs = [AR.alloc("rc%d" % i, [512], F32) for i in range(2)]
    ctr = {"rtmp": 0, "ptile": 0, "rc": 0, "maskb": 0, "junk": 0, "junkd": 0}

    def ring(lst, key):
        ctr[key] += 1
        return lst[ctr[key] % len(lst)]

    if debug:
        dbg_mask = dbg_out("mask", [NT, 128, S], BF16)
        dbg_score = dbg_out("score", [NT, 128, S], F32)

    def topk_chunk(m):
        maskT = maskTs[m % 2]
        tiles = list(range(4 * m, 4 * m + 4))
        sel = [i for i in tiles if i >= 2]
        nb = len(sel)
        col = {i: c for c, i in enumerate(sel)}
        for i in sel:
            N = 128 * (i + 1)
            c_ = col[i]
            score = scores[c_]
            for h in range(8):
                dwt = dws[c_][:, h, :]
                wsc = wi[:, i, h:h + 1]
                T.op("dve", lambda e: e.tensor_scalar(out=dwt, in0=ident, scalar1=wsc, scalar2=None, op0=ALU.mult),
                     reads=[ident, wsc], writes=[dwt])
            for kc in range((N + 511) // 512):
                k0 = kc * 512
                n = min(512, N - k0)
                sc = score[:, k0:k0 + n]
                sb_ = PS.get(hold=True)
                sacc = PS.f32(sb_)[:, 0:n]
                pendq = []

                def emit_acc(it):
                    tv_, h_ = it
                    dw_ = dws[c_][:, h_, :]
                    T.op("pe", lambda e: e.matmul(out=sacc, lhsT=dw_, rhs=tv_, start=(h_ == 0), stop=(h_ == 7)),
                         reads=[dw_, tv_], writes=[sacc], inc=(h_ == 7))
                for hp in range(4):
                    pair = []
                    for h in (2 * hp, 2 * hp + 1):
                        e2 = h % 2
                        b = PS.get()
                        p = PS.f32(b)[:, 0:n]
                        l = qiT[64 * e2:64 * e2 + 64, h // 2, i * 128:(i + 1) * 128]
                        r = kiT2[64 * e2:64 * e2 + 64, k0:k0 + n]
                        T.op("pe", lambda e: e.matmul(out=p, lhsT=l, rhs=r, start=True, stop=True),
                             reads=[l, r], writes=[p])
                        pair.append((h, p))
                    for h, p in pair:
                        tv = ring(rtmps, "rtmp")[:, 0:n]
                        if h in DVE_RELU_HEADS:
                            T.op("dve", lambda e: e.tensor_scalar(out=tv, in0=p, scalar1=0.0, scalar2=None, op0=ALU.max),
                                 reads=[p], writes=[tv])
                        else:
                            T.op("act", lambda e: e.activation(out=tv, in_=p, func=AF.Relu), reads=[p], writes=[tv])
                        pendq.append((tv, h))
                    while len(pendq) > 8:
                        emit_acc(pendq.pop(0))
                while pendq:
                    emit_acc(pendq.pop(0))
                if k0 + n == N:
                    nd = n - 128
                    if nd > 0:
                        T.op("dve", lambda e: e.tensor_copy(out=sc[:, 0:nd], in_=sacc[:, 0:nd]), reads=[sacc],
                             writes=[sc[:, 0:nd]])
                    T.op("dve", lambda e: e.tensor_tensor(out=sc[:, nd:n], in0=sacc[:, nd:n], in1=negtri, op=ALU.add),
                         reads=[sacc, negtri], writes=[sc[:, nd:n]])
                else:
                    T.op("dve", lambda e: e.tensor_copy(out=sc, in_=sacc), reads=[sacc], writes=[sc])
                PS.release(sb_)
                yield 'S'
        taus = {}
        if nb:
            half = (nb + 1) // 2
            groups = [g_ for g_ in (sel[:half], sel[half:]) if g_]
            sts = []
            for gi, grp in enumerate(groups):
                ng = len(grp)
                sm = AR.alloc("tk_small%d" % gi, [64 + 3 * NIT * 2], F32)
                st_ = {"hi": sm[:, 0:ng], "lo": sm[:, 2:2 + ng], "R": sm[:, 4:4 + ng], "nmid": sm[:, 6:6 + ng],
                       "tq": sm[:, 8:8 + ng], "tau": sm[:, 10:10 + ng], "npl": sm[:, 12:12 + ng],
                       "thr": sm[:, 14:14 + ng],
                       "Rk": sm[:, 64:64 + 2 * NIT].rearrange("p (k c) -> p k c", k=NIT),
                       "Rk2": sm[:, 64 + 2 * NIT:64 + 4 * NIT].rearrange("p (k c) -> p k c", k=NIT),
                       "cnt": sm[:, 64 + 4 * NIT:64 + 6 * NIT].rearrange("p (k c) -> p k c", k=NIT),
                       "mid": sm[:, 16:16 + ng], "grp": grp, "ng": ng, "gi": gi}
                sts.append(st_)
                for lc_, i in enumerate(grp):
                    c = col[i]
                    N = 128 * (i + 1)
                    sN = scores[c][:, 0:N]
                    sL = scores[c][:, 0:128 * i]
                    hc = st_["hi"][:, lc_:lc_ + 1]; lc = st_["lo"][:, lc_:lc_ + 1]; tc0 = st_["thr"][:, lc_:lc_ + 1]
                    T.op("dve", lambda e: e.memset(tc0, float(2 * TOPK - 2 - N) if gi == 0 else float(TOPK - 1)),
                         writes=[tc0])
                    T.op("dve", lambda e: e.tensor_reduce(out=hc, in_=sN, axis=AX.X, op=ALU.max), reads=[sN], writes=[hc])
                    T.op("dve", lambda e: e.tensor_reduce(out=lc, in_=sL, axis=AX.X, op=ALU.min), reads=[sL], writes=[lc])
                    yield 'S'
                hi = st_["hi"]; lo = st_["lo"]; R = st_["R"]; nmid = st_["nmid"]
                T.op("dve", lambda e: e.tensor_tensor(out=R, in0=hi, in1=lo, op=ALU.subtract), reads=[hi, lo], writes=[R])
                T.op("dve", lambda e: e.scalar_tensor_tensor(out=nmid, in0=R, scalar=-0.5, in1=lo, op0=ALU.mult,
                                                             op1=ALU.subtract), reads=[R, lo], writes=[nmid])
                mid_ = st_["mid"]
                T.op("dve", lambda e: e.tensor_scalar(out=mid_, in0=nmid, scalar1=-1.0, scalar2=None, op0=ALU.mult),
                     reads=[nmid], writes=[mid_])
                for lc_ in range(ng):
                    rc_ = R[:, lc_:lc_ + 1]
                    o1 = st_["Rk"][:, :, lc_]
                    o2 = st_["Rk2"][:, :, lc_]
                    T.op("dve", lambda e: e.tensor_scalar(out=o1, in0=pow2[:, 0:NIT], scalar1=rc_, scalar2=None,
                                                          op0=ALU.mult), reads=[pow2, rc_], writes=[o1])
                    T.op("dve", lambda e: e.tensor_scalar(out=o2, in0=pow2[:, NIT:2 * NIT], scalar1=rc_, scalar2=None,
                                                          op0=ALU.mult), reads=[pow2, rc_], writes=[o2])
            for k in range(NIT):
                for st_ in sts:
                    for lc_, i in enumerate(st_["grp"]):
                        c = col[i]
                        N = 128 * (i + 1)
                        sN = scores[c][:, 0:N]
                        jn = ring(junks, "junk")[:, 0:N]
                        ck = st_["cnt"][:, k, lc_:lc_ + 1]
                        if st_["gi"] == 0:
                            mc = st_["nmid"][:, lc_:lc_ + 1]
                            T.op("act", lambda e: e.activation(out=jn, in_=sN, func=AF.Sign, bias=mc, scale=1.0,
                                                               accum_out=ck), reads=[sN, mc], writes=[jn, ck])
                        else:
                            mc = st_["mid"][:, lc_:lc_ + 1]
                            jn = ring(junkd, "junkd")[:, 0:N]
                            T.op("dve", lambda e: e.tensor_scalar(out=jn, in0=sN, scalar1=mc, scalar2=0.0,
                                                                  op0=ALU.is_ge, op1=ALU.add, accum_out=ck),
                                 reads=[sN, mc], writes=[jn, ck])
                    yield 'B'
                for st_ in sts:
                    ng = st_["ng"]
                    r1 = st_["Rk"][:, k, 0:ng]
                    r2 = st_["Rk2"][:, k, 0:ng]
                    ckk = st_["cnt"][:, k, 0:ng]
                    nmid = st_["nmid"]; npl = st_["npl"]; tq = st_["tq"]; thr = st_["thr"]
                    if st_["gi"] == 1:
                        mid_ = st_["mid"]
                        T.op("dve", lambda e: e.tensor_tensor(out=npl, in0=mid_, in1=r1, op=ALU.subtract),
                             reads=[mid_, r1], writes=[npl])
                        T.op("dve", lambda e: e.scalar_tensor_tensor(out=tq, in0=ckk, scalar=TOPK - 0.5, in1=r2,
                                                                     op0=ALU.is_ge, op1=ALU.mult),
                             reads=[ckk, r2], writes=[tq])
                        T.op("dve", lambda e: e.tensor_tensor(out=mid_, in0=npl, in1=tq, op=ALU.add),
                             reads=[npl, tq], writes=[mid_])
                        continue
                    T.op("pool", lambda e: e.tensor_tensor(out=npl, in0=nmid, in1=r1, op=ALU.add), reads=[nmid, r1], writes=[npl])
                    T.op("pool", lambda e: e.tensor_tensor(out=tq, in0=ckk, in1=thr, op=ALU.subtract), reads=[ckk, thr], writes=[tq])
                    T.op("pool", lambda e: e.tensor_scalar(out=tq, in0=tq, scalar1=1.0, scalar2=0.0, op0=ALU.min,
                                                           op1=ALU.max), reads=[tq], writes=[tq])
                    T.op("pool", lambda e: e.tensor_tensor(out=tq, in0=tq, in1=r2, op=ALU.mult), reads=[tq, r2], writes=[tq])
                    T.op("pool", lambda e: e.tensor_tensor(out=nmid, in0=npl, in1=tq, op=ALU.subtract),
                         reads=[npl, tq], writes=[nmid])
            for st_ in sts:
                ng = st_["ng"]
                rl = st_["Rk"][:, NIT - 1, 0:ng]
                tau = st_["tau"]; nmid = st_["nmid"]
                if st_["gi"] == 1:
                    mid_ = st_["mid"]
                    T.op("dve", lambda e: e.tensor_tensor(out=tau, in0=mid_, in1=rl, op=ALU.subtract), reads=[mid_, rl],
                         writes=[tau])
                else:
                    T.op("dve", lambda e: e.tensor_tensor(out=tau, in0=nmid, in1=rl, op=ALU.add), reads=[nmid, rl],
                         writes=[tau])
                    T.op("dve", lambda e: e.tensor_scalar(out=tau, in0=tau, scalar1=-1.0, scalar2=None, op0=ALU.mult),
                         reads=[tau], writes=[tau])
                for lc_, i in enumerate(st_["grp"]):
                    taus[i] = tau[:, lc_:lc_ + 1]
        for i in tiles:
            N = 128 * (i + 1)
            maskb = ring(maskbs, "maskb")
            if i < 2:
                if i == 1:
                    T.op("dve", lambda e: e.memset(maskb[:, 0:128], 1.0), writes=[maskb[:, 0:128]])
                dd = maskb[:, 128 * i:128 * (i + 1)]
                T.op("dve", lambda e: e.tensor_copy(out=dd, in_=tri), reads=[tri], writes=[dd])
            else:
                c = col[i]
                sN = scores[c][:, 0:N]
                mN = maskb[:, 0:N]
                tc_ = taus[i]
                T.op("dve", lambda e: e.tensor_scalar(out=mN, in0=sN, scalar1=tc_, scalar2=None, op0=ALU.is_ge),
                     reads=[sN, tc_], writes=[mN])
                if debug:
                    T.dma("sp", dbg_score[i, :, 0:N], sN, "dbg", reads=[sN], writes=["dbg_score"])
            if debug:
                T.dma("sp", dbg_mask[i, :, 0:N], maskb[:, 0:N], "dbg", reads=[maskb[:, 0:N]], writes=["dbg_mask"])
            off = (i - 4 * m) * 128
            for j0 in range(0, i + 1, 8):
                nbk = min(8, i + 1 - j0)
                b = PS.get()
                pb = PS.bf16(b)
                for jj in range(nbk):
                    j = j0 + jj
                    o = pb[:, jj * 128:(jj + 1) * 128]
                    i_ = maskb[:, j * 128:(j + 1) * 128]
                    T.op("pe", lambda e: e.transpose(out=o, in_=i_, identity=ident), reads=[i_, ident], writes=[o],
                         inc=(jj == nbk - 1))
                dst = maskT[:, j0:j0 + nbk, off:off + 128]
                src = pb[:, 0:nbk * 128].rearrange("p (j t) -> p j t", j=nbk)
                T.op("act", lambda e: e.copy(out=dst, in_=src), reads=[pb[:, 0:nbk * 128]], writes=[dst])
            yield 'B'
        if nb:
            for gi in range(len(groups)):
                AR.release("tk_small%d" % gi)

    def attn_chunk(m):
        maskT = maskTs[m % 2]
        jmax = 4 * m + 3
        for cq in range(4):
            g = cq // 2
            abs_ = [PS.get(hold=True) for _ in range(2)]
            accs = [PS.f32(ab) for ab in abs_]
            pend = []

            def emit_pv(it):
                va_, pv_, ao_, j_ = it
                T.op("pe", lambda e: e.matmul(out=ao_, lhsT=va_, rhs=pv_, start=(j_ == 0), stop=(j_ == jmax)),
                     reads=[va_, pv_], writes=[ao_], inc=(j_ == jmax))
            for j in range(jmax + 1):
                t0 = max(512 * m, 128 * j)
                n = 512 * (m + 1) - t0
                off = t0 - 512 * m
                ps_ = []
                for e2 in range(2):
                    b = PS.get()
                    p = PS.f32(b)[:, 0:n]
                    l = kT2[64 * e2:64 * e2 + 64, g, j * 128:(j + 1) * 128]
                    r = qT[64 * e2:64 * e2 + 64, cq, t0:t0 + n]
                    T.op("pe", lambda e: e.matmul(out=p, lhsT=l, rhs=r, start=True, stop=True), reads=[l, r], writes=[p])
                    ps_.append(p)
                mt = maskT[:, j, off:off + n]
                for e2 in range(2):
                    p = ps_[e2]
                    pv = ring(ptiles, "ptile")[:, 0:n]
                    T.op("act", lambda e: e.activation(out=pv, in_=p, func=AF.Exp, scale=0.125), reads=[p], writes=[pv])
                    T.op("dve", lambda e: e.tensor_tensor(out=pv, in0=pv, in1=mt, op=ALU.mult), reads=[pv, mt],
                         writes=[pv])
                    va = Vaug[g][e2][:, j, :]
                    ao = accs[e2][:, off:off + n]
                    pend.append((va, pv, ao, j))
                while len(pend) > 4:
                    emit_pv(pend.pop(0))
                if j % 2 == 1:
                    yield
            while pend:
                emit_pv(pend.pop(0))
            for e2 in range(2):
                acc = accs[e2]
                rc = ring(rcs, "rc")
                po = 64 * e2
                pd = 64 * (1 - e2)
                rcv = rc[po:po + 64, :]
                den = acc[pd:pd + 64, :]
                T.op("dve", lambda e: e.reciprocal(out=rcv, in_=den), reads=[den], writes=[rcv])
                yo = y_bT[po:po + 64, cq, 512 * m:512 * (m + 1)]
                num = acc[po:po + 64, :]
                T.op("dve", lambda e: e.tensor_tensor(out=yo, in0=num, in1=rcv, op=ALU.mult), reads=[num, rcv],
                     writes=[yo])
                PS.release(abs_[e2])
            yield

    def run_interleaved(ga, gb, ra=1, rb=1, warm=0):
        alive = [ga is not None, gb is not None]
        gens = [ga, gb]
        reps = [ra, rb]
        for _ in range(warm):
            try:
                next(ga)
            except StopIteration:
                alive[0] = False
                break
        while any(alive):
            for q in range(2):
                if not alive[q]:
                    continue
                for _ in range(reps[q]):
                    try:
                        next(gens[q])
                    except StopIteration:
                        alive[q] = False
                        break

    def run_weighted(ga, gb, wS, wB):
        credit = 0.0
        alive = True
        for tag in gb:
            credit += wS if tag == 'S' else wB
            while alive and credit >= 1.0:
                try:
                    next(ga)
                except StopIteration:
                    alive = False
                credit -= 1.0
        if alive:
            for _ in ga:
                pass

    run_interleaved(topk_chunk(0), None)
    for m in range(3):
        nA = 4 * (2 * m + 3)
        nS = 4 * (m + 2) + 4
        nB = 2 * NIT + 4
        wS = 0.05
        run_weighted(attn_chunk(m), topk_chunk(m + 1), wS, max(0.3, (nA - wS * nS) / nB))
    for n_ in (["junk0", "junk1", "junkd0", "negtri_b", "qiT", "kiT2", "wi"] + ["score%d" % i for i in range(4)]
               + ["maskb%d" % i for i in range(2)] + ["rtmp%d" % i for i in range(10)] + ["dw%d" % i for i in range(4)]):
        AR.release(n_)
    xT = AR.alloc("xT", [NCH, S], BF16)
    T.dma("sp", xT, xT_scr, "xTsp", reads=["xT_scr"], writes=[xT])
    w_c = load_w_cols("w_c", C_C, 512)
    w_u = load_w_cols("w_u", C_U, 512)
    w_b = load_w_cols("w_b", C_B, 512)
    def proj_chunk(w, wc0, m):
        b = PS.get()
        p = PS.f32(b)
        for c in range(NCH):
            l = w[:, c, wc0:wc0 + 128]
            r = xT[:, c, m * 512:(m + 1) * 512]
            T.op("pe", lambda e: e.matmul(out=p, lhsT=l, rhs=r, start=(c == 0), stop=(c == NCH - 1)),
                 reads=[l, r], writes=[p], inc=(c == NCH - 1))
        return p

    def conv_phase():
        for j in range(4):
            c_sb = AR.ralloc("c_sb", [S], F32, 1)
            vpad = AR.ralloc("vpad", [S + 64], F32, 2)
            cacc = AR.ralloc("cacc", [S], F32, 1)
            T.op("pool", lambda e: e.memset(vpad[:, 0:2], 0.0), writes=[vpad[:, 0:2]])
            for m in range(4):
                p = proj_chunk(w_c, j * 128, m)
                evac_copy(c_sb[:, m * 512:(m + 1) * 512], p)
                yield
            for m in range(4):
                p = proj_chunk(w_u, j * 128, m)
                d_ = vpad[:, 2 + m * 512:2 + (m + 1) * 512]
                s_ = c_sb[:, m * 512:(m + 1) * 512]
                T.op("dve", lambda e: e.tensor_tensor(out=d_, in0=p, in1=s_, op=ALU.mult), reads=[p, s_], writes=[d_])
                yield
            w0 = convw[:, j * 3 + 0:j * 3 + 1]
            w1 = convw[:, j * 3 + 1:j * 3 + 2]
            w2 = convw[:, j * 3 + 2:j * 3 + 3]
            v2 = vpad[:, 2:S + 2]; v1 = vpad[:, 1:S + 1]; v0 = vpad[:, 0:S]
            T.op("dve", lambda e: e.tensor_scalar(out=cacc, in0=v2, scalar1=w2, scalar2=None, op0=ALU.mult),
                 reads=[v2, w2], writes=[cacc])
            yield
            T.op("dve", lambda e: e.scalar_tensor_tensor(out=cacc, in0=v1, scalar=w1, in1=cacc, op0=ALU.mult,
                                                         op1=ALU.add), reads=[v1, w1, cacc], writes=[cacc])
            yield
            T.op("dve", lambda e: e.scalar_tensor_tensor(out=cacc, in0=v0, scalar=w0, in1=cacc, op0=ALU.mult,
                                                         op1=ALU.add), reads=[v0, w0, cacc], writes=[cacc])
            yield
            for m in range(4):
                p = proj_chunk(w_b, j * 128, m)
                d_ = y_aT[:, j, m * 512:(m + 1) * 512]
                s_ = cacc[:, m * 512:(m + 1) * 512]
                T.op("dve", lambda e: e.tensor_tensor(out=d_, in0=p, in1=s_, op=ALU.mult), reads=[p, s_], writes=[d_])
                yield

    AR.release("maskT0")
    y_aT = AR.alloc("y_aT", [4, S], BF16)
    run_interleaved(attn_chunk(3), conv_phase(), 1, 1, warm=10)
    for n_ in (["maskT1", "qT", "kT2", "Vaug00", "Vaug01", "Vaug10", "Vaug11"]
               + ["ptile%d" % i for i in range(10)] + ["rc%d" % i for i in range(2)]):
        AR.release(n_)
    AR.release("w_c"); AR.release("w_u"); AR.release("w_b")
    for n_ in ("c_sb", "vpad", "cacc"):
        AR.rfree(n_)
    if debug:
        dump("y_bT", y_bT, [128, 4, S], BF16)

    stop("CD")
    w_gas = [None, None]
    w_gbs = [None, None]
    w_gas[0] = load_w_cols("w_ga0", C_GA, 512)
    w_gbs[0] = load_w_cols("w_gb0", C_GB, 512)
    w_pa = AR.alloc("w_pa", [4, D], BF16)
    w_pb = AR.alloc("w_pb", [4, D], BF16)
    T.dma("pool", w_pa, w_br_d[0].rearrange("(k p) n -> p k n", p=128), "w_pa", writes=[w_pa])
    T.dma("pool", w_pb, w_br_d[1].rearrange("(k p) n -> p k n", p=128), "w_pb", writes=[w_pb])
    if debug:
        dump("y_aT", y_aT, [128, 4, S], BF16)

    stop("E")
    mT = AR.alloc("mT", [NCH, S], BF16)
    w_gas[1] = load_w_cols("w_ga1", C_GA + 512, 512)
    w_gbs[1] = load_w_cols("w_gb1", C_GB + 512, 512)
    w_o = AR.alloc("w_o", [NCH, D], BF16)
    w_o_v = w_o_d.rearrange("(c p) n -> p c n", p=128)
    for q in range(2):
        T.dma("pool", w_o[:, 4 * q:4 * q + 4, :], w_o_v[:, 4 * q:4 * q + 4, :], "w_o%d" % q, writes=[w_o[:, 4 * q:4 * q + 4, :]])
    for half in range(2):
        w_ga = w_gas[half]
        w_gb = w_gbs[half]
        for dq in range(4):
            dc = half * 4 + dq
            for m in range(4):
                tsl = slice(m * 512, (m + 1) * 512)
                sa = AR.ralloc("sa", [512], F32, 3)
                sb = AR.ralloc("sb", [512], F32, 3)
                for (wg, sg) in ((w_ga, sa), (w_gb, sb)):
                    b = PS.get()
                    p = PS.f32(b)
                    for c in range(NCH):
                        l = wg[:, c, dq * 128:(dq + 1) * 128]
                        r = xT[:, c, tsl]
                        T.op("pe", lambda e: e.matmul(out=p, lhsT=l, rhs=r, start=(c == 0), stop=(c == NCH - 1)),
                             reads=[l, r], writes=[p], inc=(c == NCH - 1))
                    T.op("act", lambda e: e.activation(out=sg, in_=p, func=AF.Sigmoid), reads=[p], writes=[sg])
                t1 = AR.ralloc("t1", [512], F32, 3)
                t2 = AR.ralloc("t2", [512], F32, 3)
                for (wp, yT, sg, tt_) in ((w_pa, y_aT, sa, t1), (w_pb, y_bT, sb, t2)):
                    b = PS.get()
                    p = PS.f32(b)
                    for k in range(4):
                        l = wp[:, k, dc * 128:(dc + 1) * 128]
                        r = yT[:, k, tsl]
                        T.op("pe", lambda e: e.matmul(out=p, lhsT=l, rhs=r, start=(k == 0), stop=(k == 3)),
                             reads=[l, r], writes=[p], inc=(k == 3))
                    T.op("dve", lambda e: e.tensor_tensor(out=tt_, in0=p, in1=sg, op=ALU.mult),
                         reads=[p, sg], writes=[tt_])
                md = mT[:, dc, tsl]
                T.op("pool", lambda e: e.tensor_tensor(out=md, in0=t1, in1=t2, op=ALU.add), reads=[t1, t2], writes=[md])
                for n_ in ("sa", "sb", "t1", "t2"):
                    AR.release(n_)
        AR.release("w_ga%d" % half); AR.release("w_gb%d" % half)
    for n_ in ("xT", "y_aT", "y_bT", "w_pa", "w_pb"):
        AR.release(n_)
    for n_ in ("sa", "sb", "t1", "t2"):
        AR.rfree(n_)
    if debug:
        dump("mT", mT, [128, NCH, S], BF16)

    stop("F1")
    hT = AR.alloc("hT", [NCH, S], BF16)
    lng = AR.alloc("lng", [D], F32)
    lnb = AR.alloc("lnb", [D], F32)
    T.dma("sp", lng, ln1g_d, "lnp", writes=[lng])
    T.dma("sp", lnb, ln1b_d, "lnp", writes=[lnb], chain=True)
    w_dn_q = [AR.alloc("w_dn%d" % qq, [8, D], BF16) for qq in range(4)]
    w_dn_v = w_down_d.rearrange("(f p) n -> p f n", p=128)
    for q in range(8):
        dq_ = w_dn_q[q // 2][:, 4 * (q % 2):4 * (q % 2) + 4, :]
        T.dma("pool", dq_, w_dn_v[:, 4 * q:4 * q + 4, :], "w_dn%d" % q, writes=[dq_])

    def layernorm_tile(r, g_t, b_t, out_t, tag):
        st = AR.ralloc("lst" + tag, [16], F32, 2)
        mv = AR.ralloc("lmv" + tag, [8], F32, 2)
        for q in range(2):
            rq = r[:, q * 512:(q + 1) * 512]
            sq = st[:, q * 6:(q + 1) * 6]
            T.op("dve", lambda e: e.bn_stats(out=sq, in_=rq), reads=[rq], writes=[sq])
        s12 = st[:, 0:12]
        T.op("dve", lambda e: e.bn_aggr(out=mv[:, 0:2], in_=s12), reads=[s12], writes=[mv[:, 0:2]])
        rstd_from_var(mv)
        nmr = mv[:, 4:5]
        T.op("dve", lambda e: e.scalar_tensor_tensor(out=nmr, in0=mv[:, 0:1], scalar=-1.0, in1=mv[:, 2:3],
                                                     op0=ALU.mult, op1=ALU.mult), reads=[mv[:, 0:3]], writes=[nmr])
        T.op("act", lambda e: e.activation(out=out_t, in_=r, func=AF.Identity, bias=nmr, scale=mv[:, 2:3]),
             reads=[r, mv[:, 2:5]], writes=[out_t])
        T.op("dve", lambda e: e.tensor_tensor(out=out_t, in0=out_t, in1=g_t, op=ALU.mult), reads=[out_t, g_t],
             writes=[out_t])
        T.op("dve", lambda e: e.tensor_tensor(out=out_t, in0=out_t, in1=b_t, op=ALU.add), reads=[out_t, b_t],
             writes=[out_t])
        AR.release("lst" + tag); AR.release("lmv" + tag)

    g1c = AR.alloc("g1c", [8], F32)
    b1c = AR.alloc("b1c", [8], F32)
    T.dma("sp", g1c, ln1gc_d, "lnc", writes=[g1c])
    T.dma("sp", b1c, ln1bc_d, "lnc", writes=[b1c], chain=True)

    def emit_hT(nb_, tt_):
        b_ = PS.get()
        pb_ = PS.bf16(b_)
        for c in range(NCH):
            o = pb_[:, c * 128:(c + 1) * 128]
            i_ = nb_[:, c * 128:(c + 1) * 128]
            T.op("pe", lambda e: e.transpose(out=o, in_=i_, identity=ident), reads=[i_, ident], writes=[o],
                 inc=(c == NCH - 1))
        for c in range(NCH):
            src = pb_[:, c * 128:(c + 1) * 128]
            dst = hT[:, c, tt_ * 128:(tt_ + 1) * 128]
            gc = g1c[:, c:c + 1]
            bc = b1c[:, c:c + 1]
            T.op("act", lambda e: e.activation(out=dst, in_=src, func=AF.Identity, bias=bc, scale=gc),
                 reads=[src, gc, bc], writes=[dst])

    pend_nb = []
    xq = []

    def issue_xload(t_):
        xt_ = AR.ralloc("xr", [D], F32, 4)
        T.dma("sp", xt_, x_d[t_ * 128:(t_ + 1) * 128, :], "xr%d" % (t_ % 4), writes=[xt_])
        xq.append(xt_)
    issue_xload(0)
    issue_xload(1)
    for tt in range(NT):
        if tt + 2 < NT:
            issue_xload(tt + 2)
        xt = xq.pop(0)
        r = AR.ralloc("r1", [D], F32, 2)
        for half in range(2):
            b = PS.get()
            p = PS.f32(b)
            for dc in range(NCH):
                l = mT[:, dc, tt * 128:(tt + 1) * 128]
                rr = w_o[:, dc, half * 512:(half + 1) * 512]
                T.op("pe", lambda e: e.matmul(out=p, lhsT=l, rhs=rr, start=(dc == 0), stop=(dc == NCH - 1)),
                     reads=[l, rr], writes=[p], inc=(dc == NCH - 1))
            xh = xt[:, half * 512:(half + 1) * 512]
            rh = r[:, half * 512:(half + 1) * 512]
            T.op("dve", lambda e: e.scalar_tensor_tensor(out=rh, in0=xh, scalar=ALPHA, in1=p, op0=ALU.mult, op1=ALU.add),
                 reads=[xh, p], writes=[rh])
        st = AR.ralloc("lst1", [16], F32, 2)
        mv = AR.ralloc("lmv1", [8], F32, 2)
        for q in range(2):
            rq = r[:, q * 512:(q + 1) * 512]
            sq = st[:, q * 6:(q + 1) * 6]
            T.op("dve", lambda e: e.bn_stats(out=sq, in_=rq), reads=[rq], writes=[sq])
        s12 = st[:, 0:12]
        T.op("dve", lambda e: e.bn_aggr(out=mv[:, 0:2], in_=s12), reads=[s12], writes=[mv[:, 0:2]])
        rstd_from_var(mv)
        nmr = mv[:, 4:5]
        T.op("dve", lambda e: e.scalar_tensor_tensor(out=nmr, in0=mv[:, 0:1], scalar=-1.0, in1=mv[:, 2:3],
                                                     op0=ALU.mult, op1=ALU.mult), reads=[mv[:, 0:3]], writes=[nmr])
        nn = AR.ralloc("nn", [D], F32, 2)
        T.op("act", lambda e: e.activation(out=nn, in_=r, func=AF.Identity, bias=nmr, scale=mv[:, 2:3]),
             reads=[r, mv[:, 2:5]], writes=[nn])
        nb_t = AR.ralloc("nbt", [D], BF16, 4)
        T.op("dve", lambda e: e.tensor_copy(out=nb_t, in_=nn), reads=[nn], writes=[nb_t])
        hh = AR.ralloc("hh", [D], F32, 2)
        T.op("dve", lambda e: e.tensor_tensor(out=hh, in0=nn, in1=lng, op=ALU.mult), reads=[nn, lng], writes=[hh])
        T.op("dve", lambda e: e.tensor_tensor(out=hh, in0=hh, in1=lnb, op=ALU.add), reads=[hh, lnb], writes=[hh])
        T.dma("sp", h_scr[tt * 128:(tt + 1) * 128, :], hh, "hst%d" % (tt % 2), reads=[hh], writes=[("h", tt)])
        pend_nb.append((nb_t, tt))
        if len(pend_nb) > 2:
            emit_hT(*pend_nb.pop(0))
    AR.release("mT"); AR.release("w_o")
    AR.rfree("xr"); AR.rfree("r1")
    upT = AR.alloc("upT", [NF, 512], BF16)
    w_up_v = w_up_d.rearrange("(c p) n -> p c n", p=128)
    NSLOT = 3
    slots = [AR.alloc("wup%d" % i, [NCH, 512], BF16) for i in range(NSLOT)]

    def issue_up_load(idx):
        fq = idx % 8
        sl = slots[idx % NSLOT]
        T.dma("pool", sl, w_up_v[:, :, fq * 512:(fq + 1) * 512], "wup%d" % (idx % NSLOT), writes=[sl])

    total_loads = 4 * 8
    for idx in range(min(NSLOT, total_loads)):
        issue_up_load(idx)
    nxt_box = [NSLOT]

    def up_gen(G):
        tsl = slice(G * 512, (G + 1) * 512)
        for fq in range(8):
            idx = G * 8 + fq
            sl = slots[idx % NSLOT]
            for f4 in range(4):
                f = fq * 4 + f4
                b = PS.get()
                p = PS.f32(b)
                for c in range(NCH):
                    l = sl[:, c, f4 * 128:(f4 + 1) * 128]
                    r = hT[:, c, tsl]
                    T.op("pe", lambda e: e.matmul(out=p, lhsT=l, rhs=r, start=(c == 0), stop=(c == NCH - 1)),
                         reads=[l, r], writes=[p], inc=(c == NCH - 1))
                rt = AR.ralloc("relu_t", [512], BF16, 3)
                T.op("act", lambda e: e.activation(out=rt, in_=p, func=AF.Relu), reads=[p], writes=[rt])
                ud = upT[:, f, :]
                T.op("dve", lambda e: e.tensor_tensor(out=ud, in0=rt, in1=rt, op=ALU.mult), reads=[rt], writes=[ud])
                yield
            if nxt_box[0] < total_loads:
                issue_up_load(nxt_box[0])
                nxt_box[0] += 1

    up0 = up_gen(0)
    while pend_nb:
        for _ in range(6):
            next(up0)
        emit_hT(*pend_nb.pop(0))
    AR.release("g1c"); AR.release("b1c")
    for n_ in ("hh", "nn", "nbt", "lst1", "lmv1"):
        AR.rfree(n_)

    stop("F2")
    T.dma("sp", lng, ln2g_d, "lnp", writes=[lng])
    T.dma("sp", lnb, ln2b_d, "lnp", writes=[lnb], chain=True)
    for G in range(4):
        for _ in (up0 if G == 0 else up_gen(G)):
            pass
        hts = []
        for tq in range(4):
            tt = G * 4 + tq
            ht_ = AR.ralloc("hr", [D], F32, 4)
            T.dma("sp", ht_, h_scr[tt * 128:(tt + 1) * 128, :], "hr%d" % tq, reads=[("h", tt)], writes=[ht_])
            hts.append(ht_)
        for tq in range(4):
            tt = G * 4 + tq
            ht = hts[tq]
            r = AR.ralloc("r2", [D], F32, 2)
            for half in range(2):
                b = PS.get()
                p = PS.f32(b)
                for f in range(NF):
                    l = upT[:, f, tq * 128:(tq + 1) * 128]
                    rr = w_dn_q[f // 8][:, f % 8, half * 512:(half + 1) * 512]
                    T.op("pe", lambda e: e.matmul(out=p, lhsT=l, rhs=rr, start=(f == 0), stop=(f == NF - 1)),
                         reads=[l, rr], writes=[p], inc=(f == NF - 1))
                hq = ht[:, half * 512:(half + 1) * 512]
                rh = r[:, half * 512:(half + 1) * 512]
                T.op("dve", lambda e: e.scalar_tensor_tensor(out=rh, in0=hq, scalar=ALPHA, in1=p, op0=ALU.mult,
                                                             op1=ALU.add), reads=[hq, p], writes=[rh])
            AR.release("hr")
            ot = AR.ralloc("ot", [D], F32, 2)
            layernorm_tile(r, lng, lnb, ot, "2")
            AR.release("r2")
            T.dma("sp", out_d[tt * 128:(tt + 1) * 128, :], ot, "ost%d" % (tt % 2), reads=[ot], writes=[("o", tt)])
            AR.release("ot")
    return


def _host_consts():
    ident = np.eye(128, dtype=np.float32).astype(ml_dtypes.bfloat16)
    tri = np.triu(np.ones((128, 128), dtype=np.float32)).T
    negtri = np.where(tri > 0, 0.0, NEG).astype(np.float32)
    k = np.arange(NIT)
    p2 = np.concatenate([2.0 ** -(k + 2.0), 2.0 * 2.0 ** -(k + 2.0)]).astype(np.float32)
    pow2 = np.ascontiguousarray(np.broadcast_to(p2[None, :], (128, 2 * NIT))).astype(np.float32)
    return ident, tri.astype(ml_dtypes.bfloat16), negtri, pow2


def _bcast(v, n=128):
    v = np.asarray(v, dtype=np.float32).reshape(1, -1)
    return np.ascontiguousarray(np.broadcast_to(v, (n, v.shape[1])))


def make_in_maps(inputs, cores):
    ident, tri, negtri, pow2 = _host_consts()
    cw = np.asarray(inputs["conv_w"], dtype=np.float32)[0]
    conv_w_t = np.ascontiguousarray(cw.reshape(3, 4, 128).transpose(2, 1, 0).reshape(128, 12))
    shared = {
        "w_in": np.ascontiguousarray(inputs["w_in"][0], dtype=np.float32),
        "conv_w_t": conv_w_t,
        "ikg": _bcast(inputs["idx_k_norm_g"][0]),
        "ikb": _bcast(inputs["idx_k_norm_b"][0]),
        "w_branch": np.ascontiguousarray(inputs["w_branch"][0], dtype=np.float32),
        "w_o": np.ascontiguousarray(inputs["w_o"][0], dtype=np.float32),
        "ln1g": _bcast(inputs["ln1_g"][0]),
        "ln1b": _bcast(inputs["ln1_b"][0]),
        "ln1gc": np.ascontiguousarray(np.asarray(inputs["ln1_g"][0], dtype=np.float32).reshape(8, 128).T),
        "ln1bc": np.ascontiguousarray(np.asarray(inputs["ln1_b"][0], dtype=np.float32).reshape(8, 128).T),
        "w_up": np.ascontiguousarray(inputs["w_up"][0], dtype=np.float32),
        "w_down": np.ascontiguousarray(inputs["w_down"][0], dtype=np.float32),
        "ln2g": _bcast(inputs["ln2_g"][0]),
        "ln2b": _bcast(inputs["ln2_b"][0]),
        "ident": ident, "tri": tri, "negtri": negtri, "pow2": pow2,
    }
    x = np.asarray(inputs["x"], dtype=np.float32)
    maps = []
    for b in cores:
        m = dict(shared)
        m["x"] = np.ascontiguousarray(x[b])
        maps.append(m)
    return maps


def kernel(**inputs):
    nc, info = build_program(debug=False)
    in_maps = make_in_maps(inputs, list(range(8)))
    res = run_bass_kernel_spmd(nc, in_maps, core_ids=list(range(8)))
    out = np.stack([np.asarray(r["out"], dtype=np.float32) for r in res.results], axis=0)
    return out
```

```python
import os
import numpy as np
import ml_dtypes
import concourse.bass as bass
import concourse.mybir as mybir
from concourse.bass_utils import run_bass_kernel_spmd

F32 = mybir.dt.float32
BF16 = mybir.dt.bfloat16
ALU = mybir.AluOpType
AF = mybir.ActivationFunctionType
AX = mybir.AxisListType

S = 2048
D = 1024
NT = S // 128
NCH = D // 128
D_IN = 4936
DFF = 4096
NF = DFF // 128
ALPHA = 2.0 ** 0.25
LN_EPS = 1e-5
TOPK = 256
NIT = 10
NEG = -1.0e30
MASKNEG = -30000.0
DVE_RELU_HEADS = (1, 3, 5, 7)

C_B, C_C, C_U, C_Q, C_K, C_V, C_QI, C_KI, C_WI, C_GA, C_GB = 0, 512, 1024, 1536, 2048, 2176, 2304, 2816, 2880, 2888, 3912

SEM_LIMIT = 30000


def _esize(dt):
    return {F32: 4, BF16: 2}.get(dt, 4)


class _Eng:
    def __init__(self, name, eng, sem):
        self.name = name
        self.eng = eng
        self.sem = sem
        self.count = 0
        self.pending = False
        self.seen = {}


class Tracker:
    BLK = 256

    def __init__(self, nc):
        self.nc = nc
        self.nsem = 0
        self.engs = {}
        for name, eng in (("pe", nc.tensor), ("act", nc.scalar), ("dve", nc.vector),
                          ("pool", nc.gpsimd), ("sp", nc.sync)):
            self.engs[name] = _Eng(name, eng, self._newsem(name))
        self.blocks = {}
        self.dma_sems = {}
        self.dma_sems_by_id = {}
        self.nwaits = 0
        self.ninst = 0

    def _newsem(self, name):
        self.nsem += 1
        return self.nc.alloc_semaphore("s_%s_%d" % (name, self.nsem))

    def _keys(self, ap):
        t = ap.tensor
        space = "P" if "PSum" in type(t).__name__ else "S"
        pairs = ap.ap
        es = _esize(ap.dtype)
        pstride = pairs[0][0]
        npart = pairs[0][1]
        p0 = ap.offset // pstride if pstride else 0
        base = (ap.offset % pstride) * es if pstride else ap.offset * es
        halves = set()
        if p0 < 64:
            halves.add(0)
        if p0 + npart > 64:
            halves.add(1)
        free = pairs[1:]
        ranges = []
        if not free:
            ranges.append((base, base + es))
        else:
            outer = free[:-1]
            lstep, lcnt = free[-1]
            span = ((lcnt - 1) * abs(lstep) + 1) * es

            def rec(i, off):
                if i == len(outer):
                    ranges.append((off, off + span))
                    return
                st, cn = outer[i]
                for k in range(cn):
                    rec(i + 1, off + k * st * es)
            rec(0, base)
        keys = set()
        B = 2048 if space == "P" else self.BLK
        for lo, hi in ranges:
            for b in range(lo // B, (hi - 1) // B + 1):
                for h in halves:
                    keys.add((space, h, b))
        return keys

    def _allkeys(self, items):
        keys = set()
        for it in items:
            if it is None:
                continue
            if isinstance(it, (str, tuple)):
                keys.add(("D", it))
            else:
                keys |= self._keys(it)
        return keys

    def _deps(self, ename, rkeys, wkeys, is_dma):
        deps = {}

        def add(tk):
            sem, val, own = tk
            if (not is_dma) and own == ename and ename == "pe":
                return
            k = id(sem)
            if k not in deps or deps[k][1] < val:
                deps[k] = (sem, val, own)

        def add_waw(tk):
            sem, val, own = tk
            if (not is_dma) and own == ename and ename == "pe":
                return
            k = id(sem)
            if k not in deps or deps[k][1] < val:
                deps[k] = (sem, val, own)

        def add_raw(tk):
            sem, val, own = tk
            if (not is_dma) and own == ename and ename == "pe":
                return
            k = id(sem)
            if k not in deps or deps[k][1] < val:
                deps[k] = (sem, val, own)

        for key in rkeys:
            st = self.blocks.get(key)
            if st and st["w"] is not None:
                add_raw(st["w"])
            if st and key[0] == "P":
                for tk in st["r"].values():
                    add(tk)
        for key in wkeys:
            st = self.blocks.get(key)
            if st:
                if st["w"] is not None:
                    add_waw(st["w"])
                for tk in st["r"].values():
                    add(tk)
        return deps

    def _wait(self, E, deps):
        for k, (sem, val, own) in deps.items():
            if E.seen.get(k, 0) >= val:
                continue
            if own in self.engs:
                P = self.engs[own]
                if P.sem is sem:
                    assert val <= P.count, "wait on future inc (%s waits %s)" % (E.name, own)
            elif own.startswith("dma:"):
                val = self.dma_sems_by_id[k][1]
            E.eng.wait_ge(sem, val)
            E.seen[k] = val
            self.nwaits += 1

    def _update(self, rkeys, wkeys, tk):
        for key in rkeys:
            st = self.blocks.setdefault(key, {"w": None, "r": {}})
            k = id(tk[0])
            old = st["r"].get(k)
            if old is None or old[1] < tk[1]:
                st["r"][k] = tk
        for key in wkeys:
            self.blocks[key] = {"w": tk, "r": {}}

    def op(self, ename, fn, reads=(), writes=(), inc=True):
        E = self.engs[ename]
        rkeys = self._allkeys(reads)
        wkeys = self._allkeys(writes)
        deps = self._deps(ename, rkeys, wkeys, False)
        self._wait(E, deps)
        inst = fn(E.eng)
        self.ninst += 1
        if inc:
            E.count += 1
            inst.then_inc(E.sem, 1)
            E.pending = False
            tk = (E.sem, E.count, ename)
        else:
            E.pending = True
            tk = (E.sem, E.count + 1, ename)
        self._update(rkeys, wkeys, tk)
        if inc and E.count >= SEM_LIMIT:
            E.sem = self._newsem(ename)
            E.count = 0
        return inst

    def dma(self, qname, out, in_, skey, reads=(), writes=(), chain=False):
        E = self.engs[qname]
        rkeys = self._allkeys(list(reads))
        wkeys = self._allkeys(list(writes))
        deps = self._deps(qname, rkeys, wkeys, True)
        self._wait(E, deps)
        if skey not in self.dma_sems:
            self.dma_sems[skey] = [self._newsem("dma"), 0]
            self.dma_sems_by_id[id(self.dma_sems[skey][0])] = self.dma_sems[skey]
        rec = self.dma_sems[skey]
        if (not chain) and rec[1] > 0 and E.seen.get(id(rec[0]), 0) < rec[1]:
            E.eng.wait_ge(rec[0], rec[1])
            E.seen[id(rec[0])] = rec[1]
            self.nwaits += 1
        rec[1] += 16
        E.eng.dma_start(out=out, in_=in_).then_inc(rec[0], 16)
        self.ninst += 1
        tk = (rec[0], rec[1], "dma:" + str(skey))
        self._update(rkeys, wkeys, tk)

    def final_wait(self, qname, skey):
        rec = self.dma_sems[skey]
        self.engs[qname].eng.wait_ge(rec[0], rec[1])


class Arena:
    def __init__(self, nc, nbytes):
        self.nbytes = nbytes // 256 * 256
        self.t = nc.alloc_sbuf_tensor("arena", [128, self.nbytes // 2], BF16)
        self.free = [(0, self.nbytes)]
        self.live = {}
        self.rings = {}
        self.peak = 0

    def alloc(self, name, shape, dt):
        n = 1
        for s_ in shape:
            n *= s_
        nb = (n * _esize(dt) + 255) // 256 * 256
        for i, (lo, hi) in enumerate(self.free):
            if hi - lo >= nb:
                self.free[i] = (lo + nb, hi)
                if self.free[i][0] == self.free[i][1]:
                    self.free.pop(i)
                self.live[name] = (lo, nb)
                used = self.nbytes - sum(h - l for l, h in self.free)
                self.peak = max(self.peak, used)
                v = self.t[:, lo // 2:(lo + nb) // 2]
                if dt == F32:
                    v = v.bitcast(F32)
                v = v[:, 0:n]
                if len(shape) == 2:
                    v = v.rearrange("p (a b) -> p a b", a=shape[0])
                elif len(shape) == 3:
                    v = v.rearrange("p (a b c) -> p a b c", a=shape[0], b=shape[1])
                return v
        raise RuntimeError("arena OOM for %s (%d B); live=%s" % (name, nb, {k: v[1] for k, v in self.live.items()}))

    def ralloc(self, name, shape, dt, n=2):
        if name not in self.rings:
            self.rings[name] = [[self.alloc("%s#%d" % (name, i), shape, dt) for i in range(n)], 0]
        r = self.rings[name]
        r[1] += 1
        return r[0][r[1] % len(r[0])]

    def rfree(self, name):
        for i in range(len(self.rings[name][0])):
            self.release("%s#%d" % (name, i))
        del self.rings[name]

    def release(self, name):
        if name in self.rings:
            return
        lo, nb = self.live.pop(name)
        self.free.append((lo, lo + nb))
        self.free.sort()
        merged = []
        for l, h in self.free:
            if merged and merged[-1][1] == l:
                merged[-1] = (merged[-1][0], h)
            else:
                merged.append((l, h))
        self.free = merged


class PsumPool:
    def __init__(self, nc):
        self.t = nc.alloc_psum_tensor("psum", [128, 4096], F32)
        self.order = list(range(8))
        self.held = set()

    def get(self, hold=False):
        for b in self.order:
            if b not in self.held:
                self.order.remove(b)
                self.order.append(b)
                if hold:
                    self.held.add(b)
                return b
        raise RuntimeError("no free PSUM bank")

    def release(self, b):
        self.held.discard(b)

    def f32(self, b):
        return self.t[:, b * 512:(b + 1) * 512]

    def bf16(self, b):
        return self.t[:, b * 512:(b + 1) * 512].bitcast(BF16)


class _Stop(Exception):
    pass


def build_program(debug=False, stop_after=None):
    nc = bass.Bass("TRN2", target_bir_lowering=False)
    T = Tracker(nc)
    dbg = {}
    try:
        _build_body(nc, T, dbg, debug, stop_after)
    except _Stop:
        pass
    for skey in list(T.dma_sems.keys()):
        if skey.startswith("ost") or skey == "dbg":
            T.final_wait("sp", skey)
    info = {"ninst": T.ninst, "nwaits": T.nwaits, "nsem": T.nsem,
            "counts": {k: v.count for k, v in T.engs.items()}}
    return nc, info


def _build_body(nc, T, dbg, debug, stop_after):
    def stop(tag):
        if stop_after == tag:
            raise _Stop()

    def din(name, shape, dt=F32):
        return nc.dram_tensor(name, list(shape), dt, kind="ExternalInput").ap()

    x_d = din("x", [S, D])
    w_in_d = din("w_in", [D, D_IN])
    convw_d = din("conv_w_t", [128, 12])
    ikg_d = din("ikg", [128, 64])
    ikb_d = din("ikb", [128, 64])
    w_br_d = din("w_branch", [2, 512, D])
    w_o_d = din("w_o", [D, D])
    ln1g_d = din("ln1g", [128, D])
    ln1b_d = din("ln1b", [128, D])
    ln1gc_d = din("ln1gc", [128, 8])
    ln1bc_d = din("ln1bc", [128, 8])
    w_up_d = din("w_up", [D, DFF])
    w_down_d = din("w_down", [DFF, D])
    ln2g_d = din("ln2g", [128, D])
    ln2b_d = din("ln2b", [128, D])
    ident_d = din("ident", [128, 128], BF16)
    tri_d = din("tri", [128, 128], BF16)
    negtri_d = din("negtri", [128, 128])
    pow2_d = din("pow2", [128, 2 * NIT])
    out_d = nc.dram_tensor("out", [S, D], F32, kind="ExternalOutput").ap()
    h_scr = nc.dram_tensor("h_scr", [S, D], F32, kind="ExternalOutput" if debug else "Internal").ap()

    def dbg_out(name, shape, dt=F32):
        if not debug:
            return None
        dbg[name] = nc.dram_tensor("dbg_" + name, list(shape), dt, kind="ExternalOutput").ap()
        return dbg[name]

    AR = Arena(nc, int(nc.sbuf_bytes_remaining) - 1024)
    PS = PsumPool(nc)

    w_in_v = w_in_d.rearrange("(c p) n -> p c n", p=128)

    def dump(name, ap_sb, shape, dt=F32):
        d = dbg_out(name, shape, dt)
        if d is None:
            return
        T.dma("sp", d, ap_sb, "dbg", reads=[ap_sb], writes=["dbg_" + name])

    def rstd_from_var(mv):
        T.op("dve", lambda e: e.tensor_scalar(out=mv[:, 3:4], in0=mv[:, 1:2], scalar1=LN_EPS, scalar2=None,
                                              op0=ALU.add), reads=[mv[:, 1:2]], writes=[mv[:, 3:4]])
        T.op("act", lambda e: e.activation(out=mv[:, 3:4], in_=mv[:, 3:4], func=AF.Sqrt), reads=[mv[:, 3:4]],
             writes=[mv[:, 3:4]])
        T.op("dve", lambda e: e.reciprocal(out=mv[:, 2:3], in_=mv[:, 3:4]), reads=[mv[:, 3:4]], writes=[mv[:, 2:3]])

    ident = AR.alloc("ident", [128], BF16)
    tri = AR.alloc("tri", [128], BF16)
    negtri = AR.alloc("negtri", [128], F32)
    pow2 = AR.alloc("pow2", [2 * NIT], F32)
    convw = AR.alloc("convw", [12], F32)
    ikg = AR.alloc("ikg", [64], F32)
    ikb = AR.alloc("ikb", [64], F32)
    for dst, src in ((ident, ident_d), (tri, tri_d), (negtri, negtri_d), (pow2, pow2_d),
                     (convw, convw_d), (ikg, ikg_d), (ikb, ikb_d)):
        T.dma("sp", dst, src, "const", writes=[dst], chain=True)

    def load_w_cols(name, col0, ncols, dup64=False):
        w = AR.alloc(name, [NCH, ncols], BF16)
        T.dma("pool", w, w_in_v[:, :, col0:col0 + ncols], "w_" + name, writes=[w])
        return w

    xT = AR.alloc("xT", [NCH, S], BF16)
    xts = [AR.alloc("xt%d" % i, [D], F32) for i in range(2)]
    xbs = [AR.alloc("xb%d" % i, [D], BF16) for i in range(2)]
    w_kw = load_w_cols("w_kw", C_KI, 72)
    w_qi = load_w_cols("w_qi", C_QI, 512)
    for tt in range(NT):
        xt = xts[tt % 2]
        xb = xbs[tt % 2]
        T.dma("sp", xt, x_d[tt * 128:(tt + 1) * 128, :], "xld%d" % (tt % 2), writes=[xt])
        T.op("act", lambda e: e.copy(out=xb, in_=xt), reads=[xt], writes=[xb])
        b = PS.get()
        pb = PS.bf16(b)
        for c in range(NCH):
            o = pb[:, c * 128:(c + 1) * 128]
            i_ = xb[:, c * 128:(c + 1) * 128]
            T.op("pe", lambda e: e.transpose(out=o, in_=i_, identity=ident), reads=[i_, ident], writes=[o],
                 inc=(c == NCH - 1))
        dst = xT[:, :, tt * 128:(tt + 1) * 128]
        src = pb.rearrange("p (c t) -> p c t", c=NCH)
        T.op("dve", lambda e: e.tensor_copy(out=dst, in_=src), reads=[pb], writes=[dst])
    AR.release("xt0"); AR.release("xt1"); AR.release("xb0"); AR.release("xb1")
    if debug:
        dump("xT", xT, [128, NCH, S], BF16)
    stop("A")

    def proj_feat(w, wc0, dst_fn, evac):
        for m in range(4):
            b = PS.get()
            p = PS.f32(b)
            for c in range(NCH):
                l = w[:, c, wc0:wc0 + 128]
                r = xT[:, c, m * 512:(m + 1) * 512]
                T.op("pe", lambda e: e.matmul(out=p, lhsT=l, rhs=r, start=(c == 0), stop=(c == NCH - 1)),
                     reads=[l, r], writes=[p], inc=(c == NCH - 1))
            evac(m, p)

    cp_toggle = [0]

    def evac_copy(dst, src):
        cp_toggle[0] ^= 1
        if cp_toggle[0]:
            T.op("act", lambda e: e.copy(out=dst, in_=src), reads=[src], writes=[dst])
        else:
            T.op("dve", lambda e: e.tensor_copy(out=dst, in_=src), reads=[src], writes=[dst])

    w_q = load_w_cols("w_q", C_Q, 512)
    w_k = [AR.alloc("w_k%d" % g, [NCH, 128], BF16) for g in range(2)]
    for g in range(2):
        for half in range(2):
            dstw = w_k[g][:, :, half * 64:(half + 1) * 64]
            T.dma("pool", dstw, w_in_v[:, :, C_K + g * 64:C_K + (g + 1) * 64], "w_k%d%d" % (g, half), writes=[dstw])
    w_v = load_w_cols("w_v", C_V, 128)
    kiT2 = AR.alloc("kiT2", [S], BF16)
    wi = AR.alloc("wi", [NT, 8], F32)
    qiT = AR.alloc("qiT", [4, S], BF16)
    qT = AR.alloc("qT", [4, S], BF16)
    kT2 = AR.alloc("kT2", [2, S], BF16)

    def ki_gen():
        for tt in range(NT):
            b = PS.get()
            p = PS.f32(b)[:, 0:72]
            for c in range(NCH):
                l = xT[:, c, tt * 128:(tt + 1) * 128]
                r = w_kw[:, c, :]
                T.op("pe", lambda e: e.matmul(out=p, lhsT=l, rhs=r, start=(c == 0), stop=(c == NCH - 1)),
                     reads=[l, r], writes=[p], inc=(c == NCH - 1))
            st = AR.ralloc("kst", [8], F32, 3)
            mv = AR.ralloc("kmv", [4], F32, 3)
            kn = AR.ralloc("kn", [64], F32, 3)
            kn2 = AR.ralloc("kn2", [128], BF16, 3)
            pk = p[:, 0:64]
            T.op("dve", lambda e: e.bn_stats(out=st[:, 0:6], in_=pk), reads=[pk], writes=[st])
            T.op("dve", lambda e: e.bn_aggr(out=mv[:, 0:2], in_=st[:, 0:6]), reads=[st], writes=[mv[:, 0:2]])
            rstd_from_var(mv)
            T.op("dve", lambda e: e.tensor_scalar(out=kn, in0=pk, scalar1=mv[:, 0:1], scalar2=mv[:, 2:3],
                                                  op0=ALU.subtract, op1=ALU.mult), reads=[pk, mv], writes=[kn])
            wsrc = p[:, 64:72]
            wdst = wi[:, tt, :]
            T.op("dve", lambda e: e.tensor_copy(out=wdst, in_=wsrc), reads=[wsrc], writes=[wdst])
            T.op("dve", lambda e: e.tensor_tensor(out=kn, in0=kn, in1=ikg, op=ALU.mult), reads=[kn, ikg], writes=[kn])
            T.op("dve", lambda e: e.tensor_tensor(out=kn2[:, 0:64], in0=kn, in1=ikb, op=ALU.add),
                 reads=[kn, ikb], writes=[kn2[:, 0:64]])
            T.op("dve", lambda e: e.tensor_copy(out=kn2[:, 64:128], in_=kn2[:, 0:64]), reads=[kn2[:, 0:64]],
                 writes=[kn2[:, 64:128]])
            yield
            b2 = PS.get()
            pb = PS.bf16(b2)[:, 0:128]
            T.op("pe", lambda e: e.transpose(out=pb, in_=kn2, identity=ident), reads=[kn2, ident], writes=[pb])
            kd = kiT2[:, tt * 128:(tt + 1) * 128]
            T.op("act", lambda e: e.copy(out=kd, in_=pb), reads=[pb], writes=[kd])
            yield

    def projB_gen():
        for (w, wc0, dst) in ([(w_qi, j * 128, qiT[:, j, :]) for j in range(4)]
                              + [(w_q, j * 128, qT[:, j, :]) for j in range(4)]
                              + [(w_k[g], 0, kT2[:, g, :]) for g in range(2)]):
            for m in range(4):
                b = PS.get()
                p = PS.f32(b)
                for c in range(NCH):
                    l = w[:, c, wc0:wc0 + 128]
                    r = xT[:, c, m * 512:(m + 1) * 512]
                    T.op("pe", lambda e: e.matmul(out=p, lhsT=l, rhs=r, start=(c == 0), stop=(c == NCH - 1)),
                         reads=[l, r], writes=[p], inc=(c == NCH - 1))
                evac_copy(dst[:, m * 512:(m + 1) * 512], p)
                yield

    def run2(ga, gb, ra, rb):
        alive = [True, True]
        gens = [ga, gb]
        while any(alive):
            for q, reps in ((0, ra), (1, rb)):
                if not alive[q]:
                    continue
                for _ in range(reps):
                    try:
                        next(gens[q])
                    except StopIteration:
                        alive[q] = False
                        break
    run2(ki_gen(), projB_gen(), 1, 1)
    AR.release("w_kw"); AR.release("w_qi"); AR.release("w_q"); AR.release("w_k0"); AR.release("w_k1")
    for n_ in ("kst", "kmv", "kn", "kn2"):
        AR.rfree(n_)
    stop("B3")
    Vaug = [[AR.alloc("Vaug%d%d" % (g, e), [NT, 128], BF16) for e in range(2)] for g in range(2)]
    for g in range(2):
        for e_ in range(2):
            va = Vaug[g][e_]
            T.op("dve", lambda e: e.memset(va, 1.0), writes=[va])
    for tt in range(NT):
        b = PS.get()
        p = PS.f32(b)[:, 0:128]
        for c in range(NCH):
            l = xT[:, c, tt * 128:(tt + 1) * 128]
            r = w_v[:, c, :]
            T.op("pe", lambda e: e.matmul(out=p, lhsT=l, rhs=r, start=(c == 0), stop=(c == NCH - 1)),
                 reads=[l, r], writes=[p], inc=(c == NCH - 1))
        for g in range(2):
            src = p[:, g * 64:(g + 1) * 64]
            d0 = Vaug[g][0][:, tt, 0:64]
            d1 = Vaug[g][1][:, tt, 64:128]
            T.op("act", lambda e: e.copy(out=d0, in_=src), reads=[src], writes=[d0])
            T.op("dve", lambda e: e.tensor_copy(out=d1, in_=src), reads=[src], writes=[d1])
    AR.release("w_v")
    stop("B4")
    if debug:
        dump("qiT", qiT, [128, 4, S], BF16)
        dump("kiT2", kiT2, [128, S], BF16)
        dump("wi", wi, [128, NT, 8])
        dump("qT", qT, [128, 4, S], BF16)
        dump("kT2", kT2, [128, 2, S], BF16)
        dump("Vaug00", Vaug[0][0], [128, NT, 128], BF16)

    stop("B")
    xT_scr = nc.dram_tensor("xT_scr", [128, NCH, S], BF16, kind="Internal").ap()
    T.dma("sp", xT_scr, xT, "xTsp", reads=[xT], writes=["xT_scr"])
    AR.release("xT")

    y_bT = AR.alloc("y_bT", [4, S], BF16)
    maskTs = [AR.alloc("maskT%d" % i, [NT, 512], BF16) for i in range(2)]
    junks = [AR.alloc("junk%d" % i, [S], BF16) for i in range(2)]
    junkd = [AR.alloc("junkd%d" % i, [S], BF16) for i in range(1)]
    negtri_b = AR.alloc("negtri_b", [128], BF16)
    T.op("dve", lambda e: e.tensor_scalar(out=negtri_b, in0=negtri, scalar1=MASKNEG / NEG, scalar2=None, op0=ALU.mult),
         reads=[negtri], writes=[negtri_b])
    scores = [AR.alloc("score%d" % i, [S], F32) for i in range(4)]
    maskbs = [AR.alloc("maskb%d" % i, [S], BF16) for i in range(2)]
    rtmps = [AR.alloc("rtmp%d" % i, [512], BF16) for i in range(10)]
    dws = [AR.alloc("dw%d" % i, [8, 128], BF16) for i in range(4)]
    ptiles = [AR.alloc("ptile%d" % i, [512], BF16) for i in range(12)]
    rcs = [AR.alloc("rc%d" % i, [512], F32) for i in range(2)]
    ctr = {"rtmp": 0, "ptile": 0, "rc": 0, "maskb": 0, "junk": 0, "junkd": 0}

    def ring(lst, key):
        ctr[key] += 1
        return lst[ctr[key] % len(lst)]

    if debug:
        dbg_mask = dbg_out("mask", [NT, 128, S], BF16)
        dbg_score = dbg_out("score", [NT, 128, S], F32)

    def topk_chunk(m):
        maskT = maskTs[m % 2]
        tiles = list(range(4 * m, 4 * m + 4))
        sel = [i for i in tiles if i >= 2]
        nb = len(sel)
        col = {i: c for c, i in enumerate(sel)}
        for i in sel:
            N = 128 * (i + 1)
            c_ = col[i]
            score = scores[c_]
            for h in range(8):
                dwt = dws[c_][:, h, :]
                wsc = wi[:, i, h:h + 1]
                T.op("dve", lambda e: e.tensor_scalar(out=dwt, in0=ident, scalar1=wsc, scalar2=None, op0=ALU.mult),
                     reads=[ident, wsc], writes=[dwt])
            for kc in range((N + 511) // 512):
                k0 = kc * 512
                n = min(512, N - k0)
                sc = score[:, k0:k0 + n]
                sb_ = PS.get(hold=True)
                sacc = PS.f32(sb_)[:, 0:n]
                pendq = []

                def emit_acc(it):
                    tv_, h_ = it
                    dw_ = dws[c_][:, h_, :]
                    T.op("pe", lambda e: e.matmul(out=sacc, lhsT=dw_, rhs=tv_, start=(h_ == 0), stop=(h_ == 7)),
                         reads=[dw_, tv_], writes=[sacc], inc=(h_ == 7))
                for hp in range(4):
                    pair = []
                    for h in (2 * hp, 2 * hp + 1):
                        e2 = h % 2
                        b = PS.get()
                        p = PS.f32(b)[:, 0:n]
                        l = qiT[64 * e2:64 * e2 + 64, h // 2, i * 128:(i + 1) * 128]
                        r = kiT2[64 * e2:64 * e2 + 64, k0:k0 + n]
                        T.op("pe", lambda e: e.matmul(out=p, lhsT=l, rhs=r, start=True, stop=True),
                             reads=[l, r], writes=[p])
                        pair.append((h, p))
                    for h, p in pair:
                        tv = ring(rtmps, "rtmp")[:, 0:n]
                        if h in DVE_RELU_HEADS:
                            T.op("dve", lambda e: e.tensor_scalar(out=tv, in0=p, scalar1=0.0, scalar2=None, op0=ALU.max),
                                 reads=[p], writes=[tv])
                        else:
                            T.op("act", lambda e: e.activation(out=tv, in_=p, func=AF.Relu), reads=[p], writes=[tv])
                        pendq.append((tv, h))
                    while len(pendq) > 8:
                        emit_acc(pendq.pop(0))
                while pendq:
                    emit_acc(pendq.pop(0))
                if k0 + n == N:
                    nd = n - 128
                    if nd > 0:
                        T.op("dve", lambda e: e.tensor_copy(out=sc[:, 0:nd], in_=sacc[:, 0:nd]), reads=[sacc],
                             writes=[sc[:, 0:nd]])
                    T.op("dve", lambda e: e.tensor_tensor(out=sc[:, nd:n], in0=sacc[:, nd:n], in1=negtri, op=ALU.add),
                         reads=[sacc, negtri], writes=[sc[:, nd:n]])
                else:
                    T.op("dve", lambda e: e.tensor_copy(out=sc, in_=sacc), reads=[sacc], writes=[sc])
                PS.release(sb_)
                yield 'S'
        taus = {}
        if nb:
            half = (nb + 1) // 2
            groups = [g_ for g_ in (sel[:half], sel[half:]) if g_]
            sts = []
            for gi, grp in enumerate(groups):
                ng = len(grp)
                sm = AR.alloc("tk_small%d" % gi, [64 + 3 * NIT * 2], F32)
                st_ = {"hi": sm[:, 0:ng], "lo": sm[:, 2:2 + ng], "R": sm[:, 4:4 + ng], "nmid": sm[:, 6:6 + ng],
                       "tq": sm[:, 8:8 + ng], "tau": sm[:, 10:10 + ng], "npl": sm[:, 12:12 + ng],
                       "thr": sm[:, 14:14 + ng],
                       "Rk": sm[:, 64:64 + 2 * NIT].rearrange("p (k c) -> p k c", k=NIT),
                       "Rk2": sm[:, 64 + 2 * NIT:64 + 4 * NIT].rearrange("p (k c) -> p k c", k=NIT),
                       "cnt": sm[:, 64 + 4 * NIT:64 + 6 * NIT].rearrange("p (k c) -> p k c", k=NIT),
                       "mid": sm[:, 16:16 + ng], "grp": grp, "ng": ng, "gi": gi}
                sts.append(st_)
                for lc_, i in enumerate(grp):
                    c = col[i]
                    N = 128 * (i + 1)
                    sN = scores[c][:, 0:N]
                    sL = scores[c][:, 0:128 * i]
                    hc = st_["hi"][:, lc_:lc_ + 1]; lc = st_["lo"][:, lc_:lc_ + 1]; tc0 = st_["thr"][:, lc_:lc_ + 1]
                    T.op("dve", lambda e: e.memset(tc0, float(2 * TOPK - 2 - N) if gi == 0 else float(TOPK - 1)),
                         writes=[tc0])
                    T.op("dve", lambda e: e.tensor_reduce(out=hc, in_=sN, axis=AX.X, op=ALU.max), reads=[sN], writes=[hc])
                    T.op("dve", lambda e: e.tensor_reduce(out=lc, in_=sL, axis=AX.X, op=ALU.min), reads=[sL], writes=[lc])
                    yield 'S'
                hi = st_["hi"]; lo = st_["lo"]; R = st_["R"]; nmid = st_["nmid"]
                T.op("dve", lambda e: e.tensor_tensor(out=R, in0=hi, in1=lo, op=ALU.subtract), reads=[hi, lo], writes=[R])
                T.op("dve", lambda e: e.scalar_tensor_tensor(out=nmid, in0=R, scalar=-0.5, in1=lo, op0=ALU.mult,
                                                             op1=ALU.subtract), reads=[R, lo], writes=[nmid])
                mid_ = st_["mid"]
                T.op("dve", lambda e: e.tensor_scalar(out=mid_, in0=nmid, scalar1=-1.0, scalar2=None, op0=ALU.mult),
                     reads=[nmid], writes=[mid_])
                for lc_ in range(ng):
                    rc_ = R[:, lc_:lc_ + 1]
                    o1 = st_["Rk"][:, :, lc_]
                    o2 = st_["Rk2"][:, :, lc_]
                    T.op("dve", lambda e: e.tensor_scalar(out=o1, in0=pow2[:, 0:NIT], scalar1=rc_, scalar2=None,
                                                          op0=ALU.mult), reads=[pow2, rc_], writes=[o1])
                    T.op("dve", lambda e: e.tensor_scalar(out=o2, in0=pow2[:, NIT:2 * NIT], scalar1=rc_, scalar2=None,
                                                          op0=ALU.mult), reads=[pow2, rc_], writes=[o2])
            for k in range(NIT):
                for st_ in sts:
                    for lc_, i in enumerate(st_["grp"]):
                        c = col[i]
                        N = 128 * (i + 1)
                        sN = scores[c][:, 0:N]
                        jn = ring(junks, "junk")[:, 0:N]
                        ck = st_["cnt"][:, k, lc_:lc_ + 1]
                        if st_["gi"] == 0:
                            mc = st_["nmid"][:, lc_:lc_ + 1]
                            T.op("act", lambda e: e.activation(out=jn, in_=sN, func=AF.Sign, bias=mc, scale=1.0,
                                                               accum_out=ck), reads=[sN, mc], writes=[jn, ck])
                        else:
                            mc = st_["mid"][:, lc_:lc_ + 1]
                            jn = ring(junkd, "junkd")[:, 0:N]
                            T.op("dve", lambda e: e.tensor_scalar(out=jn, in0=sN, scalar1=mc, scalar2=0.0,
                                                                  op0=ALU.is_ge, op1=ALU.add, accum_out=ck),
                                 reads=[sN, mc], writes=[jn, ck])
                    yield 'B'
                for st_ in sts:
                    ng = st_["ng"]
                    r1 = st_["Rk"][:, k, 0:ng]
                    r2 = st_["Rk2"][:, k, 0:ng]
                    ckk = st_["cnt"][:, k, 0:ng]
                    nmid = st_["nmid"]; npl = st_["npl"]; tq = st_["tq"]; thr = st_["thr"]
                    if st_["gi"] == 1:
                        mid_ = st_["mid"]
                        T.op("dve", lambda e: e.tensor_tensor(out=npl, in0=mid_, in1=r1, op=ALU.subtract),
                             reads=[mid_, r1], writes=[npl])
                        T.op("dve", lambda e: e.scalar_tensor_tensor(out=tq, in0=ckk, scalar=TOPK - 0.5, in1=r2,
                                                                     op0=ALU.is_ge, op1=ALU.mult),
                             reads=[ckk, r2], writes=[tq])
                        T.op("dve", lambda e: e.tensor_tensor(out=mid_, in0=npl, in1=tq, op=ALU.add),
                             reads=[npl, tq], writes=[mid_])
                        continue
                    T.op("pool", lambda e: e.tensor_tensor(out=npl, in0=nmid, in1=r1, op=ALU.add), reads=[nmid, r1], writes=[npl])
                    T.op("pool", lambda e: e.tensor_tensor(out=tq, in0=ckk, in1=thr, op=ALU.subtract), reads=[ckk, thr], writes=[tq])
                    T.op("pool", lambda e: e.tensor_scalar(out=tq, in0=tq, scalar1=1.0, scalar2=0.0, op0=ALU.min,
                                                           op1=ALU.max), reads=[tq], writes=[tq])
                    T.op("pool", lambda e: e.tensor_tensor(out=tq, in0=tq, in1=r2, op=ALU.mult), reads=[tq, r2], writes=[tq])
                    T.op("pool", lambda e: e.tensor_tensor(out=nmid, in0=npl, in1=tq, op=ALU.subtract),
                         reads=[npl, tq], writes=[nmid])
            for st_ in sts:
                ng = st_["ng"]
                rl = st_["Rk"][:, NIT - 1, 0:ng]
                tau = st_["tau"]; nmid = st_["nmid"]
                if st_["gi"] == 1:
                    mid_ = st_["mid"]
                    T.op("dve", lambda e: e.tensor_tensor(out=tau, in0=mid_, in1=rl, op=ALU.subtract), reads=[mid_, rl],
                         writes=[tau])
                else:
                    T.op("dve", lambda e: e.tensor_tensor(out=tau, in0=nmid, in1=rl, op=ALU.add), reads=[nmid, rl],
                         writes=[tau])
                    T.op("dve", lambda e: e.tensor_scalar(out=tau, in0=tau, scalar1=-1.0, scalar2=None, op0=ALU.mult),
                         reads=[tau], writes=[tau])
                for lc_, i in enumerate(st_["grp"]):
                    taus[i] = tau[:, lc_:lc_ + 1]
        for i in tiles:
            N = 128 * (i + 1)
            maskb = ring(maskbs, "maskb")
            if i < 2:
                if i == 1:
                    T.op("dve", lambda e: e.memset(maskb[:, 0:128], 1.0), writes=[maskb[:, 0:128]])
                dd = maskb[:, 128 * i:128 * (i + 1)]
                T.op("dve", lambda e: e.tensor_copy(out=dd, in_=tri), reads=[tri], writes=[dd])
            else:
                c = col[i]
                sN = scores[c][:, 0:N]
                mN = maskb[:, 0:N]
                tc_ = taus[i]
                T.op("dve", lambda e: e.tensor_scalar(out=mN, in0=sN, scalar1=tc_, scalar2=None, op0=ALU.is_ge),
                     reads=[sN, tc_], writes=[mN])
                if debug:
                    T.dma("sp", dbg_score[i, :, 0:N], sN, "dbg", reads=[sN], writes=["dbg_score"])
            if debug:
                T.dma("sp", dbg_mask[i, :, 0:N], maskb[:, 0:N], "dbg", reads=[maskb[:, 0:N]], writes=["dbg_mask"])
            off = (i - 4 * m) * 128
            for j0 in range(0, i + 1, 8):
                nbk = min(8, i + 1 - j0)
                b = PS.get()
                pb = PS.bf16(b)
                for jj in range(nbk):
                    j = j0 + jj
                    o = pb[:, jj * 128:(jj + 1) * 128]
                    i_ = maskb[:, j * 128:(j + 1) * 128]
                    T.op("pe", lambda e: e.transpose(out=o, in_=i_, identity=ident), reads=[i_, ident], writes=[o],
                         inc=(jj == nbk - 1))
                dst = maskT[:, j0:j0 + nbk, off:off + 128]
                src = pb[:, 0:nbk * 128].rearrange("p (j t) -> p j t", j=nbk)
                T.op("act", lambda e: e.copy(out=dst, in_=src), reads=[pb[:, 0:nbk * 128]], writes=[dst])
            yield 'B'
        if nb:
            for gi in range(len(groups)):
                AR.release("tk_small%d" % gi)

    def attn_chunk(m):
        maskT = maskTs[m % 2]
        jmax = 4 * m + 3
        for cq in range(4):
            g = cq // 2
            abs_ = [PS.get(hold=True) for _ in range(2)]
            accs = [PS.f32(ab) for ab in abs_]
            pend = []

            def emit_pv(it):
                va_, pv_, ao_, j_ = it
                T.op("pe", lambda e: e.matmul(out=ao_, lhsT=va_, rhs=pv_, start=(j_ == 0), stop=(j_ == jmax)),
                     reads=[va_, pv_], writes=[ao_], inc=(j_ == jmax))
            for j in range(jmax + 1):
                t0 = max(512 * m, 128 * j)
                n = 512 * (m + 1) - t0
                off = t0 - 512 * m
                ps_ = []
                for e2 in range(2):
                    b = PS.get()
                    p = PS.f32(b)[:, 0:n]
                    l = kT2[64 * e2:64 * e2 + 64, g, j * 128:(j + 1) * 128]
                    r = qT[64 * e2:64 * e2 + 64, cq, t0:t0 + n]
                    T.op("pe", lambda e: e.matmul(out=p, lhsT=l, rhs=r, start=True, stop=True), reads=[l, r], writes=[p])
                    ps_.append(p)
                mt = maskT[:, j, off:off + n]
                for e2 in range(2):
                    p = ps_[e2]
                    pv = ring(ptiles, "ptile")[:, 0:n]
                    T.op("act", lambda e: e.activation(out=pv, in_=p, func=AF.Exp, scale=0.125), reads=[p], writes=[pv])
                    T.op("dve", lambda e: e.tensor_tensor(out=pv, in0=pv, in1=mt, op=ALU.mult), reads=[pv, mt],
                         writes=[pv])
                    va = Vaug[g][e2][:, j, :]
                    ao = accs[e2][:, off:off + n]
                    pend.append((va, pv, ao, j))
                while len(pend) > 6:
                    emit_pv(pend.pop(0))
                if j % 2 == 1:
                    yield
            while pend:
                emit_pv(pend.pop(0))
            for e2 in range(2):
                acc = accs[e2]
                rc = ring(rcs, "rc")
                po = 64 * e2
                pd = 64 * (1 - e2)
                rcv = rc[po:po + 64, :]
                den = acc[pd:pd + 64, :]
                T.op("dve", lambda e: e.reciprocal(out=rcv, in_=den), reads=[den], writes=[rcv])
                yo = y_bT[po:po + 64, cq, 512 * m:512 * (m + 1)]
                num = acc[po:po + 64, :]
                T.op("dve", lambda e: e.tensor_tensor(out=yo, in0=num, in1=rcv, op=ALU.mult), reads=[num, rcv],
                     writes=[yo])
                PS.release(abs_[e2])
            yield

    def run_interleaved(ga, gb, ra=1, rb=1, warm=0):
        alive = [ga is not None, gb is not None]
        gens = [ga, gb]
        reps = [ra, rb]
        for _ in range(warm):
            try:
                next(ga)
            except StopIteration:
                alive[0] = False
                break
        while any(alive):
            for q in range(2):
                if not alive[q]:
                    continue
                for _ in range(reps[q]):
                    try:
                        next(gens[q])
                    except StopIteration:
                        alive[q] = False
                        break

    def run_weighted(ga, gb, wS, wB):
        credit = 0.0
        alive = True
        for tag in gb:
            credit += wS if tag == 'S' else wB
            while alive and credit >= 1.0:
                try:
                    next(ga)
                except StopIteration:
                    alive = False
                credit -= 1.0
        if alive:
            for _ in ga:
                pass

    run_interleaved(topk_chunk(0), None)
    for m in range(3):
        nA = 4 * (2 * m + 3)
        nS = 4 * (m + 2) + 4
        nB = 2 * NIT + 4
        wS = 0.05
        run_weighted(attn_chunk(m), topk_chunk(m + 1), wS, max(0.3, (nA - wS * nS) / nB))
    for n_ in (["junk0", "junk1", "junkd0", "negtri_b", "qiT", "kiT2", "wi"] + ["score%d" % i for i in range(4)]
               + ["maskb%d" % i for i in range(2)] + ["rtmp%d" % i for i in range(10)] + ["dw%d" % i for i in range(4)]):
        AR.release(n_)
    xT = AR.alloc("xT", [NCH, S], BF16)
    T.dma("sp", xT, xT_scr, "xTsp", reads=["xT_scr"], writes=[xT])
    w_c = load_w_cols("w_c", C_C, 512)
    w_u = load_w_cols("w_u", C_U, 512)
    w_b = load_w_cols("w_b", C_B, 512)
    def proj_chunk(w, wc0, m):
        b = PS.get()
        p = PS.f32(b)
        for c in range(NCH):
            l = w[:, c, wc0:wc0 + 128]
            r = xT[:, c, m * 512:(m + 1) * 512]
            T.op("pe", lambda e: e.matmul(out=p, lhsT=l, rhs=r, start=(c == 0), stop=(c == NCH - 1)),
                 reads=[l, r], writes=[p], inc=(c == NCH - 1))
        return p

    def conv_phase():
        for j in range(4):
            c_sb = AR.ralloc("c_sb", [S], F32, 1)
            vpad = AR.ralloc("vpad", [S + 64], F32, 2)
            cacc = AR.ralloc("cacc", [S], F32, 1)
            T.op("pool", lambda e: e.memset(vpad[:, 0:2], 0.0), writes=[vpad[:, 0:2]])
            for m in range(4):
                p = proj_chunk(w_c, j * 128, m)
                evac_copy(c_sb[:, m * 512:(m + 1) * 512], p)
                yield
            for m in range(4):
                p = proj_chunk(w_u, j * 128, m)
                d_ = vpad[:, 2 + m * 512:2 + (m + 1) * 512]
                s_ = c_sb[:, m * 512:(m + 1) * 512]
                T.op("dve", lambda e: e.tensor_tensor(out=d_, in0=p, in1=s_, op=ALU.mult), reads=[p, s_], writes=[d_])
                yield
            w0 = convw[:, j * 3 + 0:j * 3 + 1]
            w1 = convw[:, j * 3 + 1:j * 3 + 2]
            w2 = convw[:, j * 3 + 2:j * 3 + 3]
            v2 = vpad[:, 2:S + 2]; v1 = vpad[:, 1:S + 1]; v0 = vpad[:, 0:S]
            T.op("dve", lambda e: e.tensor_scalar(out=cacc, in0=v2, scalar1=w2, scalar2=None, op0=ALU.mult),
                 reads=[v2, w2], writes=[cacc])
            yield
            T.op("dve", lambda e: e.scalar_tensor_tensor(out=cacc, in0=v1, scalar=w1, in1=cacc, op0=ALU.mult,
                                                         op1=ALU.add), reads=[v1, w1, cacc], writes=[cacc])
            yield
            T.op("dve", lambda e: e.scalar_tensor_tensor(out=cacc, in0=v0, scalar=w0, in1=cacc, op0=ALU.mult,
                                                         op1=ALU.add), reads=[v0, w0, cacc], writes=[cacc])
            yield
            for m in range(4):
                p = proj_chunk(w_b, j * 128, m)
                d_ = y_aT[:, j, m * 512:(m + 1) * 512]
                s_ = cacc[:, m * 512:(m + 1) * 512]
                T.op("dve", lambda e: e.tensor_tensor(out=d_, in0=p, in1=s_, op=ALU.mult), reads=[p, s_], writes=[d_])
                yield

    AR.release("maskT0")
    y_aT = AR.alloc("y_aT", [4, S], BF16)
    run_interleaved(attn_chunk(3), conv_phase(), 1, 1, warm=10)
    for n_ in (["maskT1", "qT", "kT2", "Vaug00", "Vaug01", "Vaug10", "Vaug11"]
               + ["ptile%d" % i for i in range(12)] + ["rc%d" % i for i in range(2)]):
        AR.release(n_)
    AR.release("w_c"); AR.release("w_u"); AR.release("w_b")
    for n_ in ("c_sb", "vpad", "cacc"):
        AR.rfree(n_)
    if debug:
        dump("y_bT", y_bT, [128, 4, S], BF16)

    stop("CD")
    w_gas = [None, None]
    w_gbs = [None, None]
    w_gas[0] = load_w_cols("w_ga0", C_GA, 512)
    w_gbs[0] = load_w_cols("w_gb0", C_GB, 512)
    w_pa = AR.alloc("w_pa", [4, D], BF16)
    w_pb = AR.alloc("w_pb", [4, D], BF16)
    T.dma("pool", w_pa, w_br_d[0].rearrange("(k p) n -> p k n", p=128), "w_pa", writes=[w_pa])
    T.dma("pool", w_pb, w_br_d[1].rearrange("(k p) n -> p k n", p=128), "w_pb", writes=[w_pb])
    if debug:
        dump("y_aT", y_aT, [128, 4, S], BF16)

    stop("E")
    mT = AR.alloc("mT", [NCH, S], BF16)
    w_gas[1] = load_w_cols("w_ga1", C_GA + 512, 512)
    w_gbs[1] = load_w_cols("w_gb1", C_GB + 512, 512)
    w_o = AR.alloc("w_o", [NCH, D], BF16)
    w_o_v = w_o_d.rearrange("(c p) n -> p c n", p=128)
    for q in range(2):
        T.dma("pool", w_o[:, 4 * q:4 * q + 4, :], w_o_v[:, 4 * q:4 * q + 4, :], "w_o%d" % q, writes=[w_o[:, 4 * q:4 * q + 4, :]])
    for half in range(2):
        w_ga = w_gas[half]
        w_gb = w_gbs[half]
        for dq in range(4):
            dc = half * 4 + dq
            for m in range(4):
                tsl = slice(m * 512, (m + 1) * 512)
                sa = AR.ralloc("sa", [512], F32, 3)
                sb = AR.ralloc("sb", [512], F32, 3)
                for (wg, sg) in ((w_ga, sa), (w_gb, sb)):
                    b = PS.get()
                    p = PS.f32(b)
                    for c in range(NCH):
                        l = wg[:, c, dq * 128:(dq + 1) * 128]
                        r = xT[:, c, tsl]
                        T.op("pe", lambda e: e.matmul(out=p, lhsT=l, rhs=r, start=(c == 0), stop=(c == NCH - 1)),
                             reads=[l, r], writes=[p], inc=(c == NCH - 1))
                    T.op("act", lambda e: e.activation(out=sg, in_=p, func=AF.Sigmoid), reads=[p], writes=[sg])
                t1 = AR.ralloc("t1", [512], F32, 3)
                t2 = AR.ralloc("t2", [512], F32, 3)
                for (wp, yT, sg, tt_) in ((w_pa, y_aT, sa, t1), (w_pb, y_bT, sb, t2)):
                    b = PS.get()
                    p = PS.f32(b)
                    for k in range(4):
                        l = wp[:, k, dc * 128:(dc + 1) * 128]
                        r = yT[:, k, tsl]
                        T.op("pe", lambda e: e.matmul(out=p, lhsT=l, rhs=r, start=(k == 0), stop=(k == 3)),
                             reads=[l, r], writes=[p], inc=(k == 3))
                    T.op("dve", lambda e: e.tensor_tensor(out=tt_, in0=p, in1=sg, op=ALU.mult),
                         reads=[p, sg], writes=[tt_])
                md = mT[:, dc, tsl]
                T.op("pool", lambda e: e.tensor_tensor(out=md, in0=t1, in1=t2, op=ALU.add), reads=[t1, t2], writes=[md])
                for n_ in ("sa", "sb", "t1", "t2"):
                    AR.release(n_)
        AR.release("w_ga%d" % half); AR.release("w_gb%d" % half)
    for n_ in ("xT", "y_aT", "y_bT", "w_pa", "w_pb"):
        AR.release(n_)
    for n_ in ("sa", "sb", "t1", "t2"):
        AR.rfree(n_)
    if debug:
        dump("mT", mT, [128, NCH, S], BF16)

    stop("F1")
    hT = AR.alloc("hT", [NCH, S], BF16)
    lng = AR.alloc("lng", [D], F32)
    lnb = AR.alloc("lnb", [D], F32)
    T.dma("sp", lng, ln1g_d, "lnp", writes=[lng])
    T.dma("sp", lnb, ln1b_d, "lnp", writes=[lnb], chain=True)
    w_dn_q = [AR.alloc("w_dn%d" % qq, [8, D], BF16) for qq in range(4)]
    w_dn_v = w_down_d.rearrange("(f p) n -> p f n", p=128)
    for q in range(8):
        dq_ = w_dn_q[q // 2][:, 4 * (q % 2):4 * (q % 2) + 4, :]
        T.dma("pool", dq_, w_dn_v[:, 4 * q:4 * q + 4, :], "w_dn%d" % q, writes=[dq_])

    def layernorm_tile(r, g_t, b_t, out_t, tag):
        st = AR.ralloc("lst" + tag, [16], F32, 2)
        mv = AR.ralloc("lmv" + tag, [8], F32, 2)
        for q in range(2):
            rq = r[:, q * 512:(q + 1) * 512]
            sq = st[:, q * 6:(q + 1) * 6]
            T.op("dve", lambda e: e.bn_stats(out=sq, in_=rq), reads=[rq], writes=[sq])
        s12 = st[:, 0:12]
        T.op("dve", lambda e: e.bn_aggr(out=mv[:, 0:2], in_=s12), reads=[s12], writes=[mv[:, 0:2]])
        rstd_from_var(mv)
        nmr = mv[:, 4:5]
        T.op("dve", lambda e: e.scalar_tensor_tensor(out=nmr, in0=mv[:, 0:1], scalar=-1.0, in1=mv[:, 2:3],
                                                     op0=ALU.mult, op1=ALU.mult), reads=[mv[:, 0:3]], writes=[nmr])
        T.op("act", lambda e: e.activation(out=out_t, in_=r, func=AF.Identity, bias=nmr, scale=mv[:, 2:3]),
             reads=[r, mv[:, 2:5]], writes=[out_t])
        T.op("dve", lambda e: e.tensor_tensor(out=out_t, in0=out_t, in1=g_t, op=ALU.mult), reads=[out_t, g_t],
             writes=[out_t])
        T.op("dve", lambda e: e.tensor_tensor(out=out_t, in0=out_t, in1=b_t, op=ALU.add), reads=[out_t, b_t],
             writes=[out_t])
        AR.release("lst" + tag); AR.release("lmv" + tag)

    g1c = AR.alloc("g1c", [8], F32)
    b1c = AR.alloc("b1c", [8], F32)
    T.dma("sp", g1c, ln1gc_d, "lnc", writes=[g1c])
    T.dma("sp", b1c, ln1bc_d, "lnc", writes=[b1c], chain=True)

    def emit_hT(nb_, tt_):
        b_ = PS.get()
        pb_ = PS.bf16(b_)
        for c in range(NCH):
            o = pb_[:, c * 128:(c + 1) * 128]
            i_ = nb_[:, c * 128:(c + 1) * 128]
            T.op("pe", lambda e: e.transpose(out=o, in_=i_, identity=ident), reads=[i_, ident], writes=[o],
                 inc=(c == NCH - 1))
        for c in range(NCH):
            src = pb_[:, c * 128:(c + 1) * 128]
            dst = hT[:, c, tt_ * 128:(tt_ + 1) * 128]
            gc = g1c[:, c:c + 1]
            bc = b1c[:, c:c + 1]
            T.op("act", lambda e: e.activation(out=dst, in_=src, func=AF.Identity, bias=bc, scale=gc),
                 reads=[src, gc, bc], writes=[dst])

    pend_nb = []
    xq = []

    def issue_xload(t_):
        xt_ = AR.ralloc("xr", [D], F32, 4)
        T.dma("sp", xt_, x_d[t_ * 128:(t_ + 1) * 128, :], "xr%d" % (t_ % 4), writes=[xt_])
        xq.append(xt_)
    issue_xload(0)
    issue_xload(1)
    for tt in range(NT):
        if tt + 2 < NT:
            issue_xload(tt + 2)
        xt = xq.pop(0)
        r = AR.ralloc("r1", [D], F32, 2)
        for half in range(2):
            b = PS.get()
            p = PS.f32(b)
            for dc in range(NCH):
                l = mT[:, dc, tt * 128:(tt + 1) * 128]
                rr = w_o[:, dc, half * 512:(half + 1) * 512]
                T.op("pe", lambda e: e.matmul(out=p, lhsT=l, rhs=rr, start=(dc == 0), stop=(dc == NCH - 1)),
                     reads=[l, rr], writes=[p], inc=(dc == NCH - 1))
            xh = xt[:, half * 512:(half + 1) * 512]
            rh = r[:, half * 512:(half + 1) * 512]
            T.op("dve", lambda e: e.scalar_tensor_tensor(out=rh, in0=xh, scalar=ALPHA, in1=p, op0=ALU.mult, op1=ALU.add),
                 reads=[xh, p], writes=[rh])
        st = AR.ralloc("lst1", [16], F32, 2)
        mv = AR.ralloc("lmv1", [8], F32, 2)
        for q in range(2):
            rq = r[:, q * 512:(q + 1) * 512]
            sq = st[:, q * 6:(q + 1) * 6]
            T.op("dve", lambda e: e.bn_stats(out=sq, in_=rq), reads=[rq], writes=[sq])
        s12 = st[:, 0:12]
        T.op("dve", lambda e: e.bn_aggr(out=mv[:, 0:2], in_=s12), reads=[s12], writes=[mv[:, 0:2]])
        rstd_from_var(mv)
        nmr = mv[:, 4:5]
        T.op("dve", lambda e: e.scalar_tensor_tensor(out=nmr, in0=mv[:, 0:1], scalar=-1.0, in1=mv[:, 2:3],
                                                     op0=ALU.mult, op1=ALU.mult), reads=[mv[:, 0:3]], writes=[nmr])
        nn = AR.ralloc("nn", [D], F32, 2)
        T.op("act", lambda e: e.activation(out=nn, in_=r, func=AF.Identity, bias=nmr, scale=mv[:, 2:3]),
             reads=[r, mv[:, 2:5]], writes=[nn])
        nb_t = AR.ralloc("nbt", [D], BF16, 4)
        T.op("dve", lambda e: e.tensor_copy(out=nb_t, in_=nn), reads=[nn], writes=[nb_t])
        hh = AR.ralloc("hh", [D], F32, 2)
        T.op("dve", lambda e: e.tensor_tensor(out=hh, in0=nn, in1=lng, op=ALU.mult), reads=[nn, lng], writes=[hh])
        T.op("dve", lambda e: e.tensor_tensor(out=hh, in0=hh, in1=lnb, op=ALU.add), reads=[hh, lnb], writes=[hh])
        T.dma("sp", h_scr[tt * 128:(tt + 1) * 128, :], hh, "hst%d" % (tt % 2), reads=[hh], writes=[("h", tt)])
        pend_nb.append((nb_t, tt))
        if len(pend_nb) > 2:
            emit_hT(*pend_nb.pop(0))
    AR.release("mT"); AR.release("w_o")
    AR.rfree("xr"); AR.rfree("r1")
    upT = AR.alloc("upT", [NF, 512], BF16)
    w_up_v = w_up_d.rearrange("(c p) n -> p c n", p=128)
    NSLOT = 3
    slots = [AR.alloc("wup%d" % i, [NCH, 512], BF16) for i in range(NSLOT)]

    def issue_up_load(idx):
        fq = idx % 8
        sl = slots[idx % NSLOT]
        T.dma("pool", sl, w_up_v[:, :, fq * 512:(fq + 1) * 512], "wup%d" % (idx % NSLOT), writes=[sl])

    total_loads = 4 * 8
    for idx in range(min(NSLOT, total_loads)):
        issue_up_load(idx)
    nxt_box = [NSLOT]

    def up_gen(G):
        tsl = slice(G * 512, (G + 1) * 512)
        for fq in range(8):
            idx = G * 8 + fq
            sl = slots[idx % NSLOT]
            for f4 in range(4):
                f = fq * 4 + f4
                b = PS.get()
                p = PS.f32(b)
                for c in range(NCH):
                    l = sl[:, c, f4 * 128:(f4 + 1) * 128]
                    r = hT[:, c, tsl]
                    T.op("pe", lambda e: e.matmul(out=p, lhsT=l, rhs=r, start=(c == 0), stop=(c == NCH - 1)),
                         reads=[l, r], writes=[p], inc=(c == NCH - 1))
                rt = AR.ralloc("relu_t", [512], BF16, 3)
                T.op("act", lambda e: e.activation(out=rt, in_=p, func=AF.Relu), reads=[p], writes=[rt])
                ud = upT[:, f, :]
                T.op("dve", lambda e: e.tensor_tensor(out=ud, in0=rt, in1=rt, op=ALU.mult), reads=[rt], writes=[ud])
                yield
            if nxt_box[0] < total_loads:
                issue_up_load(nxt_box[0])
                nxt_box[0] += 1

    up0 = up_gen(0)
    while pend_nb:
        for _ in range(6):
            next(up0)
        emit_hT(*pend_nb.pop(0))
    AR.release("g1c"); AR.release("b1c")
    for n_ in ("hh", "nn", "nbt", "lst1", "lmv1"):
        AR.rfree(n_)

    stop("F2")
    T.dma("sp", lng, ln2g_d, "lnp", writes=[lng])
    T.dma("sp", lnb, ln2b_d, "lnp", writes=[lnb], chain=True)
    for G in range(4):
        for _ in (up0 if G == 0 else up_gen(G)):
            pass
        hts = []
        for tq in range(4):
            tt = G * 4 + tq
            ht_ = AR.ralloc("hr", [D], F32, 4)
            T.dma("sp", ht_, h_scr[tt * 128:(tt + 1) * 128, :], "hr%d" % tq, reads=[("h", tt)], writes=[ht_])
            hts.append(ht_)
        for tq in range(4):
            tt = G * 4 + tq
            ht = hts[tq]
            r = AR.ralloc("r2", [D], F32, 2)
            for half in range(2):
                b = PS.get()
                p = PS.f32(b)
                for f in range(NF):
                    l = upT[:, f, tq * 128:(tq + 1) * 128]
                    rr = w_dn_q[f // 8][:, f % 8, half * 512:(half + 1) * 512]
                    T.op("pe", lambda e: e.matmul(out=p, lhsT=l, rhs=rr, start=(f == 0), stop=(f == NF - 1)),
                         reads=[l, rr], writes=[p], inc=(f == NF - 1))
                hq = ht[:, half * 512:(half + 1) * 512]
                rh = r[:, half * 512:(half + 1) * 512]
                T.op("dve", lambda e: e.scalar_tensor_tensor(out=rh, in0=hq, scalar=ALPHA, in1=p, op0=ALU.mult,
                                                             op1=ALU.add), reads=[hq, p], writes=[rh])
            AR.release("hr")
            ot = AR.ralloc("ot", [D], F32, 2)
            layernorm_tile(r, lng, lnb, ot, "2")
            AR.release("r2")
            T.dma("sp", out_d[tt * 128:(tt + 1) * 128, :], ot, "ost%d" % (tt % 2), reads=[ot], writes=[("o", tt)])
            AR.release("ot")
    return


def _host_consts():
    ident = np.eye(128, dtype=np.float32).astype(ml_dtypes.bfloat16)
    tri = np.triu(np.ones((128, 128), dtype=np.float32)).T
    negtri = np.where(tri > 0, 0.0, NEG).astype(np.float32)
    k = np.arange(NIT)
    p2 = np.concatenate([2.0 ** -(k + 2.0), 2.0 * 2.0 ** -(k + 2.0)]).astype(np.float32)
    pow2 = np.ascontiguousarray(np.broadcast_to(p2[None, :], (128, 2 * NIT))).astype(np.float32)
    return ident, tri.astype(ml_dtypes.bfloat16), negtri, pow2


def _bcast(v, n=128):
    v = np.asarray(v, dtype=np.float32).reshape(1, -1)
    return np.ascontiguousarray(np.broadcast_to(v, (n, v.shape[1])))


def make_in_maps(inputs, cores):
    ident, tri, negtri, pow2 = _host_consts()
    cw = np.asarray(inputs["conv_w"], dtype=np.float32)[0]
    conv_w_t = np.ascontiguousarray(cw.reshape(3, 4, 128).transpose(2, 1, 0).reshape(128, 12))
    shared = {
        "w_in": np.ascontiguousarray(inputs["w_in"][0], dtype=np.float32),
        "conv_w_t": conv_w_t,
        "ikg": _bcast(inputs["idx_k_norm_g"][0]),
        "ikb": _bcast(inputs["idx_k_norm_b"][0]),
        "w_branch": np.ascontiguousarray(inputs["w_branch"][0], dtype=np.float32),
        "w_o": np.ascontiguousarray(inputs["w_o"][0], dtype=np.float32),
        "ln1g": _bcast(inputs["ln1_g"][0]),
        "ln1b": _bcast(inputs["ln1_b"][0]),
        "ln1gc": np.ascontiguousarray(np.asarray(inputs["ln1_g"][0], dtype=np.float32).reshape(8, 128).T),
        "ln1bc": np.ascontiguousarray(np.asarray(inputs["ln1_b"][0], dtype=np.float32).reshape(8, 128).T),
        "w_up": np.ascontiguousarray(inputs["w_up"][0], dtype=np.float32),
        "w_down": np.ascontiguousarray(inputs["w_down"][0], dtype=np.float32),
        "ln2g": _bcast(inputs["ln2_g"][0]),
        "ln2b": _bcast(inputs["ln2_b"][0]),
        "ident": ident, "tri": tri, "negtri": negtri, "pow2": pow2,
    }
    x = np.asarray(inputs["x"], dtype=np.float32)
    maps = []
    for b in cores:
        m = dict(shared)
        m["x"] = np.ascontiguousarray(x[b])
        maps.append(m)
    return maps


def kernel(**inputs):
    nc, info = build_program(debug=False)
    in_maps = make_in_maps(inputs, list(range(8)))
    res = run_bass_kernel_spmd(nc, in_maps, core_ids=list(range(8)))
    out = np.stack([np.asarray(r["out"], dtype=np.float32) for r in res.results], axis=0)
    return out
```

```python
import os
import numpy as np
import ml_dtypes
import concourse.bass as bass
import concourse.mybir as mybir
from concourse.bass_utils import run_bass_kernel_spmd

F32 = mybir.dt.float32
BF16 = mybir.dt.bfloat16
ALU = mybir.AluOpType
AF = mybir.ActivationFunctionType
AX = mybir.AxisListType

S = 2048
D = 1024
NT = S // 128
NCH = D // 128
D_IN = 4936
DFF = 4096
NF = DFF // 128
ALPHA = 2.0 ** 0.25
LN_EPS = 1e-5
TOPK = 256
NIT = 10
NEG = -1.0e30
MASKNEG = -30000.0
DVE_RELU_HEADS = (1, 3, 5, 7)

C_B, C_C, C_U, C_Q, C_K, C_V, C_QI, C_KI, C_WI, C_GA, C_GB = 0, 512, 1024, 1536, 2048, 2176, 2304, 2816, 2880, 2888, 3912

SEM_LIMIT = 30000


def _esize(dt):
    return {F32: 4, BF16: 2}.get(dt, 4)


class _Eng:
    def __init__(self, name, eng, sem):
        self.name = name
        self.eng = eng
        self.sem = sem
        self.count = 0
        self.pending = False
        self.seen = {}


class Tracker:
    BLK = 256

    def __init__(self, nc):
        self.nc = nc
        self.nsem = 0
        self.engs = {}
        for name, eng in (("pe", nc.tensor), ("act", nc.scalar), ("dve", nc.vector),
                          ("pool", nc.gpsimd), ("sp", nc.sync)):
            self.engs[name] = _Eng(name, eng, self._newsem(name))
        self.blocks = {}
        self.dma_sems = {}
        self.dma_sems_by_id = {}
        self.nwaits = 0
        self.ninst = 0

    def _newsem(self, name):
        self.nsem += 1
        return self.nc.alloc_semaphore("s_%s_%d" % (name, self.nsem))

    def _keys(self, ap):
        t = ap.tensor
        space = "P" if "PSum" in type(t).__name__ else "S"
        pairs = ap.ap
        es = _esize(ap.dtype)
        pstride = pairs[0][0]
        npart = pairs[0][1]
        p0 = ap.offset // pstride if pstride else 0
        base = (ap.offset % pstride) * es if pstride else ap.offset * es
        halves = set()
        if p0 < 64:
            halves.add(0)
        if p0 + npart > 64:
            halves.add(1)
        free = pairs[1:]
        ranges = []
        if not free:
            ranges.append((base, base + es))
        else:
            outer = free[:-1]
            lstep, lcnt = free[-1]
            span = ((lcnt - 1) * abs(lstep) + 1) * es

            def rec(i, off):
                if i == len(outer):
                    ranges.append((off, off + span))
                    return
                st, cn = outer[i]
                for k in range(cn):
                    rec(i + 1, off + k * st * es)
            rec(0, base)
        keys = set()
        B = 2048 if space == "P" else self.BLK
        for lo, hi in ranges:
            for b in range(lo // B, (hi - 1) // B + 1):
                for h in halves:
                    keys.add((space, h, b))
        return keys

    def _allkeys(self, items):
        keys = set()
        for it in items:
            if it is None:
                continue
            if isinstance(it, (str, tuple)):
                keys.add(("D", it))
            else:
                keys |= self._keys(it)
        return keys

    def _deps(self, ename, rkeys, wkeys, is_dma):
        deps = {}

        def add(tk):
            sem, val, own = tk
            if (not is_dma) and own == ename and ename == "pe":
                return
            k = id(sem)
            if k not in deps or deps[k][1] < val:
                deps[k] = (sem, val, own)

        def add_waw(tk):
            sem, val, own = tk
            if (not is_dma) and own == ename and ename == "pe":
                return
            k = id(sem)
            if k not in deps or deps[k][1] < val:
                deps[k] = (sem, val, own)

        def add_raw(tk):
            sem, val, own = tk
            if (not is_dma) and own == ename and ename == "pe":
                return
            k = id(sem)
            if k not in deps or deps[k][1] < val:
                deps[k] = (sem, val, own)

        for key in rkeys:
            st = self.blocks.get(key)
            if st and st["w"] is not None:
                add_raw(st["w"])
            if st and key[0] == "P":
                for tk in st["r"].values():
                    add(tk)
        for key in wkeys:
            st = self.blocks.get(key)
            if st:
                if st["w"] is not None:
                    add_waw(st["w"])
                for tk in st["r"].values():
                    add(tk)
        return deps

    def _wait(self, E, deps):
        for k, (sem, val, own) in deps.items():
            if E.seen.get(k, 0) >= val:
                continue
            if own in self.engs:
                P = self.engs[own]
                if P.sem is sem:
                    assert val <= P.count, "wait on future inc (%s waits %s)" % (E.name, own)
            elif own.startswith("dma:"):
                val = self.dma_sems_by_id[k][1]
            E.eng.wait_ge(sem, val)
            E.seen[k] = val
            self.nwaits += 1

    def _update(self, rkeys, wkeys, tk):
        for key in rkeys:
            st = self.blocks.setdefault(key, {"w": None, "r": {}})
            k = id(tk[0])
            old = st["r"].get(k)
            if old is None or old[1] < tk[1]:
                st["r"][k] = tk
        for key in wkeys:
            self.blocks[key] = {"w": tk, "r": {}}

    def op(self, ename, fn, reads=(), writes=(), inc=True):
        E = self.engs[ename]
        rkeys = self._allkeys(reads)
        wkeys = self._allkeys(writes)
        deps = self._deps(ename, rkeys, wkeys, False)
        self._wait(E, deps)
        inst = fn(E.eng)
        self.ninst += 1
        if inc:
            E.count += 1
            inst.then_inc(E.sem, 1)
            E.pending = False
            tk = (E.sem, E.count, ename)
        else:
            E.pending = True
            tk = (E.sem, E.count + 1, ename)
        self._update(rkeys, wkeys, tk)
        if inc and E.count >= SEM_LIMIT:
            E.sem = self._newsem(ename)
            E.count = 0
        return inst

    def dma(self, qname, out, in_, skey, reads=(), writes=(), chain=False):
        E = self.engs[qname]
        rkeys = self._allkeys(list(reads))
        wkeys = self._allkeys(list(writes))
        deps = self._deps(qname, rkeys, wkeys, True)
        self._wait(E, deps)
        if skey not in self.dma_sems:
            self.dma_sems[skey] = [self._newsem("dma"), 0]
            self.dma_sems_by_id[id(self.dma_sems[skey][0])] = self.dma_sems[skey]
        rec = self.dma_sems[skey]
        if (not chain) and rec[1] > 0 and E.seen.get(id(rec[0]), 0) < rec[1]:
            E.eng.wait_ge(rec[0], rec[1])
            E.seen[id(rec[0])] = rec[1]
            self.nwaits += 1
        rec[1] += 16
        E.eng.dma_start(out=out, in_=in_).then_inc(rec[0], 16)
        self.ninst += 1
        tk = (rec[0], rec[1], "dma:" + str(skey))
        self._update(rkeys, wkeys, tk)

    def final_wait(self, qname, skey):
        rec = self.dma_sems[skey]
        self.engs[qname].eng.wait_ge(rec[0], rec[1])


class Arena:
    def __init__(self, nc, nbytes):
        self.nbytes = nbytes // 256 * 256
        self.t = nc.alloc_sbuf_tensor("arena", [128, self.nbytes // 2], BF16)
        self.free = [(0, self.nbytes)]
        self.live = {}
        self.rings = {}
        self.peak = 0

    def alloc(self, name, shape, dt):
        n = 1
        for s_ in shape:
            n *= s_
        nb = (n * _esize(dt) + 255) // 256 * 256
        for i, (lo, hi) in enumerate(self.free):
            if hi - lo >= nb:
                self.free[i] = (lo + nb, hi)
                if self.free[i][0] == self.free[i][1]:
                    self.free.pop(i)
                self.live[name] = (lo, nb)
                used = self.nbytes - sum(h - l for l, h in self.free)
                self.peak = max(self.peak, used)
                v = self.t[:, lo // 2:(lo + nb) // 2]
                if dt == F32:
                    v = v.bitcast(F32)
                v = v[:, 0:n]
                if len(shape) == 2:
                    v = v.rearrange("p (a b) -> p a b", a=shape[0])
                elif len(shape) == 3:
                    v = v.rearrange("p (a b c) -> p a b c", a=shape[0], b=shape[1])
                return v
        raise RuntimeError("arena OOM for %s (%d B); live=%s" % (name, nb, {k: v[1] for k, v in self.live.items()}))

    def ralloc(self, name, shape, dt, n=2):
        if name not in self.rings:
            self.rings[name] = [[self.alloc("%s#%d" % (name, i), shape, dt) for i in range(n)], 0]
        r = self.rings[name]
        r[1] += 1
        return r[0][r[1] % len(r[0])]

    def rfree(self, name):
        for i in range(len(self.rings[name][0])):
            self.release("%s#%d" % (name, i))
        del self.rings[name]

    def release(self, name):
        if name in self.rings:
            return
        lo, nb = self.live.pop(name)
        self.free.append((lo, lo + nb))
        self.free.sort()
        merged = []
        for l, h in self.free:
            if merged and merged[-1][1] == l:
                merged[-1] = (merged[-1][0], h)
            else:
                merged.append((l, h))
        self.free = merged


class PsumPool:
    def __init__(self, nc):
        self.t = nc.alloc_psum_tensor("psum", [128, 4096], F32)
        self.order = list(range(8))
        self.held = set()

    def get(self, hold=False):
        for b in self.order:
            if b not in self.held:
                self.order.remove(b)
                self.order.append(b)
                if hold:
                    self.held.add(b)
                return b
        raise RuntimeError("no free PSUM bank")

    def release(self, b):
        self.held.discard(b)

    def f32(self, b):
        return self.t[:, b * 512:(b + 1) * 512]

    def bf16(self, b):
        return self.t[:, b * 512:(b + 1) * 512].bitcast(BF16)


class _Stop(Exception):
    pass


def build_program(debug=False, stop_after=None):
    nc = bass.Bass("TRN2", target_bir_lowering=False)
    T = Tracker(nc)
    dbg = {}
    try:
        _build_body(nc, T, dbg, debug, stop_after)
    except _Stop:
        pass
    for skey in list(T.dma_sems.keys()):
        if skey.startswith("ost") or skey == "dbg":
            T.final_wait("sp", skey)
    info = {"ninst": T.ninst, "nwaits": T.nwaits, "nsem": T.nsem,
            "counts": {k: v.count for k, v in T.engs.items()}}
    return nc, info


def _build_body(nc, T, dbg, debug, stop_after):
    def stop(tag):
        if stop_after == tag:
            raise _Stop()

    def din(name, shape, dt=F32):
        return nc.dram_tensor(name, list(shape), dt, kind="ExternalInput").ap()

    x_d = din("x", [S, D])
    w_in_d = din("w_in", [D, D_IN])
    convw_d = din("conv_w_t", [128, 12])
    ikg_d = din("ikg", [128, 64])
    ikb_d = din("ikb", [128, 64])
    w_br_d = din("w_branch", [2, 512, D])
    w_o_d = din("w_o", [D, D])
    ln1g_d = din("ln1g", [128, D])
    ln1b_d = din("ln1b", [128, D])
    ln1gc_d = din("ln1gc", [128, 8])
    ln1bc_d = din("ln1bc", [128, 8])
    w_up_d = din("w_up", [D, DFF])
    w_down_d = din("w_down", [DFF, D])
    ln2g_d = din("ln2g", [128, D])
    ln2b_d = din("ln2b", [128, D])
    ident_d = din("ident", [128, 128], BF16)
    tri_d = din("tri", [128, 128], BF16)
    negtri_d = din("negtri", [128, 128])
    pow2_d = din("pow2", [128, 2 * NIT])
    out_d = nc.dram_tensor("out", [S, D], F32, kind="ExternalOutput").ap()
    h_scr = nc.dram_tensor("h_scr", [S, D], F32, kind="ExternalOutput" if debug else "Internal").ap()

    def dbg_out(name, shape, dt=F32):
        if not debug:
            return None
        dbg[name] = nc.dram_tensor("dbg_" + name, list(shape), dt, kind="ExternalOutput").ap()
        return dbg[name]

    AR = Arena(nc, int(nc.sbuf_bytes_remaining) - 1024)
    PS = PsumPool(nc)

    w_in_v = w_in_d.rearrange("(c p) n -> p c n", p=128)

    def dump(name, ap_sb, shape, dt=F32):
        d = dbg_out(name, shape, dt)
        if d is None:
            return
        T.dma("sp", d, ap_sb, "dbg", reads=[ap_sb], writes=["dbg_" + name])

    def rstd_from_var(mv):
        T.op("dve", lambda e: e.tensor_scalar(out=mv[:, 3:4], in0=mv[:, 1:2], scalar1=LN_EPS, scalar2=None,
                                              op0=ALU.add), reads=[mv[:, 1:2]], writes=[mv[:, 3:4]])
        T.op("act", lambda e: e.activation(out=mv[:, 3:4], in_=mv[:, 3:4], func=AF.Sqrt), reads=[mv[:, 3:4]],
             writes=[mv[:, 3:4]])
        T.op("dve", lambda e: e.reciprocal(out=mv[:, 2:3], in_=mv[:, 3:4]), reads=[mv[:, 3:4]], writes=[mv[:, 2:3]])

    ident = AR.alloc("ident", [128], BF16)
    tri = AR.alloc("tri", [128], BF16)
    negtri = AR.alloc("negtri", [128], F32)
    pow2 = AR.alloc("pow2", [2 * NIT], F32)
    convw = AR.alloc("convw", [12], F32)
    ikg = AR.alloc("ikg", [64], F32)
    ikb = AR.alloc("ikb", [64], F32)
    for dst, src in ((ident, ident_d), (tri, tri_d), (negtri, negtri_d), (pow2, pow2_d),
                     (convw, convw_d), (ikg, ikg_d), (ikb, ikb_d)):
        T.dma("sp", dst, src, "const", writes=[dst], chain=True)

    def load_w_cols(name, col0, ncols, dup64=False):
        w = AR.alloc(name, [NCH, ncols], BF16)
        T.dma("pool", w, w_in_v[:, :, col0:col0 + ncols], "w_" + name, writes=[w])
        return w

    xT = AR.alloc("xT", [NCH, S], BF16)
    xts = [AR.alloc("xt%d" % i, [D], F32) for i in range(2)]
    xbs = [AR.alloc("xb%d" % i, [D], BF16) for i in range(2)]
    w_kw = load_w_cols("w_kw", C_KI, 72)
    w_qi = load_w_cols("w_qi", C_QI, 512)
    for tt in range(NT):
        xt = xts[tt % 2]
        xb = xbs[tt % 2]
        T.dma("sp", xt, x_d[tt * 128:(tt + 1) * 128, :], "xld%d" % (tt % 2), writes=[xt])
        T.op("act", lambda e: e.copy(out=xb, in_=xt), reads=[xt], writes=[xb])
        b = PS.get()
        pb = PS.bf16(b)
        for c in range(NCH):
            o = pb[:, c * 128:(c + 1) * 128]
            i_ = xb[:, c * 128:(c + 1) * 128]
            T.op("pe", lambda e: e.transpose(out=o, in_=i_, identity=ident), reads=[i_, ident], writes=[o],
                 inc=(c == NCH - 1))
        dst = xT[:, :, tt * 128:(tt + 1) * 128]
        src = pb.rearrange("p (c t) -> p c t", c=NCH)
        T.op("dve", lambda e: e.tensor_copy(out=dst, in_=src), reads=[pb], writes=[dst])
    AR.release("xt0"); AR.release("xt1"); AR.release("xb0"); AR.release("xb1")
    if debug:
        dump("xT", xT, [128, NCH, S], BF16)
    stop("A")

    def proj_feat(w, wc0, dst_fn, evac):
        for m in range(4):
            b = PS.get()
            p = PS.f32(b)
            for c in range(NCH):
                l = w[:, c, wc0:wc0 + 128]
                r = xT[:, c, m * 512:(m + 1) * 512]
                T.op("pe", lambda e: e.matmul(out=p, lhsT=l, rhs=r, start=(c == 0), stop=(c == NCH - 1)),
                     reads=[l, r], writes=[p], inc=(c == NCH - 1))
            evac(m, p)

    cp_toggle = [0]

    def evac_copy(dst, src):
        cp_toggle[0] ^= 1
        if cp_toggle[0]:
            T.op("act", lambda e: e.copy(out=dst, in_=src), reads=[src], writes=[dst])
        else:
            T.op("dve", lambda e: e.tensor_copy(out=dst, in_=src), reads=[src], writes=[dst])

    w_q = load_w_cols("w_q", C_Q, 512)
    w_k = [AR.alloc("w_k%d" % g, [NCH, 128], BF16) for g in range(2)]
    for g in range(2):
        for half in range(2):
            dstw = w_k[g][:, :, half * 64:(half + 1) * 64]
            T.dma("pool", dstw, w_in_v[:, :, C_K + g * 64:C_K + (g + 1) * 64], "w_k%d%d" % (g, half), writes=[dstw])
    w_v = load_w_cols("w_v", C_V, 128)
    kiT2 = AR.alloc("kiT2", [S], BF16)
    wi = AR.alloc("wi", [NT, 8], F32)
    qiT = AR.alloc("qiT", [4, S], BF16)
    qT = AR.alloc("qT", [4, S], BF16)
    kT2 = AR.alloc("kT2", [2, S], BF16)

    def ki_gen():
        for tt in range(NT):
            b = PS.get()
            p = PS.f32(b)[:, 0:72]
            for c in range(NCH):
                l = xT[:, c, tt * 128:(tt + 1) * 128]
                r = w_kw[:, c, :]
                T.op("pe", lambda e: e.matmul(out=p, lhsT=l, rhs=r, start=(c == 0), stop=(c == NCH - 1)),
                     reads=[l, r], writes=[p], inc=(c == NCH - 1))
            st = AR.ralloc("kst", [8], F32, 3)
            mv = AR.ralloc("kmv", [4], F32, 3)
            kn = AR.ralloc("kn", [64], F32, 3)
            kn2 = AR.ralloc("kn2", [128], BF16, 3)
            pk = p[:, 0:64]
            T.op("dve", lambda e: e.bn_stats(out=st[:, 0:6], in_=pk), reads=[pk], writes=[st])
            T.op("dve", lambda e: e.bn_aggr(out=mv[:, 0:2], in_=st[:, 0:6]), reads=[st], writes=[mv[:, 0:2]])
            rstd_from_var(mv)
            T.op("dve", lambda e: e.tensor_scalar(out=kn, in0=pk, scalar1=mv[:, 0:1], scalar2=mv[:, 2:3],
                                                  op0=ALU.subtract, op1=ALU.mult), reads=[pk, mv], writes=[kn])
            wsrc = p[:, 64:72]
            wdst = wi[:, tt, :]
            T.op("dve", lambda e: e.tensor_copy(out=wdst, in_=wsrc), reads=[wsrc], writes=[wdst])
            T.op("dve", lambda e: e.tensor_tensor(out=kn, in0=kn, in1=ikg, op=ALU.mult), reads=[kn, ikg], writes=[kn])
            T.op("dve", lambda e: e.tensor_tensor(out=kn2[:, 0:64], in0=kn, in1=ikb, op=ALU.add),
                 reads=[kn, ikb], writes=[kn2[:, 0:64]])
            T.op("dve", lambda e: e.tensor_copy(out=kn2[:, 64:128], in_=kn2[:, 0:64]), reads=[kn2[:, 0:64]],
                 writes=[kn2[:, 64:128]])
            yield
            b2 = PS.get()
            pb = PS.bf16(b2)[:, 0:128]
            T.op("pe", lambda e: e.transpose(out=pb, in_=kn2, identity=ident), reads=[kn2, ident], writes=[pb])
            kd = kiT2[:, tt * 128:(tt + 1) * 128]
            T.op("act", lambda e: e.copy(out=kd, in_=pb), reads=[pb], writes=[kd])
            yield

    def projB_gen():
        for (w, wc0, dst) in ([(w_qi, j * 128, qiT[:, j, :]) for j in range(4)]
                              + [(w_q, j * 128, qT[:, j, :]) for j in range(4)]
                              + [(w_k[g], 0, kT2[:, g, :]) for g in range(2)]):
            for m in range(4):
                b = PS.get()
                p = PS.f32(b)
                for c in range(NCH):
                    l = w[:, c, wc0:wc0 + 128]
                    r = xT[:, c, m * 512:(m + 1) * 512]
                    T.op("pe", lambda e: e.matmul(out=p, lhsT=l, rhs=r, start=(c == 0), stop=(c == NCH - 1)),
                         reads=[l, r], writes=[p], inc=(c == NCH - 1))
                evac_copy(dst[:, m * 512:(m + 1) * 512], p)
                yield

    def run2(ga, gb, ra, rb):
        alive = [True, True]
        gens = [ga, gb]
        while any(alive):
            for q, reps in ((0, ra), (1, rb)):
                if not alive[q]:
                    continue
                for _ in range(reps):
                    try:
                        next(gens[q])
                    except StopIteration:
                        alive[q] = False
                        break
    run2(ki_gen(), projB_gen(), 1, 1)
    AR.release("w_kw"); AR.release("w_qi"); AR.release("w_q"); AR.release("w_k0"); AR.release("w_k1")
    for n_ in ("kst", "kmv", "kn", "kn2"):
        AR.rfree(n_)
    stop("B3")
    Vaug = [[AR.alloc("Vaug%d%d" % (g, e), [NT, 128], BF16) for e in range(2)] for g in range(2)]
    for g in range(2):
        for e_ in range(2):
            va = Vaug[g][e_]
            T.op("dve", lambda e: e.memset(va, 1.0), writes=[va])
    for tt in range(NT):
        b = PS.get()
        p = PS.f32(b)[:, 0:128]
        for c in range(NCH):
            l = xT[:, c, tt * 128:(tt + 1) * 128]
            r = w_v[:, c, :]
            T.op("pe", lambda e: e.matmul(out=p, lhsT=l, rhs=r, start=(c == 0), stop=(c == NCH - 1)),
                 reads=[l, r], writes=[p], inc=(c == NCH - 1))
        for g in range(2):
            src = p[:, g * 64:(g + 1) * 64]
            d0 = Vaug[g][0][:, tt, 0:64]
            d1 = Vaug[g][1][:, tt, 64:128]
            T.op("act", lambda e: e.copy(out=d0, in_=src), reads=[src], writes=[d0])
            T.op("dve", lambda e: e.tensor_copy(out=d1, in_=src), reads=[src], writes=[d1])
    AR.release("w_v")
    stop("B4")
    if debug:
        dump("qiT", qiT, [128, 4, S], BF16)
        dump("kiT2", kiT2, [128, S], BF16)
        dump("wi", wi, [128, NT, 8])
        dump("qT", qT, [128, 4, S], BF16)
        dump("kT2", kT2, [128, 2, S], BF16)
        dump("Vaug00", Vaug[0][0], [128, NT, 128], BF16)

    stop("B")
    xT_scr = nc.dram_tensor("xT_scr", [128, NCH, S], BF16, kind="Internal").ap()
    T.dma("sp", xT_scr, xT, "xTsp", reads=[xT], writes=["xT_scr"])
    AR.release("xT")

    y_bT = AR.alloc("y_bT", [4, S], BF16)
    maskTs = [AR.alloc("maskT%d" % i, [NT, 512], BF16) for i in range(2)]
    junks = [AR.alloc("junk%d" % i, [S], BF16) for i in range(2)]
    junkd = [AR.alloc("junkd%d" % i, [S], BF16) for i in range(1)]
    negtri_b = AR.alloc("negtri_b", [128], BF16)
    T.op("dve", lambda e: e.tensor_scalar(out=negtri_b, in0=negtri, scalar1=MASKNEG / NEG, scalar2=None, op0=ALU.mult),
         reads=[negtri], writes=[negtri_b])
    scores = [AR.alloc("score%d" % i, [S], F32) for i in range(4)]
    maskbs = [AR.alloc("maskb%d" % i, [S], BF16) for i in range(2)]
    rtmps = [AR.alloc("rtmp%d" % i, [512], BF16) for i in range(10)]
    dws = [AR.alloc("dw%d" % i, [8, 128], BF16) for i in range(4)]
    ptiles = [AR.alloc("ptile%d" % i, [512], BF16) for i in range(14)]
    rcs = [AR.alloc("rc%d" % i, [512], F32) for i in range(2)]
    ctr = {"rtmp": 0, "ptile": 0, "rc": 0, "maskb": 0, "junk": 0, "junkd": 0}

    def ring(lst, key):
        ctr[key] += 1
        return lst[ctr[key] % len(lst)]

    if debug:
        dbg_mask = dbg_out("mask", [NT, 128, S], BF16)
        dbg_score = dbg_out("score", [NT, 128, S], F32)

    def topk_chunk(m):
        maskT = maskTs[m % 2]
        tiles = list(range(4 * m, 4 * m + 4))
        sel = [i for i in tiles if i >= 2]
        nb = len(sel)
        col = {i: c for c, i in enumerate(sel)}
        for i in sel:
            N = 128 * (i + 1)
            c_ = col[i]
            score = scores[c_]
            for h in range(8):
                dwt = dws[c_][:, h, :]
                wsc = wi[:, i, h:h + 1]
                T.op("dve", lambda e: e.tensor_scalar(out=dwt, in0=ident, scalar1=wsc, scalar2=None, op0=ALU.mult),
                     reads=[ident, wsc], writes=[dwt])
            for kc in range((N + 511) // 512):
                k0 = kc * 512
                n = min(512, N - k0)
                sc = score[:, k0:k0 + n]
                sb_ = PS.get(hold=True)
                sacc = PS.f32(sb_)[:, 0:n]
                pendq = []

                def emit_acc(it):
                    tv_, h_ = it
                    dw_ = dws[c_][:, h_, :]
                    T.op("pe", lambda e: e.matmul(out=sacc, lhsT=dw_, rhs=tv_, start=(h_ == 0), stop=(h_ == 7)),
                         reads=[dw_, tv_], writes=[sacc], inc=(h_ == 7))
                for hp in range(4):
                    pair = []
                    for h in (2 * hp, 2 * hp + 1):
                        e2 = h % 2
                        b = PS.get()
                        p = PS.f32(b)[:, 0:n]
                        l = qiT[64 * e2:64 * e2 + 64, h // 2, i * 128:(i + 1) * 128]
                        r = kiT2[64 * e2:64 * e2 + 64, k0:k0 + n]
                        T.op("pe", lambda e: e.matmul(out=p, lhsT=l, rhs=r, start=True, stop=True),
                             reads=[l, r], writes=[p])
                        pair.append((h, p))
                    for h, p in pair:
                        tv = ring(rtmps, "rtmp")[:, 0:n]
                        if h in DVE_RELU_HEADS:
                            T.op("dve", lambda e: e.tensor_scalar(out=tv, in0=p, scalar1=0.0, scalar2=None, op0=ALU.max),
                                 reads=[p], writes=[tv])
                        else:
                            T.op("act", lambda e: e.activation(out=tv, in_=p, func=AF.Relu), reads=[p], writes=[tv])
                        pendq.append((tv, h))
                    while len(pendq) > 8:
                        emit_acc(pendq.pop(0))
                while pendq:
                    emit_acc(pendq.pop(0))
                if k0 + n == N:
                    nd = n - 128
                    if nd > 0:
                        T.op("dve", lambda e: e.tensor_copy(out=sc[:, 0:nd], in_=sacc[:, 0:nd]), reads=[sacc],
                             writes=[sc[:, 0:nd]])
                    T.op("dve", lambda e: e.tensor_tensor(out=sc[:, nd:n], in0=sacc[:, nd:n], in1=negtri, op=ALU.add),
                         reads=[sacc, negtri], writes=[sc[:, nd:n]])
                else:
                    T.op("dve", lambda e: e.tensor_copy(out=sc, in_=sacc), reads=[sacc], writes=[sc])
                PS.release(sb_)
                yield 'S'
        taus = {}
        if nb:
            half = (nb + 1) // 2
            groups = [g_ for g_ in (sel[:half], sel[half:]) if g_]
            sts = []
            for gi, grp in enumerate(groups):
                ng = len(grp)
                sm = AR.alloc("tk_small%d" % gi, [64 + 3 * NIT * 2], F32)
                st_ = {"hi": sm[:, 0:ng], "lo": sm[:, 2:2 + ng], "R": sm[:, 4:4 + ng], "nmid": sm[:, 6:6 + ng],
                       "tq": sm[:, 8:8 + ng], "tau": sm[:, 10:10 + ng], "npl": sm[:, 12:12 + ng],
                       "thr": sm[:, 14:14 + ng],
                       "Rk": sm[:, 64:64 + 2 * NIT].rearrange("p (k c) -> p k c", k=NIT),
                       "Rk2": sm[:, 64 + 2 * NIT:64 + 4 * NIT].rearrange("p (k c) -> p k c", k=NIT),
                       "cnt": sm[:, 64 + 4 * NIT:64 + 6 * NIT].rearrange("p (k c) -> p k c", k=NIT),
                       "mid": sm[:, 16:16 + ng], "grp": grp, "ng": ng, "gi": gi}
                sts.append(st_)
                for lc_, i in enumerate(grp):
                    c = col[i]
                    N = 128 * (i + 1)
                    sN = scores[c][:, 0:N]
                    sL = scores[c][:, 0:128 * i]
                    hc = st_["hi"][:, lc_:lc_ + 1]; lc = st_["lo"][:, lc_:lc_ + 1]; tc0 = st_["thr"][:, lc_:lc_ + 1]
                    T.op("dve", lambda e: e.memset(tc0, float(2 * TOPK - 2 - N) if gi == 0 else float(TOPK - 1)),
                         writes=[tc0])
                    T.op("dve", lambda e: e.tensor_reduce(out=hc, in_=sN, axis=AX.X, op=ALU.max), reads=[sN], writes=[hc])
                    T.op("dve", lambda e: e.tensor_reduce(out=lc, in_=sL, axis=AX.X, op=ALU.min), reads=[sL], writes=[lc])
                    yield 'S'
                hi = st_["hi"]; lo = st_["lo"]; R = st_["R"]; nmid = st_["nmid"]
                T.op("dve", lambda e: e.tensor_tensor(out=R, in0=hi, in1=lo, op=ALU.subtract), reads=[hi, lo], writes=[R])
                T.op("dve", lambda e: e.scalar_tensor_tensor(out=nmid, in0=R, scalar=-0.5, in1=lo, op0=ALU.mult,
                                                             op1=ALU.subtract), reads=[R, lo], writes=[nmid])
                mid_ = st_["mid"]
                T.op("dve", lambda e: e.tensor_scalar(out=mid_, in0=nmid, scalar1=-1.0, scalar2=None, op0=ALU.mult),
                     reads=[nmid], writes=[mid_])
                for lc_ in range(ng):
                    rc_ = R[:, lc_:lc_ + 1]
                    o1 = st_["Rk"][:, :, lc_]
                    o2 = st_["Rk2"][:, :, lc_]
                    T.op("dve", lambda e: e.tensor_scalar(out=o1, in0=pow2[:, 0:NIT], scalar1=rc_, scalar2=None,
                                                          op0=ALU.mult), reads=[pow2, rc_], writes=[o1])
                    T.op("dve", lambda e: e.tensor_scalar(out=o2, in0=pow2[:, NIT:2 * NIT], scalar1=rc_, scalar2=None,
                                                          op0=ALU.mult), reads=[pow2, rc_], writes=[o2])
            for k in range(NIT):
                for st_ in sts:
                    for lc_, i in enumerate(st_["grp"]):
                        c = col[i]
                        N = 128 * (i + 1)
                        sN = scores[c][:, 0:N]
                        jn = ring(junks, "junk")[:, 0:N]
                        ck = st_["cnt"][:, k, lc_:lc_ + 1]
                        if st_["gi"] == 0:
                            mc = st_["nmid"][:, lc_:lc_ + 1]
                            T.op("act", lambda e: e.activation(out=jn, in_=sN, func=AF.Sign, bias=mc, scale=1.0,
                                                               accum_out=ck), reads=[sN, mc], writes=[jn, ck])
                        else:
                            mc = st_["mid"][:, lc_:lc_ + 1]
                            jn = ring(junkd, "junkd")[:, 0:N]
                            T.op("dve", lambda e: e.tensor_scalar(out=jn, in0=sN, scalar1=mc, scalar2=0.0,
                                                                  op0=ALU.is_ge, op1=ALU.add, accum_out=ck),
                                 reads=[sN, mc], writes=[jn, ck])
                    yield 'B'
                for st_ in sts:
                    ng = st_["ng"]
                    r1 = st_["Rk"][:, k, 0:ng]
                    r2 = st_["Rk2"][:, k, 0:ng]
                    ckk = st_["cnt"][:, k, 0:ng]
                    nmid = st_["nmid"]; npl = st_["npl"]; tq = st_["tq"]; thr = st_["thr"]
                    if st_["gi"] == 1:
                        mid_ = st_["mid"]
                        T.op("dve", lambda e: e.tensor_tensor(out=npl, in0=mid_, in1=r1, op=ALU.subtract),
                             reads=[mid_, r1], writes=[npl])
                        T.op("dve", lambda e: e.scalar_tensor_tensor(out=tq, in0=ckk, scalar=TOPK - 0.5, in1=r2,
                                                                     op0=ALU.is_ge, op1=ALU.mult),
                             reads=[ckk, r2], writes=[tq])
                        T.op("dve", lambda e: e.tensor_tensor(out=mid_, in0=npl, in1=tq, op=ALU.add),
                             reads=[npl, tq], writes=[mid_])
                        continue
                    T.op("pool", lambda e: e.tensor_tensor(out=npl, in0=nmid, in1=r1, op=ALU.add), reads=[nmid, r1], writes=[npl])
                    T.op("pool", lambda e: e.tensor_tensor(out=tq, in0=ckk, in1=thr, op=ALU.subtract), reads=[ckk, thr], writes=[tq])
                    T.op("pool", lambda e: e.tensor_scalar(out=tq, in0=tq, scalar1=1.0, scalar2=0.0, op0=ALU.min,
                                                           op1=ALU.max), reads=[tq], writes=[tq])
                    T.op("pool", lambda e: e.tensor_tensor(out=tq, in0=tq, in1=r2, op=ALU.mult), reads=[tq, r2], writes=[tq])
                    T.op("pool", lambda e: e.tensor_tensor(out=nmid, in0=npl, in1=tq, op=ALU.subtract),
                         reads=[npl, tq], writes=[nmid])
            for st_ in sts:
                ng = st_["ng"]
                rl = st_["Rk"][:, NIT - 1, 0:ng]
                tau = st_["tau"]; nmid = st_["nmid"]
                if st_["gi"] == 1:
                    mid_ = st_["mid"]
                    T.op("dve", lambda e: e.tensor_tensor(out=tau, in0=mid_, in1=rl, op=ALU.subtract), reads=[mid_, rl],
                         writes=[tau])
                else:
                    T.op("dve", lambda e: e.tensor_tensor(out=tau, in0=nmid, in1=rl, op=ALU.add), reads=[nmid, rl],
                         writes=[tau])
                    T.op("dve", lambda e: e.tensor_scalar(out=tau, in0=tau, scalar1=-1.0, scalar2=None, op0=ALU.mult),
                         reads=[tau], writes=[tau])
                for lc_, i in enumerate(st_["grp"]):
                    taus[i] = tau[:, lc_:lc_ + 1]
        for i in tiles:
            N = 128 * (i + 1)
            maskb = ring(maskbs, "maskb")
            if i < 2:
                if i == 1:
                    T.op("dve", lambda e: e.memset(maskb[:, 0:128], 1.0), writes=[maskb[:, 0:128]])
                dd = maskb[:, 128 * i:128 * (i + 1)]
                T.op("dve", lambda e: e.tensor_copy(out=dd, in_=tri), reads=[tri], writes=[dd])
            else:
                c = col[i]
                sN = scores[c][:, 0:N]
                mN = maskb[:, 0:N]
                tc_ = taus[i]
                T.op("dve", lambda e: e.tensor_scalar(out=mN, in0=sN, scalar1=tc_, scalar2=None, op0=ALU.is_ge),
                     reads=[sN, tc_], writes=[mN])
                if debug:
                    T.dma("sp", dbg_score[i, :, 0:N], sN, "dbg", reads=[sN], writes=["dbg_score"])
            if debug:
                T.dma("sp", dbg_mask[i, :, 0:N], maskb[:, 0:N], "dbg", reads=[maskb[:, 0:N]], writes=["dbg_mask"])
            off = (i - 4 * m) * 128
            for j0 in range(0, i + 1, 8):
                nbk = min(8, i + 1 - j0)
                b = PS.get()
                pb = PS.bf16(b)
                for jj in range(nbk):
                    j = j0 + jj
                    o = pb[:, jj * 128:(jj + 1) * 128]
                    i_ = maskb[:, j * 128:(j + 1) * 128]
                    T.op("pe", lambda e: e.transpose(out=o, in_=i_, identity=ident), reads=[i_, ident], writes=[o],
                         inc=(jj == nbk - 1))
                dst = maskT[:, j0:j0 + nbk, off:off + 128]
                src = pb[:, 0:nbk * 128].rearrange("p (j t) -> p j t", j=nbk)
                T.op("act", lambda e: e.copy(out=dst, in_=src), reads=[pb[:, 0:nbk * 128]], writes=[dst])
            yield 'B'
        if nb:
            for gi in range(len(groups)):
                AR.release("tk_small%d" % gi)

    def attn_chunk(m):
        maskT = maskTs[m % 2]
        jmax = 4 * m + 3
        for cq in range(4):
            g = cq // 2
            abs_ = [PS.get(hold=True) for _ in range(2)]
            accs = [PS.f32(ab) for ab in abs_]
            pend = []

            def emit_pv(it):
                va_, pv_, ao_, j_ = it
                T.op("pe", lambda e: e.matmul(out=ao_, lhsT=va_, rhs=pv_, start=(j_ == 0), stop=(j_ == jmax)),
                     reads=[va_, pv_], writes=[ao_], inc=(j_ == jmax))
            for j in range(jmax + 1):
                t0 = max(512 * m, 128 * j)
                n = 512 * (m + 1) - t0
                off = t0 - 512 * m
                ps_ = []
                for e2 in range(2):
                    b = PS.get()
                    p = PS.f32(b)[:, 0:n]
                    l = kT2[64 * e2:64 * e2 + 64, g, j * 128:(j + 1) * 128]
                    r = qT[64 * e2:64 * e2 + 64, cq, t0:t0 + n]
                    T.op("pe", lambda e: e.matmul(out=p, lhsT=l, rhs=r, start=True, stop=True), reads=[l, r], writes=[p])
                    ps_.append(p)
                mt = maskT[:, j, off:off + n]
                for e2 in range(2):
                    p = ps_[e2]
                    pv = ring(ptiles, "ptile")[:, 0:n]
                    T.op("act", lambda e: e.activation(out=pv, in_=p, func=AF.Exp, scale=0.125), reads=[p], writes=[pv])
                    T.op("dve", lambda e: e.tensor_tensor(out=pv, in0=pv, in1=mt, op=ALU.mult), reads=[pv, mt],
                         writes=[pv])
                    va = Vaug[g][e2][:, j, :]
                    ao = accs[e2][:, off:off + n]
                    pend.append((va, pv, ao, j))
                while len(pend) > 8:
                    emit_pv(pend.pop(0))
                if j % 2 == 1:
                    yield
            while pend:
                emit_pv(pend.pop(0))
            for e2 in range(2):
                acc = accs[e2]
                rc = ring(rcs, "rc")
                po = 64 * e2
                pd = 64 * (1 - e2)
                rcv = rc[po:po + 64, :]
                den = acc[pd:pd + 64, :]
                T.op("dve", lambda e: e.reciprocal(out=rcv, in_=den), reads=[den], writes=[rcv])
                yo = y_bT[po:po + 64, cq, 512 * m:512 * (m + 1)]
                num = acc[po:po + 64, :]
                T.op("dve", lambda e: e.tensor_tensor(out=yo, in0=num, in1=rcv, op=ALU.mult), reads=[num, rcv],
                     writes=[yo])
                PS.release(abs_[e2])
            yield

    def run_interleaved(ga, gb, ra=1, rb=1, warm=0):
        alive = [ga is not None, gb is not None]
        gens = [ga, gb]
        reps = [ra, rb]
        for _ in range(warm):
            try:
                next(ga)
            except StopIteration:
                alive[0] = False
                break
        while any(alive):
            for q in range(2):
                if not alive[q]:
                    continue
                for _ in range(reps[q]):
                    try:
                        next(gens[q])
                    except StopIteration:
                        alive[q] = False
                        break

    def run_weighted(ga, gb, wS, wB):
        credit = 0.0
        alive = True
        for tag in gb:
            credit += wS if tag == 'S' else wB
            while alive and credit >= 1.0:
                try:
                    next(ga)
                except StopIteration:
                    alive = False
                credit -= 1.0
        if alive:
            for _ in ga:
                pass

    run_interleaved(topk_chunk(0), None)
    for m in range(3):
        nA = 4 * (2 * m + 3)
        nS = 4 * (m + 2) + 4
        nB = 2 * NIT + 4
        wS = 0.05
        run_weighted(attn_chunk(m), topk_chunk(m + 1), wS, max(0.3, (nA - wS * nS) / nB))
    for n_ in (["junk0", "junk1", "junkd0", "negtri_b", "qiT", "kiT2", "wi"] + ["score%d" % i for i in range(4)]
               + ["maskb%d" % i for i in range(2)] + ["rtmp%d" % i for i in range(10)] + ["dw%d" % i for i in range(4)]):
        AR.release(n_)
    xT = AR.alloc("xT", [NCH, S], BF16)
    T.dma("sp", xT, xT_scr, "xTsp", reads=["xT_scr"], writes=[xT])
    w_c = load_w_cols("w_c", C_C, 512)
    w_u = load_w_cols("w_u", C_U, 512)
    w_b = load_w_cols("w_b", C_B, 512)
    def proj_chunk(w, wc0, m):
        b = PS.get()
        p = PS.f32(b)
        for c in range(NCH):
            l = w[:, c, wc0:wc0 + 128]
            r = xT[:, c, m * 512:(m + 1) * 512]
            T.op("pe", lambda e: e.matmul(out=p, lhsT=l, rhs=r, start=(c == 0), stop=(c == NCH - 1)),
                 reads=[l, r], writes=[p], inc=(c == NCH - 1))
        return p

    def conv_phase():
        for j in range(4):
            c_sb = AR.ralloc("c_sb", [S], F32, 1)
            vpad = AR.ralloc("vpad", [S + 64], F32, 1)
            cacc = AR.ralloc("cacc", [S], F32, 1)
            T.op("pool", lambda e: e.memset(vpad[:, 0:2], 0.0), writes=[vpad[:, 0:2]])
            for m in range(4):
                p = proj_chunk(w_c, j * 128, m)
                evac_copy(c_sb[:, m * 512:(m + 1) * 512], p)
                yield
            for m in range(4):
                p = proj_chunk(w_u, j * 128, m)
                d_ = vpad[:, 2 + m * 512:2 + (m + 1) * 512]
                s_ = c_sb[:, m * 512:(m + 1) * 512]
                T.op("dve", lambda e: e.tensor_tensor(out=d_, in0=p, in1=s_, op=ALU.mult), reads=[p, s_], writes=[d_])
                yield
            w0 = convw[:, j * 3 + 0:j * 3 + 1]
            w1 = convw[:, j * 3 + 1:j * 3 + 2]
            w2 = convw[:, j * 3 + 2:j * 3 + 3]
            v2 = vpad[:, 2:S + 2]; v1 = vpad[:, 1:S + 1]; v0 = vpad[:, 0:S]
            T.op("dve", lambda e: e.tensor_scalar(out=cacc, in0=v2, scalar1=w2, scalar2=None, op0=ALU.mult),
                 reads=[v2, w2], writes=[cacc])
            yield
            T.op("dve", lambda e: e.scalar_tensor_tensor(out=cacc, in0=v1, scalar=w1, in1=cacc, op0=ALU.mult,
                                                         op1=ALU.add), reads=[v1, w1, cacc], writes=[cacc])
            yield
            T.op("dve", lambda e: e.scalar_tensor_tensor(out=cacc, in0=v0, scalar=w0, in1=cacc, op0=ALU.mult,
                                                         op1=ALU.add), reads=[v0, w0, cacc], writes=[cacc])
            yield
            for m in range(4):
                p = proj_chunk(w_b, j * 128, m)
                d_ = y_aT[:, j, m * 512:(m + 1) * 512]
                s_ = cacc[:, m * 512:(m + 1) * 512]
                T.op("dve", lambda e: e.tensor_tensor(out=d_, in0=p, in1=s_, op=ALU.mult), reads=[p, s_], writes=[d_])
                yield

    AR.release("maskT0")
    y_aT = AR.alloc("y_aT", [4, S], BF16)
    run_interleaved(attn_chunk(3), conv_phase(), 1, 1, warm=10)
    for n_ in (["maskT1", "qT", "kT2", "Vaug00", "Vaug01", "Vaug10", "Vaug11"]
               + ["ptile%d" % i for i in range(14)] + ["rc%d" % i for i in range(2)]):
        AR.release(n_)
    AR.release("w_c"); AR.release("w_u"); AR.release("w_b")
    for n_ in ("c_sb", "vpad", "cacc"):
        AR.rfree(n_)
    if debug:
        dump("y_bT", y_bT, [128, 4, S], BF16)

    stop("CD")
    w_gas = [None, None]
    w_gbs = [None, None]
    w_gas[0] = load_w_cols("w_ga0", C_GA, 512)
    w_gbs[0] = load_w_cols("w_gb0", C_GB, 512)
    w_pa = AR.alloc("w_pa", [4, D], BF16)
    w_pb = AR.alloc("w_pb", [4, D], BF16)
    T.dma("pool", w_pa, w_br_d[0].rearrange("(k p) n -> p k n", p=128), "w_pa", writes=[w_pa])
    T.dma("pool", w_pb, w_br_d[1].rearrange("(k p) n -> p k n", p=128), "w_pb", writes=[w_pb])
    if debug:
        dump("y_aT", y_aT, [128, 4, S], BF16)

    stop("E")
    mT = AR.alloc("mT", [NCH, S], BF16)
    w_gas[1] = load_w_cols("w_ga1", C_GA + 512, 512)
    w_gbs[1] = load_w_cols("w_gb1", C_GB + 512, 512)
    w_o = AR.alloc("w_o", [NCH, D], BF16)
    w_o_v = w_o_d.rearrange("(c p) n -> p c n", p=128)
    for q in range(2):
        T.dma("pool", w_o[:, 4 * q:4 * q + 4, :], w_o_v[:, 4 * q:4 * q + 4, :], "w_o%d" % q, writes=[w_o[:, 4 * q:4 * q + 4, :]])
    for half in range(2):
        w_ga = w_gas[half]
        w_gb = w_gbs[half]
        for dq in range(4):
            dc = half * 4 + dq
            for m in range(4):
                tsl = slice(m * 512, (m + 1) * 512)
                sa = AR.ralloc("sa", [512], F32, 3)
                sb = AR.ralloc("sb", [512], F32, 3)
                for (wg, sg) in ((w_ga, sa), (w_gb, sb)):
                    b = PS.get()
                    p = PS.f32(b)
                    for c in range(NCH):
                        l = wg[:, c, dq * 128:(dq + 1) * 128]
                        r = xT[:, c, tsl]
                        T.op("pe", lambda e: e.matmul(out=p, lhsT=l, rhs=r, start=(c == 0), stop=(c == NCH - 1)),
                             reads=[l, r], writes=[p], inc=(c == NCH - 1))
                    T.op("act", lambda e: e.activation(out=sg, in_=p, func=AF.Sigmoid), reads=[p], writes=[sg])
                t1 = AR.ralloc("t1", [512], F32, 3)
                t2 = AR.ralloc("t2", [512], F32, 3)
                for (wp, yT, sg, tt_) in ((w_pa, y_aT, sa, t1), (w_pb, y_bT, sb, t2)):
                    b = PS.get()
                    p = PS.f32(b)
                    for k in range(4):
                        l = wp[:, k, dc * 128:(dc + 1) * 128]
                        r = yT[:, k, tsl]
                        T.op("pe", lambda e: e.matmul(out=p, lhsT=l, rhs=r, start=(k == 0), stop=(k == 3)),
                             reads=[l, r], writes=[p], inc=(k == 3))
                    T.op("dve", lambda e: e.tensor_tensor(out=tt_, in0=p, in1=sg, op=ALU.mult),
                         reads=[p, sg], writes=[tt_])
                md = mT[:, dc, tsl]
                T.op("pool", lambda e: e.tensor_tensor(out=md, in0=t1, in1=t2, op=ALU.add), reads=[t1, t2], writes=[md])
                for n_ in ("sa", "sb", "t1", "t2"):
                    AR.release(n_)
        AR.release("w_ga%d" % half); AR.release("w_gb%d" % half)
    for n_ in ("xT", "y_aT", "y_bT", "w_pa", "w_pb"):
        AR.release(n_)
    for n_ in ("sa", "sb", "t1", "t2"):
        AR.rfree(n_)
    if debug:
        dump("mT", mT, [128, NCH, S], BF16)

    stop("F1")
    hT = AR.alloc("hT", [NCH, S], BF16)
    lng = AR.alloc("lng", [D], F32)
    lnb = AR.alloc("lnb", [D], F32)
    T.dma("sp", lng, ln1g_d, "lnp", writes=[lng])
    T.dma("sp", lnb, ln1b_d, "lnp", writes=[lnb], chain=True)
    w_dn_q = [AR.alloc("w_dn%d" % qq, [8, D], BF16) for qq in range(4)]
    w_dn_v = w_down_d.rearrange("(f p) n -> p f n", p=128)
    for q in range(8):
        dq_ = w_dn_q[q // 2][:, 4 * (q % 2):4 * (q % 2) + 4, :]
        T.dma("pool", dq_, w_dn_v[:, 4 * q:4 * q + 4, :], "w_dn%d" % q, writes=[dq_])

    def layernorm_tile(r, g_t, b_t, out_t, tag):
        st = AR.ralloc("lst" + tag, [16], F32, 2)
        mv = AR.ralloc("lmv" + tag, [8], F32, 2)
        for q in range(2):
            rq = r[:, q * 512:(q + 1) * 512]
            sq = st[:, q * 6:(q + 1) * 6]
            T.op("dve", lambda e: e.bn_stats(out=sq, in_=rq), reads=[rq], writes=[sq])
        s12 = st[:, 0:12]
        T.op("dve", lambda e: e.bn_aggr(out=mv[:, 0:2], in_=s12), reads=[s12], writes=[mv[:, 0:2]])
        rstd_from_var(mv)
        nmr = mv[:, 4:5]
        T.op("dve", lambda e: e.scalar_tensor_tensor(out=nmr, in0=mv[:, 0:1], scalar=-1.0, in1=mv[:, 2:3],
                                                     op0=ALU.mult, op1=ALU.mult), reads=[mv[:, 0:3]], writes=[nmr])
        T.op("act", lambda e: e.activation(out=out_t, in_=r, func=AF.Identity, bias=nmr, scale=mv[:, 2:3]),
             reads=[r, mv[:, 2:5]], writes=[out_t])
        T.op("dve", lambda e: e.tensor_tensor(out=out_t, in0=out_t, in1=g_t, op=ALU.mult), reads=[out_t, g_t],
             writes=[out_t])
        T.op("dve", lambda e: e.tensor_tensor(out=out_t, in0=out_t, in1=b_t, op=ALU.add), reads=[out_t, b_t],
             writes=[out_t])
        AR.release("lst" + tag); AR.release("lmv" + tag)

    g1c = AR.alloc("g1c", [8], F32)
    b1c = AR.alloc("b1c", [8], F32)
    T.dma("sp", g1c, ln1gc_d, "lnc", writes=[g1c])
    T.dma("sp", b1c, ln1bc_d, "lnc", writes=[b1c], chain=True)

    def emit_hT(nb_, tt_):
        b_ = PS.get()
        pb_ = PS.bf16(b_)
        for c in range(NCH):
            o = pb_[:, c * 128:(c + 1) * 128]
            i_ = nb_[:, c * 128:(c + 1) * 128]
            T.op("pe", lambda e: e.transpose(out=o, in_=i_, identity=ident), reads=[i_, ident], writes=[o],
                 inc=(c == NCH - 1))
        for c in range(NCH):
            src = pb_[:, c * 128:(c + 1) * 128]
            dst = hT[:, c, tt_ * 128:(tt_ + 1) * 128]
            gc = g1c[:, c:c + 1]
            bc = b1c[:, c:c + 1]
            T.op("act", lambda e: e.activation(out=dst, in_=src, func=AF.Identity, bias=bc, scale=gc),
                 reads=[src, gc, bc], writes=[dst])

    pend_nb = []
    xq = []

    def issue_xload(t_):
        xt_ = AR.ralloc("xr", [D], F32, 4)
        T.dma("sp", xt_, x_d[t_ * 128:(t_ + 1) * 128, :], "xr%d" % (t_ % 4), writes=[xt_])
        xq.append(xt_)
    issue_xload(0)
    issue_xload(1)
    for tt in range(NT):
        if tt + 2 < NT:
            issue_xload(tt + 2)
        xt = xq.pop(0)
        r = AR.ralloc("r1", [D], F32, 2)
        for half in range(2):
            b = PS.get()
            p = PS.f32(b)
            for dc in range(NCH):
                l = mT[:, dc, tt * 128:(tt + 1) * 128]
                rr = w_o[:, dc, half * 512:(half + 1) * 512]
                T.op("pe", lambda e: e.matmul(out=p, lhsT=l, rhs=rr, start=(dc == 0), stop=(dc == NCH - 1)),
                     reads=[l, rr], writes=[p], inc=(dc == NCH - 1))
            xh = xt[:, half * 512:(half + 1) * 512]
            rh = r[:, half * 512:(half + 1) * 512]
            T.op("dve", lambda e: e.scalar_tensor_tensor(out=rh, in0=xh, scalar=ALPHA, in1=p, op0=ALU.mult, op1=ALU.add),
                 reads=[xh, p], writes=[rh])
        st = AR.ralloc("lst1", [16], F32, 2)
        mv = AR.ralloc("lmv1", [8], F32, 2)
        for q in range(2):
            rq = r[:, q * 512:(q + 1) * 512]
            sq = st[:, q * 6:(q + 1) * 6]
            T.op("dve", lambda e: e.bn_stats(out=sq, in_=rq), reads=[rq], writes=[sq])
        s12 = st[:, 0:12]
        T.op("dve", lambda e: e.bn_aggr(out=mv[:, 0:2], in_=s12), reads=[s12], writes=[mv[:, 0:2]])
        rstd_from_var(mv)
        nmr = mv[:, 4:5]
        T.op("dve", lambda e: e.scalar_tensor_tensor(out=nmr, in0=mv[:, 0:1], scalar=-1.0, in1=mv[:, 2:3],
                                                     op0=ALU.mult, op1=ALU.mult), reads=[mv[:, 0:3]], writes=[nmr])
        nn = AR.ralloc("nn", [D], F32, 2)
        T.op("act", lambda e: e.activation(out=nn, in_=r, func=AF.Identity, bias=nmr, scale=mv[:, 2:3]),
             reads=[r, mv[:, 2:5]], writes=[nn])
        nb_t = AR.ralloc("nbt", [D], BF16, 4)
        T.op("dve", lambda e: e.tensor_copy(out=nb_t, in_=nn), reads=[nn], writes=[nb_t])
        hh = AR.ralloc("hh", [D], F32, 2)
        T.op("dve", lambda e: e.tensor_tensor(out=hh, in0=nn, in1=lng, op=ALU.mult), reads=[nn, lng], writes=[hh])
        T.op("dve", lambda e: e.tensor_tensor(out=hh, in0=hh, in1=lnb, op=ALU.add), reads=[hh, lnb], writes=[hh])
        T.dma("sp", h_scr[tt * 128:(tt + 1) * 128, :], hh, "hst%d" % (tt % 2), reads=[hh], writes=[("h", tt)])
        pend_nb.append((nb_t, tt))
        if len(pend_nb) > 2:
            emit_hT(*pend_nb.pop(0))
    AR.release("mT"); AR.release("w_o")
    AR.rfree("xr"); AR.rfree("r1")
    upT = AR.alloc("upT", [NF, 512], BF16)
    w_up_v = w_up_d.rearrange("(c p) n -> p c n", p=128)
    NSLOT = 3
    slots = [AR.alloc("wup%d" % i, [NCH, 512], BF16) for i in range(NSLOT)]

    def issue_up_load(idx):
        fq = idx % 8
        sl = slots[idx % NSLOT]
        T.dma("pool", sl, w_up_v[:, :, fq * 512:(fq + 1) * 512], "wup%d" % (idx % NSLOT), writes=[sl])

    total_loads = 4 * 8
    for idx in range(min(NSLOT, total_loads)):
        issue_up_load(idx)
    nxt_box = [NSLOT]

    def up_gen(G):
        tsl = slice(G * 512, (G + 1) * 512)
        for fq in range(8):
            idx = G * 8 + fq
            sl = slots[idx % NSLOT]
            for f4 in range(4):
                f = fq * 4 + f4
                b = PS.get()
                p = PS.f32(b)
                for c in range(NCH):
                    l = sl[:, c, f4 * 128:(f4 + 1) * 128]
                    r = hT[:, c, tsl]
                    T.op("pe", lambda e: e.matmul(out=p, lhsT=l, rhs=r, start=(c == 0), stop=(c == NCH - 1)),
                         reads=[l, r], writes=[p], inc=(c == NCH - 1))
                rt = AR.ralloc("relu_t", [512], BF16, 3)
                T.op("act", lambda e: e.activation(out=rt, in_=p, func=AF.Relu), reads=[p], writes=[rt])
                ud = upT[:, f, :]
                T.op("dve", lambda e: e.tensor_tensor(out=ud, in0=rt, in1=rt, op=ALU.mult), reads=[rt], writes=[ud])
                yield
            if nxt_box[0] < total_loads:
                issue_up_load(nxt_box[0])
                nxt_box[0] += 1

    up0 = up_gen(0)
    while pend_nb:
        for _ in range(6):
            next(up0)
        emit_hT(*pend_nb.pop(0))
    AR.release("g1c"); AR.release("b1c")
    for n_ in ("hh", "nn", "nbt", "lst1", "lmv1"):
        AR.rfree(n_)

    stop("F2")
    T.dma("sp", lng, ln2g_d, "lnp", writes=[lng])
    T.dma("sp", lnb, ln2b_d, "lnp", writes=[lnb], chain=True)
    for G in range(4):
        for _ in (up0 if G == 0 else up_gen(G)):
            pass
        hts = []
        for tq in range(4):
            tt = G * 4 + tq
            ht_ = AR.ralloc("hr", [D], F32, 4)
            T.dma("sp", ht_, h_scr[tt * 128:(tt + 1) * 128, :], "hr%d" % tq, reads=[("h", tt)], writes=[ht_])
            hts.append(ht_)
        for tq in range(4):
            tt = G * 4 + tq
            ht = hts[tq]
            r = AR.ralloc("r2", [D], F32, 2)
            for half in range(2):
                b = PS.get()
                p = PS.f32(b)
                for f in range(NF):
                    l = upT[:, f, tq * 128:(tq + 1) * 128]
                    rr = w_dn_q[f // 8][:, f % 8, half * 512:(half + 1) * 512]
                    T.op("pe", lambda e: e.matmul(out=p, lhsT=l, rhs=rr, start=(f == 0), stop=(f == NF - 1)),
                         reads=[l, rr], writes=[p], inc=(f == NF - 1))
                hq = ht[:, half * 512:(half + 1) * 512]
                rh = r[:, half * 512:(half + 1) * 512]
                T.op("dve", lambda e: e.scalar_tensor_tensor(out=rh, in0=hq, scalar=ALPHA, in1=p, op0=ALU.mult,
                                                             op1=ALU.add), reads=[hq, p], writes=[rh])
            AR.release("hr")
            ot = AR.ralloc("ot", [D], F32, 2)
            layernorm_tile(r, lng, lnb, ot, "2")
            AR.release("r2")
            T.dma("sp", out_d[tt * 128:(tt + 1) * 128, :], ot, "ost%d" % (tt % 2), reads=[ot], writes=[("o", tt)])
            AR.release("ot")
    return


def _host_consts():
    ident = np.eye(128, dtype=np.float32).astype(ml_dtypes.bfloat16)
    tri = np.triu(np.ones((128, 128), dtype=np.float32)).T
    negtri = np.where(tri > 0, 0.0, NEG).astype(np.float32)
    k = np.arange(NIT)
    p2 = np.concatenate([2.0 ** -(k + 2.0), 2.0 * 2.0 ** -(k + 2.0)]).astype(np.float32)
    pow2 = np.ascontiguousarray(np.broadcast_to(p2[None, :], (128, 2 * NIT))).astype(np.float32)
    return ident, tri.astype(ml_dtypes.bfloat16), negtri, pow2


def _bcast(v, n=128):
    v = np.asarray(v, dtype=np.float32).reshape(1, -1)
    return np.ascontiguousarray(np.broadcast_to(v, (n, v.shape[1])))


def make_in_maps(inputs, cores):
    ident, tri, negtri, pow2 = _host_consts()
    cw = np.asarray(inputs["conv_w"], dtype=np.float32)[0]
    conv_w_t = np.ascontiguousarray(cw.reshape(3, 4, 128).transpose(2, 1, 0).reshape(128, 12))
    shared = {
        "w_in": np.ascontiguousarray(inputs["w_in"][0], dtype=np.float32),
        "conv_w_t": conv_w_t,
        "ikg": _bcast(inputs["idx_k_norm_g"][0]),
        "ikb": _bcast(inputs["idx_k_norm_b"][0]),
        "w_branch": np.ascontiguousarray(inputs["w_branch"][0], dtype=np.float32),
        "w_o": np.ascontiguousarray(inputs["w_o"][0], dtype=np.float32),
        "ln1g": _bcast(inputs["ln1_g"][0]),
        "ln1b": _bcast(inputs["ln1_b"][0]),
        "ln1gc": np.ascontiguousarray(np.asarray(inputs["ln1_g"][0], dtype=np.float32).reshape(8, 128).T),
        "ln1bc": np.ascontiguousarray(np.asarray(inputs["ln1_b"][0], dtype=np.float32).reshape(8, 128).T),
        "w_up": np.ascontiguousarray(inputs["w_up"][0], dtype=np.float32),
        "w_down": np.ascontiguousarray(inputs["w_down"][0], dtype=np.float32),
        "ln2g": _bcast(inputs["ln2_g"][0]),
        "ln2b": _bcast(inputs["ln2_b"][0]),
        "ident": ident, "tri": tri, "negtri": negtri, "pow2": pow2,
    }
    x = np.asarray(inputs["x"], dtype=np.float32)
    maps = []
    for b in cores:
        m = dict(shared)
        m["x"] = np.ascontiguousarray(x[b])
        maps.append(m)
    return maps


def kernel(**inputs):
    nc, info = build_program(debug=False)
    in_maps = make_in_maps(inputs, list(range(8)))
    res = run_bass_kernel_spmd(nc, in_maps, core_ids=list(range(8)))
    out = np.stack([np.asarray(r["out"], dtype=np.float32) for r in res.results], axis=0)
    return out
```
